# Optimizing a Trainium2 kernel written in Bass

```python
import jax, jax.numpy as jnp
from jax import lax
import numpy as np

D_MODEL = 2048
BATCH = 4
SEQ = 8192
DEPTH = 1
DEC_BATCH = 2
DEC_SEQ = 4096
PAST_LEN = 128

D_MLSTM = 1024
N_MLSTM_HEADS = 4
HEAD_DIM = D_MLSTM // N_MLSTM_HEADS
D_POOL = D_MODEL - D_MLSTM
POOL_WINDOWS = (2, 4, 8, 16)
N_POOL_GROUPS = len(POOL_WINDOWS)
POOL_GROUP_DIM = D_POOL // N_POOL_GROUPS
N_GATES = 4 * N_MLSTM_HEADS
D_IN = 4 * D_MLSTM + N_GATES + D_POOL
D_FF = 5632
CONV_WIDTH = 3
CHUNK = 128
LN_EPS = 1e-5
DEEPNORM_ALPHA = (2.0 * DEPTH) ** 0.25
DEEPNORM_BETA = (8.0 * DEPTH) ** -0.25

kernel_name = "hymba_mlstm_multipool_convffn_encoder"


def layer_norm(x, g, b):
    xf = x.astype(jnp.float32)
    mu = jnp.mean(xf, axis=-1, keepdims=True)
    var = jnp.mean(jnp.square(xf - mu), axis=-1, keepdims=True)
    y = (xf - mu) * lax.rsqrt(var + LN_EPS) * g.astype(jnp.float32) + b.astype(jnp.float32)
    return y.astype(x.dtype)


def _mlstm_chunk_step(carry, xs):
    C, n, m = carry
    q, k, v, ig, lf = xs
    L = q.shape[2]
    b = jnp.cumsum(lf, axis=-1)
    lower = jnp.tril(jnp.ones((L, L), dtype=bool))
    D = jnp.where(lower, b[..., :, None] - b[..., None, :] + ig[..., None, :], -jnp.inf)
    inter = b + m[..., None]
    m_t = jnp.maximum(inter, jnp.max(D, axis=-1))
    w_intra = jnp.exp(D - m_t[..., None])
    w_inter = jnp.exp(inter - m_t)
    s = jnp.einsum('bhtd,bhsd->bhts', q, k) * w_intra
    num = jnp.einsum('bhts,bhsd->bhtd', s, v) + w_inter[..., None] * jnp.einsum('bhtd,bhde->bhte', q, C)
    den = jnp.sum(s, axis=-1) + w_inter * jnp.einsum('bhtd,bhd->bht', q, n)
    h = num / jnp.maximum(jnp.abs(den), jnp.exp(-m_t))[..., None]
    bL = b[..., -1]
    g = bL[..., None] - b + ig
    m_new = jnp.maximum(bL + m, jnp.max(g, axis=-1))
    decay = jnp.exp(bL + m - m_new)
    w_s = jnp.exp(g - m_new[..., None])
    C_new = decay[..., None, None] * C + jnp.einsum('bhs,bhsd,bhse->bhde', w_s, k, v)
    n_new = decay[..., None] * n + jnp.einsum('bhs,bhsd->bhd', w_s, k)
    return (C_new, n_new, m_new), h


def mlstm_scan(q, k, v, ig, lf):
    B, H, S, d = q.shape
    nc = S // CHUNK
    to_chunks = lambda a: jnp.moveaxis(a.reshape((B, H, nc, CHUNK) + a.shape[3:]), 2, 0)
    init = (jnp.zeros((B, H, d, d), jnp.float32), jnp.zeros((B, H, d), jnp.float32),
            jnp.zeros((B, H), jnp.float32))
    _, h = lax.scan(_mlstm_chunk_step, init, tuple(to_chunks(a) for a in (q, k, v, ig, lf)))
    return jnp.moveaxis(h, 0, 2).reshape(B, H, S, d)


def centred_mean_minus_self(u, w):
    S = u.shape[1]
    P = jnp.concatenate([jnp.zeros_like(u[:, :1]), jnp.cumsum(u, axis=1)], axis=1)
    t = jnp.arange(S)
    lo = jnp.clip(t - w // 2, 0, S)
    hi = jnp.clip(t + (w - w // 2), 0, S)
    cnt = (hi - lo).astype(jnp.float32)
    mean = (jnp.take(P, hi, axis=1) - jnp.take(P, lo, axis=1)) / cnt[None, :, None]
    return mean - u


def encoder_layer(x, w_in, b_in, mh_norm_g, w_pool, pool_scale, w_out, b_out, ln1_g, ln1_b,
                  w_up, b_up, w_conv, b_conv, w_down, b_down, ln2_g, ln2_b):
    B, S, _ = x.shape
    H, hd = N_MLSTM_HEADS, HEAD_DIM
    z = x @ w_in + b_in
    q, k, v, o, gates, u = jnp.split(
        z, np.cumsum([D_MLSTM, D_MLSTM, D_MLSTM, D_MLSTM, N_GATES]).tolist(), axis=-1)
    heads = lambda a: a.astype(jnp.float32).reshape(B, S, H, hd).transpose(0, 2, 1, 3)
    qh, kh, vh = heads(q), heads(k) * (hd ** -0.5), heads(v)
    gates = gates.astype(jnp.float32).transpose(0, 2, 1)
    ig_f, fp_f, ig_b, fp_b = jnp.split(gates, 4, axis=1)
    h_fwd = mlstm_scan(qh, kh, vh, ig_f, jax.nn.log_sigmoid(fp_f))
    fl = lambda a: jnp.flip(a, axis=2)
    h_bwd = fl(mlstm_scan(fl(qh), fl(kh), fl(vh), fl(ig_b), fl(jax.nn.log_sigmoid(fp_b))))
    hm = (h_fwd + h_bwd).transpose(0, 2, 1, 3)
    mu = jnp.mean(hm, axis=-1, keepdims=True)
    var = jnp.mean(jnp.square(hm - mu), axis=-1, keepdims=True)
    hm = ((hm - mu) * lax.rsqrt(var + LN_EPS)).reshape(B, S, D_MLSTM) * mh_norm_g.astype(jnp.float32)
    hm = hm * jax.nn.sigmoid(o.astype(jnp.float32))
    ug = u.astype(jnp.float32).reshape(B, S, N_POOL_GROUPS, POOL_GROUP_DIM)
    pooled = jnp.stack([centred_mean_minus_self(ug[:, :, gi], w) for gi, w in enumerate(POOL_WINDOWS)], axis=2)
    hp = jnp.einsum('bsgc,gcd->bsgd', pooled, w_pool.astype(jnp.float32)).reshape(B, S, D_POOL)
    hp = hp * pool_scale.astype(jnp.float32)
    mix = jnp.concatenate([hm, hp], axis=-1).astype(x.dtype) @ w_out + b_out
    x = layer_norm(DEEPNORM_ALPHA * x + mix, ln1_g, ln1_b)
    a = x @ w_up + b_up
    ap = jnp.pad(a, ((0, 0), (1, 1), (0, 0)))
    a = w_conv[0] * ap[:, :-2] + w_conv[1] * ap[:, 1:-1] + w_conv[2] * ap[:, 2:] + b_conv
    gate, val = jnp.split(a, 2, axis=-1)
    f = (jax.nn.silu(gate) * val) @ w_down + b_down
    return layer_norm(DEEPNORM_ALPHA * x + f, ln2_g, ln2_b)


def run_trunk(x, w_in, b_in, mh_norm_g, w_pool, pool_scale, w_out, b_out, ln1_g, ln1_b,
              w_up, b_up, w_conv, b_conv, w_down, b_down, ln2_g, ln2_b):
    for l in range(DEPTH):
        x = encoder_layer(x, w_in[l], b_in[l], mh_norm_g[l], w_pool[l], pool_scale[l], w_out[l], b_out[l],
                          ln1_g[l], ln1_b[l], w_up[l], b_up[l], w_conv[l], b_conv[l], w_down[l], b_down[l],
                          ln2_g[l], ln2_b[l])
    return x


def setup_inputs(seed: int = 0) -> dict:
    key = jax.random.key(seed)
    ks = jax.random.split(key, 20)
    nrm = lambda k, shape, s: jax.random.normal(k, shape, jnp.float32) * s
    gain = lambda k, n: 1.0 + nrm(k, (DEPTH, n), 0.02)
    H = N_MLSTM_HEADS
    b_in = nrm(ks[3], (DEPTH, D_IN), 0.02)
    f_bias = jnp.linspace(3.0, 6.0, H, dtype=jnp.float32)
    g0 = 4 * D_MLSTM
    b_in = b_in.at[:, g0 + H:g0 + 2 * H].add(f_bias).at[:, g0 + 3 * H:g0 + 4 * H].add(f_bias)
    return {
        "x_prompt": nrm(ks[0], (BATCH, SEQ, D_MODEL), 1.0),
        "x_sample": nrm(ks[1], (DEC_BATCH, DEC_SEQ, D_MODEL), 1.0),
        "w_in": nrm(ks[2], (DEPTH, D_MODEL, D_IN), D_MODEL ** -0.5),
        "b_in": b_in,
        "mh_norm_g": gain(ks[4], D_MLSTM),
        "w_pool": nrm(ks[5], (DEPTH, N_POOL_GROUPS, POOL_GROUP_DIM, POOL_GROUP_DIM), POOL_GROUP_DIM ** -0.5),
        "pool_scale": gain(ks[6], D_POOL),
        "w_out": nrm(ks[7], (DEPTH, D_MODEL, D_MODEL), DEEPNORM_BETA * D_MODEL ** -0.5),
        "b_out": nrm(ks[8], (DEPTH, D_MODEL), 0.02),
        "ln1_g": gain(ks[9], D_MODEL),
        "ln1_b": nrm(ks[10], (DEPTH, D_MODEL), 0.02),
        "w_up": nrm(ks[11], (DEPTH, D_MODEL, 2 * D_FF), D_MODEL ** -0.5),
        "b_up": nrm(ks[12], (DEPTH, 2 * D_FF), 0.02),
        "w_conv": nrm(ks[13], (DEPTH, CONV_WIDTH, 2 * D_FF), CONV_WIDTH ** -0.5),
        "b_conv": nrm(ks[14], (DEPTH, 2 * D_FF), 0.02),
        "w_down": nrm(ks[15], (DEPTH, D_FF, D_MODEL), DEEPNORM_BETA * D_FF ** -0.5),
        "b_down": nrm(ks[16], (DEPTH, D_MODEL), 0.02),
        "ln2_g": gain(ks[17], D_MODEL),
        "ln2_b": nrm(ks[18], (DEPTH, D_MODEL), 0.02),
    }


def reference(x_prompt, x_sample, w_in, b_in, mh_norm_g, w_pool, pool_scale, w_out, b_out, ln1_g, ln1_b,
              w_up, b_up, w_conv, b_conv, w_down, b_down, ln2_g, ln2_b):
    y_prompt = run_trunk(x_prompt, w_in, b_in, mh_norm_g, w_pool, pool_scale, w_out, b_out, ln1_g, ln1_b,
                         w_up, b_up, w_conv, b_conv, w_down, b_down, ln2_g, ln2_b)
    y_sample = run_trunk(x_sample, w_in, b_in, mh_norm_g, w_pool, pool_scale, w_out, b_out, ln1_g, ln1_b,
                         w_up, b_up, w_conv, b_conv, w_down, b_down, ln2_g, ln2_b)
    return (y_prompt, y_sample)
```

```python
import numpy as np
from contextlib import ExitStack
import concourse.bass as bass
import concourse.mybir as mybir
from concourse.bass_utils import run_bass_kernel_spmd

F32 = mybir.dt.float32
BF16 = mybir.dt.bfloat16
AF = mybir.ActivationFunctionType
ALU = mybir.AluOpType

D = 2048
DM = 1024
H = 4
HD = 256
DFF = 5632
NCORES = 8
LN_EPS = 1e-5
ALPHA = 2.0 ** 0.25
POOL_W = (2, 4, 8, 16)

COMPUTE = ('pe', 'act', 'dve', 'pool')
import os
_STOP = int(os.environ.get('P3STOP', '0'))


class Instr:
    __slots__ = ('eng', 'fn', 'deps', 'is_dma', 'needs_inc', 'semval', 'dsem', 'dval', 'gid')


class Sems:
    def __init__(self, nc, es, ndma=12):
        self.ndma = ndma
        self.c = {e: es.enter_context(nc.semaphore("c_" + e)) for e in COMPUTE}
        self.d = {e: [es.enter_context(nc.semaphore("d_%s%d" % (e, i))) for i in range(ndma)]
                  for e in ('act', 'pool', 'sp')}
        self.cbase = {e: 0 for e in COMPUTE}
        self.dbase = {e: [0] * ndma for e in ('act', 'pool', 'sp')}


class Prog:
    def __init__(self, nc, name, sems):
        self.nc = nc
        self.name = name
        self.sems = sems
        ndma_sems = sems.ndma
        self.q = {e: [] for e in ('pe', 'act', 'dve', 'pool', 'sp')}
        self.bufs = {}
        self.gid = 0
        self.ndma_sems = ndma_sems
        self.dma_count = {e: 0 for e in ('act', 'pool', 'sp')}
        self.dma_hist = {e: [] for e in ('act', 'pool', 'sp')}

    def _add(self, eng, fn, reads, writes, is_dma, banks=()):
        writes = list(writes) + [('bank', b) for b in banks]
        ins = Instr()
        ins.eng = eng
        ins.fn = fn
        ins.is_dma = is_dma
        ins.needs_inc = False
        ins.semval = None
        ins.dsem = None
        ins.dval = None
        ins.gid = self.gid
        self.gid += 1
        deps = {}
        for b in reads:
            st = self.bufs.get(b)
            if st is not None and st[0] is not None:
                deps[id(st[0])] = (st[0], True)
        for b in writes:
            st = self.bufs.get(b)
            if st is not None:
                if st[0] is not None and id(st[0]) not in deps:
                    deps[id(st[0])] = (st[0], False)
                for r in st[1]:
                    if id(r) not in deps:
                        deps[id(r)] = (r, False)
        for b in reads:
            st = self.bufs.get(b)
            if st is None:
                st = [None, []]
                self.bufs[b] = st
            st[1].append(ins)
        for b in writes:
            self.bufs[b] = [ins, []]
        out = []
        for d, raw in deps.values():
            if d is ins:
                continue
            if (not is_dma) and (not d.is_dma) and d.eng == eng:
                if eng == 'pe':
                    continue
            out.append(d)
        if is_dma:
            k = self.dma_count[eng]
            self.dma_count[eng] += 1
            hist = self.dma_hist[eng]
            if k >= self.ndma_sems:
                out.append(hist[k - self.ndma_sems])
            hist.append(ins)
            ins.dsem = k % self.ndma_sems
            ins.dval = self.sems.dbase[eng][ins.dsem] + 16 * (k // self.ndma_sems + 1)
        for d in out:
            if not d.is_dma:
                d.needs_inc = True
        ins.deps = out
        self.q[eng].append(ins)
        return ins

    def pe(self, fn, reads=(), writes=(), banks=()):
        return self._add('pe', fn, reads, writes, False, banks)

    def act(self, fn, reads=(), writes=(), banks=()):
        return self._add('act', fn, reads, writes, False, banks)

    def dve(self, fn, reads=(), writes=(), banks=()):
        return self._add('dve', fn, reads, writes, False, banks)

    def pool(self, fn, reads=(), writes=(), banks=()):
        return self._add('pool', fn, reads, writes, False, banks)

    def dma(self, q, fn, reads=(), writes=()):
        return self._add(q, fn, reads, writes, True)

    def run(self):
        nc = self.nc
        sems = self.sems
        for e in COMPUTE:
            c = sems.cbase[e]
            for ins in self.q[e]:
                if (not ins.is_dma) and ins.needs_inc:
                    c += 1
                    ins.semval = c
            sems.cbase[e] = c
        for e in ('act', 'pool', 'sp'):
            for ins in self.q[e]:
                if ins.is_dma:
                    sems.dbase[e][ins.dsem] = ins.dval
        with ExitStack() as st:
            csem = sems.c
            dsem = sems.d
            block = st.enter_context(nc.Block())
            engmap = {'pe': (block.tensor, nc.tensor), 'act': (block.scalar, nc.scalar),
                      'dve': (block.vector, nc.vector), 'pool': (block.gpsimd, nc.gpsimd),
                      'sp': (block.sync, nc.sync)}

            def make(e):
                def body(eng):
                    seen = {}
                    for ins in self.q[e]:
                        for d in ins.deps:
                            if d.is_dma:
                                key = ('d', d.eng, d.dsem)
                                sem, val = dsem[d.eng][d.dsem], d.dval
                            else:
                                key = ('c', d.eng)
                                sem, val = csem[d.eng], d.semval
                            if seen.get(key, 0) >= val:
                                continue
                            seen[key] = val
                            eng.wait_ge(sem, val)
                        r = ins.fn(eng)
                        if ins.is_dma:
                            r.then_inc(dsem[e][ins.dsem], 16)
                        elif ins.needs_inc:
                            r.then_inc(csem[e], 1)
                    if e in dsem:
                        last = {}
                        for ins in self.q[e]:
                            if ins.is_dma:
                                last[ins.dsem] = ins.dval
                        for k, v in last.items():
                            if seen.get(('d', e, k), 0) < v:
                                eng.wait_ge(dsem[e][k], v)
                return body

            for e, (dec, _) in engmap.items():
                dec(make(e))


def _pool_mats(rev=False):
    mats = np.zeros((4, 5, 128, 128), np.float32)
    for g, w in enumerate(POOL_W):
        lo_off, hi_off = (w // 2, w - w // 2) if not rev else (w - w // 2 - 1, w // 2 + 1)
        for t in range(128):
            lo, hi = t - lo_off, t + hi_off
            for s in range(lo, hi):
                if s < 0:
                    mats[g, 0, s + 128, t] += 1.0 / w
                elif s >= 128:
                    mats[g, 2, s - 128, t] += 1.0 / w
                else:
                    mats[g, 1, s, t] += 1.0 / w
            mats[g, 1, t, t] -= 1.0
            lo_c = max(lo, 0)
            cnt = hi - lo_c if hi <= 128 else None
            for s in range(lo_c, min(hi, 128)):
                mats[g, 3, s, t] += 1.0 / (min(hi, 10 ** 9) - lo_c)
            mats[g, 3, t, t] -= 1.0
            hi_c = min(hi, 128)
            for s in range(max(lo, 0), hi_c):
                mats[g, 4, s, t] += 1.0 / (hi_c - lo)
            mats[g, 4, t, t] -= 1.0
    return mats


def _consts(rev=False):
    s = np.arange(128)[:, None]
    t = np.arange(128)[None, :]
    trif = (s <= t).astype(np.float32)
    trib = (s >= t).astype(np.float32)
    ones = np.ones((128, 128), np.float32)
    ident = np.eye(128, dtype=np.float32)
    c = np.concatenate([trif, trib, ones, ident], axis=1)
    pm = _pool_mats(rev).reshape(20, 128, 128).transpose(1, 0, 2).reshape(128, 20 * 128)
    return np.ascontiguousarray(c), np.ascontiguousarray(pm)


class Ctx:
    pass


class Lay:
    def __init__(self, P_own, P_ctx, S_bef, S_own, S_aft, S_pad):
        self.P_own, self.P_ctx, self.S_bef, self.S_own, self.S_aft, self.S_pad = P_own, P_ctx, S_bef, S_own, S_aft, S_pad
        self.NPS = P_own + 1 + P_ctx
        self.S0 = self.NPS
        self.NSS = S_bef + 1 + S_own + 1 + S_aft
        self.NX = self.NPS + self.NSS
        self.P_M = list(range(0, P_own + 1))
        self.S_M = list(range(self.S0 + S_bef, self.S0 + S_bef + S_own + 2))
        self.M = self.P_M + self.S_M
        self.NM = len(self.M)
        self.segs = [(0, len(self.P_M)), (len(self.P_M), self.NM)]
        self.M2 = list(self.M)
        extra = P_own + 1
        while len(self.M2) % 4:
            self.M2.append(extra)
            extra += 1
        self.tiles = [4 * i for i in range(P_own // 4)] + [len(self.P_M) + 1 + 4 * i for i in range(S_own // 4)]
        self.NTILE = len(self.tiles)
        self.iS = P_own // 4
        self.NF = P_own + S_own

    def seg_of(self, r):
        return 0 if r < len(self.P_M) else 1


LAY_FULL = (32, 31, 7, 8, 23, 8)


def _alloc(es, nc):
    def sb(name, shape, dt):
        return es.enter_context(nc.sbuf_tensor(name, shape, dt))

    def ps(name, shape, dt=F32):
        return es.enter_context(nc.psum_tensor(name, shape, dt))
    return sb, ps


def declare(nc, lay, debug, es):
    c = Ctx()
    c.lay = lay
    NCH = lay.NX
    c.NCH = NCH
    c.sems = Sems(nc, es)
    NT = lay.NM * 128

    def inp(name, shape, dt=F32):
        return nc.dram_tensor(name, shape, dt, kind="ExternalInput").ap()

    def scr(name, shape, dt):
        return nc.dram_tensor(name, shape, dt, kind="ExternalOutput" if debug else "Internal").ap()

    c.xT = inp("xT", [NCH, 128, 2048])
    c.x = inp("x", [NT, D])
    c.w_kv = inp("w_kv", [D, 2048])
    c.w_g = inp("w_g", [D, 16])
    c.w_qou = inp("w_qou", [D, 3072])
    c.b_kvg = inp("b_kvg", [1, 2064])
    c.b_q = inp("b_q", [128, 8])
    c.b_ou = inp("b_ou", [1, 2048])
    c.gain = inp("gain", [1, 1024])
    c.w_pool = inp("w_pool", [4, 256, 256])
    c.pool_scale = inp("pool_scale", [1, 1024])
    c.w_out = inp("w_out", [D, D])
    c.b_out = inp("b_out", [1, D])
    c.ln1_g = inp("ln1_g", [1, D])
    c.ln1_b = inp("ln1_b", [1, D])
    c.w_up = inp("w_up", [D, 2 * DFF])
    c.b_up = inp("b_up", [128, 88])
    c.w_conv = inp("w_conv", [128, 3 * 88])
    c.b_conv = inp("b_conv", [128, 88])
    c.w_down = inp("w_down", [DFF, D])
    c.b_down = inp("b_down", [1, D])
    c.ln2_g = inp("ln2_g", [1, D])
    c.ln2_b = inp("ln2_b", [1, D])
    c.flags = inp("flags", [128, 4])
    c.consts = inp("consts", [128, 512])
    c.pmats = inp("pmats", [128, 20 * 128])
    c.y = nc.dram_tensor("y", [lay.NF * 128, D], F32, kind="ExternalOutput").ap()
    c.debug = debug
    c.nc = nc

    def dbg(P, name, tile_ap, shape, dt, reads):
        if not debug:
            return
        t = nc.dram_tensor("dbg_" + name, shape, dt, kind="ExternalOutput").ap()
        P.dma('sp', lambda e: e.dma_start(out=t, in_=tile_ap), reads=reads)
    c.dbg = dbg
    c.k_s = scr("k_s", [NCH, 128, 1024], BF16)
    c.v_s = scr("v_s", [NCH, 128, 1024], BF16)
    c.sb_s = scr("sb_s", [NCH, 128, 8 * 257], BF16)
    c.g_s = scr("g_s", [128, NCH * 32], F32)
    c.qT_s = scr("qT_s", [NCH, 128, 1024], BF16)
    c.so_s = scr("so_s", [NCH, 128, 1024], BF16)
    c.u_s = scr("u_s", [NCH, 128, 1024], BF16)
    c.hm_s = scr("hm_s", [NCH, 128, 1024], BF16)
    c.x1_s = scr("x1_s", [NT, D], F32)
    if debug:
        c.x1d_s = scr("x1d_s", [NT, D], F32)
    c.x1T_s = scr("x1T_s", [lay.NM, 128, 2048], BF16)
    c.sf0_s = scr("sf0_s", [128, 8 * 257], F32)
    c.wu_t = scr("wu_t", [22, 128, 16 * 2 * 256], BF16)
    c.wd_t = scr("wd_t", [8, 128, 44 * 256], BF16)
    c.xh = es.enter_context(nc.sbuf_tensor("g_xh", [128, 16, 2 * lay.NTILE], BF16))
    return c


def phase1(nc, c):
    NCH = c.NCH
    P = Prog(nc, "p1", c.sems)
    with ExitStack() as es:
        sb, ps = _alloc(es, nc)
        wkv = sb("p1_wkv", [128, 16, 2048], BF16)
        wg = sb("p1_wg", [128, 16, 16], BF16)
        bias = sb("p1_bias", [128, 2064], F32)
        cst = sb("p1_cst", [128, 512], F32)
        flg = sb("p1_flg", [128, 4], F32)
        xt = sb("p1_xt", [128, 3, 2048], BF16)
        kb = sb("p1_kb", [128, 2, 1024], BF16)
        va = sb("p1_va", [128, 2, 4, 257], BF16)
        kd = sb("p1_kd", [128, 2, 1024], BF16)
        gs = sb("p1_gs", [128, 16], F32)
        e1 = sb("p1_e1", [128, 8], F32)
        Lt = sb("p1_L", [128, 8], F32)
        tmp8 = sb("p1_tmp8", [128, 8], F32)
        wv = sb("p1_wv", [128, 8], F32)
        G = sb("p1_G", [128, NCH, 32], F32)
        S = sb("p1_S", [128, 8, 257], F32)
        Sst = sb("p1_Sst", [128, 2, 8 * 257], BF16)
        psA = ps("p1_psA", [128, 2, 512])
        psG = ps("p1_psG", [128, 512])
        psS = ps("p1_psS", [128, 4, 512])

        wsrc = c.w_kv.rearrange("(kt p) n -> p kt n", p=128)
        for i in range(4):
            P.dma('pool', lambda e, i=i: e.dma_start(out=wkv[:, 4 * i:4 * i + 4, :], in_=wsrc[:, 4 * i:4 * i + 4, :]),
                  writes=[('wkv', i)])
        P.dma('pool', lambda e: e.dma_start(out=wg[:], in_=c.w_g.rearrange("(kt p) n -> p kt n", p=128)),
              writes=['wg'])
        P.dma('sp', lambda e: e.dma_start(out=bias[:], in_=c.b_kvg.to_broadcast([128, 2064])), writes=['bias'])
        P.dma('sp', lambda e: e.dma_start(out=cst[:], in_=c.consts), writes=['cst'])
        P.dma('sp', lambda e: e.dma_start(out=flg[:], in_=c.flags), writes=['flg'])
        P.dve(lambda e: e.memset(S[:], 0.0), writes=[('S', i) for i in range(8)])
        for s in range(2):
            P.pool(lambda e, s=s: e.memset(va[:, s, :, 256:257], 1.0), writes=[('va', s)])
        trif = cst[:, 0:128]
        trib = cst[:, 128:256]
        ones = cst[:, 256:384]
        WKV = [('wkv', i) for i in range(4)]

        lay = c.lay
        SK = [('S', i) for i in range(8)]
        Mset = set(lay.M)
        order = []
        for sl in range(lay.S0, lay.S0 + lay.S_bef):
            order.append([sl, 'f', None, None])
        if lay.S_bef:
            order[0][2] = 'zero'
            order[-1][3] = 'sf0'
        first = len(order)
        for sl in range(lay.NPS - 1, -1, -1):
            order.append([sl, 'b', None, None])
        order[first][2] = 'zero'
        first = len(order)
        for sl in range(lay.S0 + lay.NSS - 1, lay.S0 + lay.S_bef - 1, -1):
            order.append([sl, 'b', None, None])
        order[first][2] = 'zero'
        if lay.S_pad:
            order[first + lay.S_pad - 1][3] = 'fpad'
        NORD = len(order)

        def load_x(pos):
            s = pos % 3
            sl = order[pos][0]
            P.dma('pool', lambda e: e.dma_start(out=xt[:, s, :], in_=c.xT[sl]), writes=[('xt', s)])

        load_x(0)
        if NORD > 1:
            load_x(1)
        def part_a(idx, j):
            mode = order[idx][1]
            if idx + 2 < NORD:
                load_x(idx + 2)
            s3 = idx % 3
            s2 = idx % 2
            for kt in range(16):
                P.pe(lambda e, kt=kt: e.matmul(
                    psG[:, 0:16], lhsT=xt[:, s3, kt * 128:(kt + 1) * 128], rhs=wg[:, kt, :],
                    start=(kt == 0), stop=(kt == 15)),
                    reads=[('xt', s3), 'wg'], banks=['G'])
            P.dve(lambda e: e.tensor_tensor(out=gs[:], in0=psG[:, 0:16], in1=bias[:, 2048:2064], op=ALU.add),
                  banks=['G'], reads=['bias'], writes=['gs'])
            P.act(lambda e: e.activation(out=e1[:], in_=gs[:, 8:16], func=AF.Exp, scale=-1.0),
                  reads=['gs'], writes=['e1'])
            P.act(lambda e: e.activation(out=Lt[:], in_=e1[:], func=AF.Ln, bias=1.0),
                  reads=['e1'], writes=['L'])
            CGS = (0, 1)
            for cg in CGS:
                b = cg % 2
                for kt in range(16):
                    P.pe(lambda e, b=b, kt=kt, cg=cg: e.matmul(
                        psA[:, b, :], lhsT=xt[:, s3, kt * 128:(kt + 1) * 128],
                        rhs=wkv[:, kt, cg * 512:(cg + 1) * 512], start=(kt == 0), stop=(kt == 15)),
                        reads=[('xt', s3)] + WKV, banks=[('A', b)])
                if cg < 2:
                    P.dve(lambda e, b=b, cg=cg: e.tensor_tensor(
                        out=kb[:, s2, cg * 512:(cg + 1) * 512], in0=psA[:, b, :],
                        in1=bias[:, cg * 512:(cg + 1) * 512], op=ALU.add),
                        banks=[('A', b)], reads=['bias'], writes=[('kb', s2, cg)])
                else:
                    h0 = (cg - 2) * 2
                    P.dve(lambda e, b=b, cg=cg, h0=h0: e.tensor_tensor(
                        out=va[:, s2, h0:h0 + 2, 0:256],
                        in0=psA[:, b, :].rearrange("p (h d) -> p h d", h=2),
                        in1=bias[:, cg * 512:(cg + 1) * 512].rearrange("p (h d) -> p h d", h=2), op=ALU.add),
                        banks=[('A', b)], reads=['bias'], writes=[('va', s2)])
            P.pe(lambda e: e.matmul(psG[:, 32:36], lhsT=trif, rhs=Lt[:, 0:4], start=True, stop=True),
                 reads=['L', 'cst'], banks=['G'])
            P.pe(lambda e: e.matmul(psG[:, 36:40], lhsT=trib, rhs=Lt[:, 4:8], start=True, stop=True),
                 reads=['L', 'cst'], banks=['G'])
            P.pe(lambda e: e.matmul(psG[:, 40:48], lhsT=ones, rhs=Lt[:, 0:8], start=True, stop=True),
                 reads=['L', 'cst'], banks=['G'])
            P.dve(lambda e: e.tensor_tensor(out=tmp8[:], in0=psG[:, 32:40], in1=gs[:, 0:8], op=ALU.add),
                  banks=['G'], reads=['gs'], writes=['tmp8'])
            P.act(lambda e: e.activation(out=wv[:], in_=tmp8[:], func=AF.Exp), reads=['tmp8'], writes=['wv'])
            P.act(lambda e, j=j: e.activation(out=G[:, j, 8:16], in_=psG[:, 32:40], func=AF.Exp),
                  banks=['G'], writes=[('G', j, 1)])
            P.act(lambda e, j=j: e.activation(out=G[:, j, 16:24], in_=psG[:, 40:48], func=AF.Exp, scale=-1.0),
                  banks=['G'], writes=[('G', j, 2)])
            P.dve(lambda e, j=j: e.tensor_scalar(out=G[:, j, 0:8], in0=wv[:], scalar1=1.0 / 16.0, scalar2=None,
                                                 op0=ALU.mult),
                  reads=['wv'], writes=[('G', j, 0)])
            P.dve(lambda e, j=j: e.tensor_tensor(out=G[:, j, 24:32], in0=G[:, j, 0:8], in1=G[:, j, 16:24],
                                                 op=ALU.mult),
                  reads=[('G', j, 0), ('G', j, 2)], writes=[('G', j, 3)])
            CGS = (2, 3)
            for cg in CGS:
                b = cg % 2
                for kt in range(16):
                    P.pe(lambda e, b=b, kt=kt, cg=cg: e.matmul(
                        psA[:, b, :], lhsT=xt[:, s3, kt * 128:(kt + 1) * 128],
                        rhs=wkv[:, kt, cg * 512:(cg + 1) * 512], start=(kt == 0), stop=(kt == 15)),
                        reads=[('xt', s3)] + WKV, banks=[('A', b)])
                if cg < 2:
                    P.dve(lambda e, b=b, cg=cg: e.tensor_tensor(
                        out=kb[:, s2, cg * 512:(cg + 1) * 512], in0=psA[:, b, :],
                        in1=bias[:, cg * 512:(cg + 1) * 512], op=ALU.add),
                        banks=[('A', b)], reads=['bias'], writes=[('kb', s2, cg)])
                else:
                    h0 = (cg - 2) * 2
                    P.dve(lambda e, b=b, cg=cg, h0=h0: e.tensor_tensor(
                        out=va[:, s2, h0:h0 + 2, 0:256],
                        in0=psA[:, b, :].rearrange("p (h d) -> p h d", h=2),
                        in1=bias[:, cg * 512:(cg + 1) * 512].rearrange("p (h d) -> p h d", h=2), op=ALU.add),
                        banks=[('A', b)], reads=['bias'], writes=[('va', s2)])
            if j in Mset and mode == 'b':
                P.dma('sp', lambda e, j=j: e.dma_start(out=c.k_s[j], in_=kb[:, s2, :]),
                      reads=[('kb', s2, 0), ('kb', s2, 1)])
                P.dma('sp', lambda e, j=j: e.dma_start(out=c.v_s[j].rearrange("p (h d) -> p h d", h=4),
                                                       in_=va[:, s2, :, 0:256]), reads=[('va', s2)])

        def part_b1(idx, j):
            mode, pre, post = order[idx][1], order[idx][2], order[idx][3]
            kcol = 28 if mode == 'b' else 24
            dcol = 20 if mode == 'b' else 16
            s2 = idx % 2
            if pre == 'zero':
                P.dve(lambda e: e.memset(S[:], 0.0), reads=SK, writes=SK)
            if j in Mset and mode == 'b':
                P.act(lambda e: e.copy(out=Sst[:, s2, :], in_=S[:].rearrange("p a b -> p (a b)")),
                      reads=[('S', i) for i in range(8)], writes=[('Sst', s2)])
                P.dma('sp', lambda e, j=j: e.dma_start(out=c.sb_s[j], in_=Sst[:, s2, :]), reads=[('Sst', s2)])
            for h in range(4):
                P.act(lambda e, h=h, j=j: e.activation(
                    out=kd[:, s2, h * 256:(h + 1) * 256], in_=kb[:, s2, h * 256:(h + 1) * 256],
                    func=AF.Copy, scale=G[:, j, kcol + h:kcol + 1 + h]),
                    reads=[('kb', s2, h // 2), ('G', j, 3)], writes=[('kd', s2, h)])

        def part_b2(idx, j):
            mode, pre, post = order[idx][1], order[idx][2], order[idx][3]
            kcol = 28 if mode == 'b' else 24
            dcol = 20 if mode == 'b' else 16
            s2 = idx % 2
            for h in range(4):
                for db in range(2):
                    i8 = h * 2 + db
                    bk = i8 % 4
                    P.pe(lambda e, h=h, db=db, bk=bk: e.matmul(
                        psS[:, bk, 0:257], lhsT=kd[:, s2, h * 256 + db * 128:h * 256 + db * 128 + 128],
                        rhs=va[:, s2, h, :], start=True, stop=True),
                        reads=[('kd', s2, h), ('va', s2)], banks=[('S', bk)])
                    P.dve(lambda e, h=h, i8=i8, bk=bk, j=j: e.scalar_tensor_tensor(
                        out=S[:, i8, :], in0=S[:, i8, :], scalar=G[:, j, dcol + h:dcol + 1 + h], in1=psS[:, bk, 0:257],
                        op0=ALU.mult, op1=ALU.add),
                        banks=[('S', bk)], reads=[('S', i8), ('G', j, 2)], writes=[('S', i8)])
            if post == 'fpad':
                P.dve(lambda e: e.tensor_scalar(out=S[:], in0=S[:], scalar1=flg[:, 2:3], scalar2=None, op0=ALU.mult),
                      reads=SK + ['flg'], writes=SK)
            if post == 'sf0':
                P.dve(lambda e: e.tensor_scalar(out=S[:], in0=S[:], scalar1=flg[:, 0:1], scalar2=None, op0=ALU.mult),
                      reads=SK + ['flg'], writes=SK)
                P.dma('sp', lambda e: e.dma_start(out=c.sf0_s, in_=S[:].rearrange("p a b -> p (a b)")), reads=SK)
        for idx in range(NORD):
            if idx > 0:
                part_b1(idx - 1, order[idx - 1][0])
            part_a(idx, order[idx][0])
            if idx > 0:
                part_b2(idx - 1, order[idx - 1][0])
        part_b1(NORD - 1, order[NORD - 1][0])
        part_b2(NORD - 1, order[NORD - 1][0])
        P.dma('sp', lambda e: e.dma_start(out=c.g_s, in_=G[:].rearrange("p a b -> p (a b)")),
              reads=[('G', j, i) for j in range(NCH) for i in range(4)])
        P.run()


def phase2(nc, c):
    M2 = c.lay.M2
    NST = len(M2) // 4
    P = Prog(nc, "p2", c.sems)
    with ExitStack() as es:
        sb, ps = _alloc(es, nc)
        w = sb("p2_w", [128, 16, 3072], BF16)
        bq = sb("p2_bq", [128, 8], F32)
        bou = sb("p2_bou", [128, 2048], F32)
        gn = sb("p2_gn", [128, 1024], F32)
        xt = sb("p2_xt", [128, 2, 4, 2048], BF16)
        qst = sb("p2_qst", [128, 2, 8, 512], BF16)
        ot = sb("p2_ot", [128, 2, 512], F32)
        ot2 = sb("p2_ot2", [128, 2, 512], F32)
        so = sb("p2_so", [128, 2, 1024], BF16)
        ub = sb("p2_ub", [128, 2, 1024], BF16)
        psQ = ps("p2_psQ", [128, 2, 512])
        psO = ps("p2_psO", [128, 4, 512])

        wsrc = c.w_qou.rearrange("(kt p) n -> p kt n", p=128)
        for i in range(4):
            for cgp in range(3):
                P.dma('pool', lambda e, i=i, cgp=cgp: e.dma_start(
                    out=w[:, 4 * i:4 * i + 4, cgp * 1024:(cgp + 1) * 1024],
                    in_=wsrc[:, 4 * i:4 * i + 4, cgp * 1024:(cgp + 1) * 1024]), writes=[('w', i, cgp)])
        WQ = [('w', i, 0) for i in range(4)]
        WO = [('w', i, 1) for i in range(4)]
        WU = [('w', i, 2) for i in range(4)]
        P.dma('sp', lambda e: e.dma_start(out=bq[:], in_=c.b_q), writes=['bq'])
        P.dma('sp', lambda e: e.dma_start(out=bou[:], in_=c.b_ou.to_broadcast([128, 2048])), writes=['bou'])
        P.dma('sp', lambda e: e.dma_start(out=gn[:], in_=c.gain.to_broadcast([128, 1024])), writes=['gn'])

        def load_x(st):
            s = st % 2
            for cc in range(4):
                P.dma('pool', lambda e, cc=cc: e.dma_start(out=xt[:, s, cc, :], in_=c.xT[M2[st * 4 + cc]]),
                      writes=[('xt', s, cc)])

        cnt = [0]

        def do_st(st):
            if st + 1 < NST:
                load_x(st + 1)
            s = st % 2
            XT = [('xt', s, cc) for cc in range(4)]
            for blk in range(8):
                b = blk % 2
                for kt in range(16):
                    P.pe(lambda e, blk=blk, b=b, kt=kt: e.matmul(
                        psQ[:, b, :], lhsT=w[:, kt, blk * 128:(blk + 1) * 128],
                        rhs=xt[:, s, :, kt * 128:(kt + 1) * 128], start=(kt == 0), stop=(kt == 15)),
                        reads=XT + WQ, banks=[('Q', b)])
                P.act(lambda e, blk=blk, b=b: e.activation(
                    out=qst[:, s, blk, :], in_=psQ[:, b, :], func=AF.Identity, bias=bq[:, blk:blk + 1]),
                    banks=[('Q', b)], reads=['bq'], writes=[('qst', s, blk)])
            for cc in range(4):
                j = M2[st * 4 + cc]
                P.dma('sp', lambda e, cc=cc, j=j: e.dma_start(
                    out=c.qT_s[j].rearrange("p (b t) -> p b t", b=8), in_=qst[:, s, :, cc * 128:(cc + 1) * 128]),
                    reads=[('qst', s, blk) for blk in range(8)])
            for cc in range(4):
                j = M2[st * 4 + cc]
                s2 = cnt[0] % 2
                cnt[0] += 1
                for cg in range(4):
                    b = cg
                    for kt in range(16):
                        P.pe(lambda e, cc=cc, cg=cg, b=b, kt=kt: e.matmul(
                            psO[:, b, :], lhsT=xt[:, s, cc, kt * 128:(kt + 1) * 128],
                            rhs=w[:, kt, 1024 + cg * 512:1024 + (cg + 1) * 512], start=(kt == 0), stop=(kt == 15)),
                            reads=[('xt', s, cc)] + (WO if cg < 2 else WU), banks=[('O', b)])
                    if cg < 2:
                        P.dve(lambda e, cg=cg, b=b: e.tensor_tensor(
                            out=ot[:, cg, :], in0=psO[:, b, :], in1=bou[:, cg * 512:(cg + 1) * 512], op=ALU.add),
                            banks=[('O', b)], reads=['bou'], writes=[('ot', cg)])
                        P.act(lambda e, cg=cg: e.activation(out=ot2[:, cg, :], in_=ot[:, cg, :], func=AF.Sigmoid),
                              reads=[('ot', cg)], writes=[('ot2', cg)])
                        P.pool(lambda e, cg=cg, s2=s2: e.tensor_tensor(
                            out=so[:, s2, cg * 512:(cg + 1) * 512], in0=ot2[:, cg, :],
                            in1=gn[:, cg * 512:(cg + 1) * 512], op=ALU.mult),
                            reads=[('ot2', cg), 'gn'], writes=[('so', s2, cg)])
                    else:
                        P.dve(lambda e, cg=cg, b=b, s2=s2: e.tensor_tensor(
                            out=ub[:, s2, (cg - 2) * 512:(cg - 1) * 512], in0=psO[:, b, :],
                            in1=bou[:, cg * 512:(cg + 1) * 512], op=ALU.add),
                            banks=[('O', b)], reads=['bou'], writes=[('ub', s2, cg)])
                P.dma('sp', lambda e, j=j, s2=s2: e.dma_start(out=c.so_s[j], in_=so[:, s2, :]),
                      reads=[('so', s2, 0), ('so', s2, 1)])
                P.dma('sp', lambda e, j=j, s2=s2: e.dma_start(out=c.u_s[j], in_=ub[:, s2, :]),
                      reads=[('ub', s2, 2), ('ub', s2, 3)])

        load_x(0)
        for st in range(NST):
            do_st(st)
        P.run()


def phase3(nc, c):
    NCH = c.NCH
    lay = c.lay
    M = lay.M
    NM = lay.NM
    RS = lay.segs[1][0]
    P = Prog(nc, "p3", c.sems)
    with ExitStack() as es:
        sb, ps = _alloc(es, nc)
        G = sb("p3_G", [128, NCH, 32], F32)
        cst = sb("p3_cst", [128, 512], F32)
        identb = sb("p3_identb", [128, 128], BF16)
        flg = sb("p3_flg", [128, 4], F32)
        qT = sb("p3_qT", [128, 3, 1024], BF16)
        kb = sb("p3_kb", [128, 3, 1024], BF16)
        va = sb("p3_va", [128, 3, 4, 257], BF16)
        so = sb("p3_so", [128, 3, 1024], BF16)
        sbs = sb("p3_sbs", [128, 3, 8, 257], BF16)
        Sf = sb("p3_Sf", [128, 8, 257], F32)
        Sfb = sb("p3_Sfb", [128, 2, 8, 257], BF16)
        kT = sb("p3_kT", [128, 2, 1024], BF16)
        kdf = sb("p3_kdf", [128, 2, 1024], BF16)
        STf = sb("p3_STf", [128, 2, 4, 128], BF16)
        STb = sb("p3_STb", [128, 2, 4, 128], BF16)
        hf32 = sb("p3_hf32", [128, 2, 1024], F32)
        h32 = sb("p3_h32", [128, 2, 1024], F32)
        stats = sb("p3_stats", [128, 4, 6], F32)
        mv = sb("p3_mv", [128, 4, 2], F32)
        den = sb("p3_den", [128, 4, 2], F32)
        rden = sb("p3_rden", [128, 4, 2], F32)
        lnv = sb("p3_lnv", [128, 4], F32)
        rstd = sb("p3_rstd", [128, 4], F32)
        nmr = sb("p3_nmr", [128, 4], F32)
        hm = sb("p3_hm", [128, 2, 1024], BF16)
        psT = ps("p3_psT", [128, 512])
        psSc = ps("p3_psSc", [128, 4, 128])
        psP = ps("p3_psP", [128, 4, 512])
        psD = ps("p3_psD", [128, 2, 512])
        psTb = psT[:].bitcast(BF16)

        P.dma('sp', lambda e: e.dma_start(out=G[:].rearrange("p a b -> p (a b)"), in_=c.g_s), writes=['G'])
        P.dma('sp', lambda e: e.dma_start(out=cst[:], in_=c.consts), writes=['cst'])
        P.dma('sp', lambda e: e.dma_start(out=flg[:], in_=c.flags), writes=['flg'])
        P.dve(lambda e: e.tensor_copy(out=identb[:], in_=cst[:, 384:512]), reads=['cst'], writes=['identb'])
        P.dve(lambda e: e.memset(Sf[:], 0.0), writes=[('Sf', i) for i in range(8)])
        P.pool(lambda e: e.memset(Sfb[:, 0, :, :], 0.0), writes=[('Sfb', 0)])
        for l in range(3):
            P.pool(lambda e, l=l: e.memset(va[:, l, :, 256:257], 1.0), writes=[('va', l)])
        trif = cst[:, 0:128]
        trib = cst[:, 128:256]

        def load(r):
            j = M[r]
            l = r % 3
            P.dma('sp', lambda e: e.dma_start(out=qT[:, l, :], in_=c.qT_s[j]), writes=[('qT', l)])
            P.dma('sp', lambda e: e.dma_start(out=kb[:, l, :], in_=c.k_s[j]), writes=[('kb', l)])
            P.dma('sp', lambda e: e.dma_start(out=va[:, l, :, 0:256],
                                              in_=c.v_s[j].rearrange("p (h d) -> p h d", h=4)), writes=[('va', l)])
            P.dma('sp', lambda e: e.dma_start(out=so[:, l, :], in_=c.so_s[j]), writes=[('so', l)])
            P.dma('sp', lambda e: e.dma_start(out=sbs[:, l, :, :].rearrange("p a b -> p (a b)"), in_=c.sb_s[j]),
                  writes=[('sbs', l)])

        def ctx(r):
            return M[r], r % 2, (r + 1) % 2, r % 3

        def st_init(r):
            j, s, nx, l = ctx(r)
            if r == RS:
                P.dma('sp', lambda e: e.dma_start(out=Sf[:].rearrange("p a b -> p (a b)"), in_=c.sf0_s),
                      writes=[('Sf', i) for i in range(8)])
                P.act(lambda e: e.copy(out=Sfb[:, s, :, :], in_=Sf[:]), reads=[('Sf', i) for i in range(8)],
                      writes=[('Sfb', s)])

        def st_x1(r):
            j, s, nx, l = ctx(r)
            for blk in range(8):
                P.pe(lambda e, blk=blk: e.transpose(
                    out=psTb[:, blk * 128:(blk + 1) * 128], in_=kb[:, l, blk * 128:(blk + 1) * 128],
                    identity=identb[:]),
                    reads=[('kb', l), 'identb'], banks=['T'])
            P.dve(lambda e: e.tensor_copy(out=kT[:, s, :], in_=psTb), banks=['T'], writes=[('kT', s)])
            for h in range(4):
                P.pool(lambda e, h=h: e.tensor_scalar(
                    out=kdf[:, s, h * 256:(h + 1) * 256], in0=kb[:, l, h * 256:(h + 1) * 256],
                    scalar1=G[:, j, 24 + h:25 + h], scalar2=0.0, op0=ALU.mult, op1=ALU.add),
                    reads=[('kb', l), 'G'], writes=[('kdf', s, h)])

        def st_x2(r):
            j, s, nx, l = ctx(r)
            for h in range(4):
                for blk in range(2):
                    cb = (2 * h + blk) * 128
                    P.pe(lambda e, h=h, blk=blk, cb=cb: e.matmul(
                        psSc[:, h, :], lhsT=kT[:, s, cb:cb + 128], rhs=qT[:, l, cb:cb + 128],
                        start=(blk == 0), stop=(blk == 1)),
                        reads=[('kT', s), ('qT', l)], banks=['Sc'])
            for h in range(4):
                P.dve(lambda e, h=h: e.scalar_tensor_tensor(
                    out=STf[:, s, h, :], in0=psSc[:, h, :], scalar=G[:, j, h:h + 1], in1=trif,
                    op0=ALU.mult, op1=ALU.mult),
                    banks=['Sc'], reads=['G', 'cst'], writes=[('STf', s, h)])
                P.dve(lambda e, h=h: e.scalar_tensor_tensor(
                    out=STb[:, s, h, :], in0=psSc[:, h, :], scalar=G[:, j, 4 + h:5 + h], in1=trib,
                    op0=ALU.mult, op1=ALU.mult),
                    banks=['Sc'], reads=['G', 'cst'], writes=[('STb', s, h)])

        def st_u(r):
            j, s, nx, l = ctx(r)
            for h in range(4):
                for db in range(2):
                    i8 = 2 * h + db
                    bk = i8 % 2
                    P.pe(lambda e, h=h, db=db, bk=bk: e.matmul(
                        psD[:, bk, 0:257], lhsT=kdf[:, s, h * 256 + db * 128:h * 256 + db * 128 + 128],
                        rhs=va[:, l, h, :], start=True, stop=True),
                        reads=[('kdf', s, h), ('va', l)], banks=[('D', bk)])
                    P.dve(lambda e, h=h, i8=i8, bk=bk: e.scalar_tensor_tensor(
                        out=Sf[:, i8, :], in0=Sf[:, i8, :], scalar=G[:, j, 16 + h:17 + h], in1=psD[:, bk, 0:257],
                        op0=ALU.mult, op1=ALU.add),
                        banks=[('D', bk)], reads=[('Sf', i8), 'G'], writes=[('Sf', i8)])
            SFK = [('Sf', i) for i in range(8)]
            if r == RS:
                P.dve(lambda e: e.tensor_scalar(out=Sf[:], in0=Sf[:], scalar1=flg[:, 0:1], scalar2=None,
                                                op0=ALU.mult), reads=SFK + ['flg'], writes=SFK)
            P.act(lambda e: e.copy(out=Sfb[:, nx, :, :], in_=Sf[:]), reads=SFK, writes=[('Sfb', nx)])

        def st_y(r):
            j, s, nx, l = ctx(r)
            for hp in range(2):
                hs = (2 * hp, 2 * hp + 1)
                for h in hs:
                    bf_ = 2 * (h % 2)
                    bb_ = bf_ + 1
                    c0 = (2 * h) * 128
                    c1 = (2 * h + 1) * 128
                    P.pe(lambda e, h=h, bf_=bf_: e.matmul(psP[:, bf_, 0:257], lhsT=STf[:, s, h, :], rhs=va[:, l, h, :],
                                                          start=True, stop=False),
                         reads=[('STf', s, h), ('va', l)], banks=[('P', bf_)])
                    P.pe(lambda e, h=h, bf_=bf_, c0=c0: e.matmul(psP[:, bf_, 0:257], lhsT=qT[:, l, c0:c0 + 128],
                                                                 rhs=Sfb[:, s, 2 * h, :], start=False, stop=False),
                         reads=[('qT', l), ('Sfb', s)], banks=[('P', bf_)])
                    P.pe(lambda e, h=h, bf_=bf_, c1=c1: e.matmul(psP[:, bf_, 0:257], lhsT=qT[:, l, c1:c1 + 128],
                                                                 rhs=Sfb[:, s, 2 * h + 1, :], start=False, stop=True),
                         reads=[('qT', l), ('Sfb', s)], banks=[('P', bf_)])
                    P.pe(lambda e, h=h, bb_=bb_: e.matmul(psP[:, bb_, 0:257], lhsT=STb[:, s, h, :], rhs=va[:, l, h, :],
                                                          start=True, stop=False),
                         reads=[('STb', s, h), ('va', l)], banks=[('P', bb_)])
                    P.pe(lambda e, h=h, bb_=bb_, c0=c0: e.matmul(psP[:, bb_, 0:257], lhsT=qT[:, l, c0:c0 + 128],
                                                                 rhs=sbs[:, l, 2 * h, :], start=False, stop=False),
                         reads=[('qT', l), ('sbs', l)], banks=[('P', bb_)])
                    P.pe(lambda e, h=h, bb_=bb_, c1=c1: e.matmul(psP[:, bb_, 0:257], lhsT=qT[:, l, c1:c1 + 128],
                                                                 rhs=sbs[:, l, 2 * h + 1, :], start=False, stop=True),
                         reads=[('qT', l), ('sbs', l)], banks=[('P', bb_)])
                for h in hs:
                    bf_ = 2 * (h % 2)
                    bb_ = bf_ + 1
                    c0 = (2 * h) * 128
                    c1 = (2 * h + 1) * 128
                    P.act(lambda e, h=h, bf_=bf_: e.activation(out=den[:, h, 0:1], in_=psP[:, bf_, 256:257], func=AF.Abs),
                          banks=[('P', bf_)], reads=[], writes=[('den', h)])
                    P.act(lambda e, h=h, bb_=bb_: e.activation(out=den[:, h, 1:2], in_=psP[:, bb_, 256:257], func=AF.Abs),
                          banks=[('P', bb_)], reads=[], writes=[('den', h)])
                h0 = 2 * hp
                P.dve(lambda e, h0=h0: e.tensor_tensor(
                    out=den[:, h0:h0 + 2, :], in0=den[:, h0:h0 + 2, :],
                    in1=G[:, j, 8:16].rearrange("p (d h) -> p h d", d=2)[:, h0:h0 + 2, :], op=ALU.max),
                    reads=[('den', h0), ('den', h0 + 1), 'G'], writes=[('den', h0), ('den', h0 + 1)])
                P.dve(lambda e, h0=h0: e.reciprocal(out=rden[:, h0:h0 + 2, :], in_=den[:, h0:h0 + 2, :]),
                      reads=[('den', h0), ('den', h0 + 1)], writes=[('rden', h0), ('rden', h0 + 1)])
                for h in hs:
                    bf_ = 2 * (h % 2)
                    bb_ = bf_ + 1
                    c0 = (2 * h) * 128
                    c1 = (2 * h + 1) * 128
                    P.act(lambda e, h=h, bf_=bf_: e.activation(
                        out=hf32[:, s, h * 256:(h + 1) * 256], in_=psP[:, bf_, 0:256], func=AF.Copy,
                        scale=rden[:, h, 0:1]),
                        banks=[('P', bf_)], reads=[('rden', h)], writes=[('hf32', s, h)])
                for h in hs:
                    bf_ = 2 * (h % 2)
                    bb_ = bf_ + 1
                    c0 = (2 * h) * 128
                    c1 = (2 * h + 1) * 128
                    P.dve(lambda e, h=h, bb_=bb_: e.scalar_tensor_tensor(
                        out=h32[:, s, h * 256:(h + 1) * 256], in0=psP[:, bb_, 0:256], scalar=rden[:, h, 1:2],
                        in1=hf32[:, s, h * 256:(h + 1) * 256], op0=ALU.mult, op1=ALU.add),
                        banks=[('P', bb_)], reads=[('rden', h), ('hf32', s, h)], writes=[('h32', s, h)])
                for h in hs:
                    P.dve(lambda e, h=h: e.bn_stats(out=stats[:, h, :], in_=h32[:, s, h * 256:(h + 1) * 256]),
                          reads=[('h32', s, h)], writes=[('stats', h)])


        def st_z(r):
            j, s, nx, l = ctx(r)
            SFK = [('Sf', i) for i in range(8)]
            for h in range(4):
                P.dve(lambda e, h=h: e.bn_aggr(out=mv[:, h, :], in_=stats[:, h, :]),
                      reads=[('stats', h)], writes=[('mv', h)])
            MV = [('mv', h) for h in range(4)]
            P.act(lambda e: e.activation(out=lnv[:], in_=mv[:, :, 1], func=AF.Ln, bias=LN_EPS),
                  reads=MV, writes=['lnv'])
            P.act(lambda e: e.activation(out=rstd[:], in_=lnv[:], func=AF.Exp, scale=-0.5),
                  reads=['lnv'], writes=['rstd'])
            P.dve(lambda e: e.scalar_tensor_tensor(out=nmr[:], in0=mv[:, :, 0], scalar=-1.0, in1=rstd[:],
                                                   op0=ALU.mult, op1=ALU.mult),
                  reads=MV + ['rstd'], writes=['nmr'])
            for h in range(4):
                P.act(lambda e, h=h: e.activation(
                    out=hf32[:, s, h * 256:(h + 1) * 256], in_=h32[:, s, h * 256:(h + 1) * 256], func=AF.Identity,
                    scale=rstd[:, h:h + 1], bias=nmr[:, h:h + 1]),
                    reads=[('h32', s, h), 'rstd', 'nmr'], writes=[('hf32', s, h)])
            P.pool(lambda e: e.tensor_tensor(out=hm[:, s, :], in0=hf32[:, s, :], in1=so[:, l, :], op=ALU.mult),
                   reads=[('hf32', s, h) for h in range(4)] + [('so', l)], writes=[('hm', s)])
            P.dma('pool', lambda e: e.dma_start(out=c.hm_s[j], in_=hm[:, s, :]), reads=[('hm', s)])
            if r == 0:
                H4 = list(range(4))
                c.dbg(P, 'kT', kT[:, s, :], [128, 1024], BF16, [('kT', s)])
                c.dbg(P, 'identb', identb[:], [128, 128], BF16, ['identb'])
                c.dbg(P, 'kb', kb[:, l, :], [128, 1024], BF16, [('kb', l)])
                c.dbg(P, 'cst', cst[:], [128, 512], F32, ['cst'])
                c.dbg(P, 'STf', STf[:, s, :, :].rearrange("p a b -> p (a b)"), [128, 512], BF16, [('STf', s, h) for h in H4])
                c.dbg(P, 'STb', STb[:, s, :, :].rearrange("p a b -> p (a b)"), [128, 512], BF16, [('STb', s, h) for h in H4])
                c.dbg(P, 'h32', h32[:, s, :], [128, 1024], F32, [('h32', s, h) for h in H4])
                c.dbg(P, 'hn', hf32[:, s, :], [128, 1024], F32, [('hf32', s, h) for h in H4])
                c.dbg(P, 'den', den[:].rearrange("p a b -> p (a b)"), [128, 8], F32, [('den', h) for h in H4])
                c.dbg(P, 'rden', rden[:].rearrange("p a b -> p (a b)"), [128, 8], F32, [('rden', h) for h in H4])
                c.dbg(P, 'mv', mv[:].rearrange("p a b -> p (a b)"), [128, 8], F32, [('mv', h) for h in H4])
                c.dbg(P, 'rstd', rstd[:], [128, 4], F32, ['rstd'])
                c.dbg(P, 'nmr', nmr[:], [128, 4], F32, ['nmr'])
                c.dbg(P, 'kdf', kdf[:, s, :], [128, 1024], BF16, [('kdf', s, h) for h in H4])
                c.dbg(P, 'Sf', Sf[:].rearrange("p a b -> p (a b)"), [128, 8 * 257], F32, SFK)

        load(0)
        if NM > 1:
            load(1)
        st_x1(0)
        st_x2(0)
        nconv = 0
        for r in range(NM):
            if r + 2 < NM:
                load(r + 2)
            if r + 1 < NM:
                st_x1(r + 1)
            st_init(r)
            st_u(r)
            if r + 1 < NM:
                st_x2(r + 1)
            st_y(r)
            st_z(r)
            want = min(NCONV, ((r + 1) * NCONV + NM - 1) // NM)
            while nconv < want:
                emit_weight_convert(P, c, nconv)
                nconv += 1
        P.run()


def phase4(nc, c):
    lay = c.lay
    M = lay.M
    NCH = lay.NM
    FIRST = [a for a, b in lay.segs]
    LAST = [b - 1 for a, b in lay.segs]
    SECOND_S = lay.segs[1][0] + 1
    P = Prog(nc, "p4", c.sems)
    with ExitStack() as es:
        sb, ps = _alloc(es, nc)
        wout = sb("p4_wout", [128, 16, 2048], BF16)
        wpr = sb("p4_wpr", [128, 4, 2, 256], F32)
        wp = sb("p4_wp", [128, 4, 2, 256], BF16)
        psc = sb("p4_psc", [128, 1024], F32)
        pm = sb("p4_pm", [128, 20, 128], F32)
        Bm = sb("p4_Bm", [128, 20, 128], BF16)
        Bx = sb("p4_Bx", [128, 4, 4, 128], BF16)
        tmpb = sb("p4_tmpb", [128, 4, 128], F32)
        flg = sb("p4_flg", [128, 4], F32)
        cst = sb("p4_cst", [128, 512], F32)
        identb = sb("p4_identb", [128, 128], BF16)
        bo = sb("p4_bo", [128, 2048], F32)
        g1 = sb("p4_g1", [128, 2048], F32)
        b1 = sb("p4_b1", [128, 2048], F32)
        bdn = sb("p4_bdn", [128, 2048], F32)
        u = sb("p4_u", [128, 4, 1024], BF16)
        hm = sb("p4_hm", [128, 3, 1024], BF16)
        R = sb("p4_R", [128, 4, 2048], F32)
        xb16 = sb("p4_xb16", [128, 2, 2048], BF16)
        mixT = sb("p4_mixT", [128, 2, 16, 128], BF16)
        pT = sb("p4_pT", [128, 8, 128], BF16)
        xTs = sb("p4_xTs", [128, 1, 2048], BF16)
        stats = sb("p4_stats", [128, 4, 6], F32)
        mv = sb("p4_mv", [128, 2], F32)
        lnv = sb("p4_lnv", [128, 1], F32)
        rstd = sb("p4_rstd", [128, 1], F32)
        nmr = sb("p4_nmr", [128, 1], F32)
        psPo = ps("p4_psPo", [128, 2, 512])
        psHp = ps("p4_psHp", [128, 2, 512])
        psX = ps("p4_psX", [128, 2, 512])
        psW = ps("p4_psW", [128, 2, 512])
        psXb = psX[:].rearrange("p a b -> p (a b)").bitcast(BF16)

        wsrc = c.w_out.rearrange("(kt p) n -> p kt n", p=128)
        for i in range(4):
            P.dma('pool', lambda e, i=i: e.dma_start(out=wout[:, 4 * i:4 * i + 4, :], in_=wsrc[:, 4 * i:4 * i + 4, :]),
                  writes=[('wout', i)])
        WOUT = [('wout', i) for i in range(4)]
        for g in range(4):
            P.dma('sp', lambda e, g=g: e.dma_start(out=wpr[:, g, :, :],
                                                   in_=c.w_pool[g].rearrange("(cb p) d -> p cb d", p=128)),
                  writes=[('wpr', g)])
        P.dma('sp', lambda e: e.dma_start(out=psc[:], in_=c.pool_scale.to_broadcast([128, 1024])), writes=['psc'])
        P.dma('sp', lambda e: e.dma_start(out=pm[:].rearrange("p a b -> p (a b)"), in_=c.pmats), writes=['pm'])
        P.dma('sp', lambda e: e.dma_start(out=flg[:], in_=c.flags), writes=['flg'])
        P.dma('sp', lambda e: e.dma_start(out=cst[:], in_=c.consts), writes=['cst'])
        P.dma('sp', lambda e: e.dma_start(out=bo[:], in_=c.b_out.to_broadcast([128, 2048])), writes=['bo'])
        P.dma('sp', lambda e: e.dma_start(out=g1[:], in_=c.ln1_g.to_broadcast([128, 2048])), writes=['g1'])
        P.dma('sp', lambda e: e.dma_start(out=b1[:], in_=c.ln1_b.to_broadcast([128, 2048])), writes=['b1'])
        P.dma('sp', lambda e: e.dma_start(out=bdn[:], in_=c.b_down.to_broadcast([128, 2048])), writes=['bdn'])
        P.dve(lambda e: e.tensor_copy(out=identb[:], in_=cst[:, 384:512]), reads=['cst'], writes=['identb'])
        for cb in range(2):
            P.dve(lambda e, cb=cb: e.tensor_tensor(out=wp[:, :, cb, :], in0=wpr[:, :, cb, :],
                                                   in1=psc[:].rearrange("p (g d) -> p g d", g=4), op=ALU.mult),
                  reads=[('wpr', g) for g in range(4)] + ['psc'], writes=['wp'])
        P.dve(lambda e: e.tensor_copy(out=Bm[:], in_=pm[:]), reads=['pm'], writes=['Bm'])
        P.pool(lambda e: e.memset(c.xh[:], 0.0), writes=['xh'])
        pm4 = pm[:].rearrange("p (g k) t -> p g k t", k=5)
        for idx, (ka, kb_) in enumerate([(1, 4), (2, None), (1, 3), (0, None)]):
            if kb_ is None:
                P.dve(lambda e, idx=idx, ka=ka: e.tensor_scalar(out=Bx[:, idx, :, :], in0=pm4[:, :, ka, :],
                                                                scalar1=flg[:, 0:1], scalar2=None, op0=ALU.mult),
                      reads=['pm', 'flg'], writes=[('Bx', idx)])
            else:
                P.dve(lambda e, ka=ka: e.tensor_scalar(out=tmpb[:], in0=pm4[:, :, ka, :], scalar1=flg[:, 0:1],
                                                       scalar2=None, op0=ALU.mult),
                      reads=['pm', 'flg'], writes=['tmpb'])
                P.dve(lambda e, idx=idx, kb_=kb_: e.scalar_tensor_tensor(
                    out=Bx[:, idx, :, :], in0=pm4[:, :, kb_, :], scalar=flg[:, 1:2], in1=tmpb[:],
                    op0=ALU.mult, op1=ALU.add),
                    reads=['pm', 'flg', 'tmpb'], writes=[('Bx', idx)])
        Bm4 = Bm[:].rearrange("p (g k) t -> p g k t", k=5)

        def load_uh(j):
            P.dma('sp', lambda e: e.dma_start(out=u[:, j % 4, :], in_=c.u_s[M[j]]), writes=[('u', j % 4)])
            P.dma('sp', lambda e: e.dma_start(out=hm[:, j % 3, :], in_=c.hm_s[M[j]]), writes=[('hm', j % 3)])

        def load_r(j):
            P.dma('sp', lambda e: e.dma_start(out=R[:, j % 4, :], in_=c.x[j * 128:(j + 1) * 128, :]),
                  writes=[('R', j % 4)])

        def load(j):
            load_uh(j)
            load_r(j)

        def xT_part(j):
            s = j % 2
            for kt in range(16):
                P.pe(lambda e, kt=kt: e.transpose(out=psXb[:, kt * 128:(kt + 1) * 128],
                                                  in_=xb16[:, s, kt * 128:(kt + 1) * 128], identity=identb[:]),
                     reads=[('xb16', s), 'identb'], banks=['X0', 'X1'])
            P.dve(lambda e: e.tensor_copy(out=xTs[:, 0, :], in_=psXb), banks=['X0', 'X1'], writes=[('xTs', 0)])
            P.dma('act', lambda e: e.dma_start(out=c.x1T_s[j], in_=xTs[:, 0, :]), reads=[('xTs', 0)])
            xv = xTs[:, 0, :].rearrange("p (k t) -> p k t", k=16)
            for ti, m0 in enumerate(lay.tiles):
                if j == m0 - 1 and lay.seg_of(j) == lay.seg_of(m0):
                    P.pool(lambda e, ti=ti: e.tensor_copy(out=c.xh[:, :, 2 * ti:2 * ti + 1], in_=xv[:, :, 127:128]),
                           reads=[('xTs', 0)], writes=['xh'])
                if j == m0 + 4:
                    P.pool(lambda e, ti=ti: e.tensor_copy(out=c.xh[:, :, 2 * ti + 1:2 * ti + 2], in_=xv[:, :, 0:1]),
                           reads=[('xTs', 0)], writes=['xh'])

        def do_chunk(j):
            s = j % 2
            rs = j % 4
            hs3 = j % 3
            P.act(lambda e: e.activation(out=R[:, rs, :], in_=R[:, rs, :], func=AF.Copy, scale=ALPHA),
                  reads=[('R', rs)], writes=[('R', rs)])
            P.pool(lambda e: e.tensor_tensor(out=R[:, rs, :], in0=R[:, rs, :], in1=bo[:], op=ALU.add),
                   reads=[('R', rs), 'bo'], writes=[('R', rs)])
            for blk in range(8):
                P.pe(lambda e, blk=blk: e.transpose(out=psXb[:, blk * 128:(blk + 1) * 128],
                                                    in_=hm[:, hs3, blk * 128:(blk + 1) * 128], identity=identb[:]),
                     reads=[('hm', hs3), 'identb'], banks=['X0'])
            P.dve(lambda e: e.tensor_copy(out=mixT[:, s, 0:8, :].rearrange("p a b -> p (a b)"), in_=psXb[:, 0:1024]),
                  banks=['X0'], writes=[('mixT', s, 0)])
            yield
            srcs = []
            is_first = j in FIRST
            is_last = j in LAST
            if j == 0:
                srcs.append((j, lambda g: Bm4[:, g, 3, :], 'Bm'))
            elif j == SECOND_S:
                srcs.append((j - 1, lambda g: Bx[:, 3, g, :], ('Bx', 3)))
                srcs.append((j, lambda g: Bx[:, 2, g, :], ('Bx', 2)))
            else:
                if not is_first:
                    srcs.append((j - 1, lambda g: Bm4[:, g, 0, :], 'Bm'))
                srcs.append((j, lambda g: Bm4[:, g, 1, :], 'Bm'))
            if not is_last:
                srcs.append((j + 1, lambda g: Bm4[:, g, 2, :], 'Bm'))
            for g in range(4):
                for cb in range(2):
                    i8 = g * 2 + cb
                    bank = i8 // 4
                    for n, (jj, bf, bkey) in enumerate(srcs):
                        P.pe(lambda e, g=g, cb=cb, i8=i8, bank=bank, jj=jj, bf=bf, n=n: e.matmul(
                            psPo[:, bank, (i8 % 4) * 128:(i8 % 4 + 1) * 128],
                            lhsT=u[:, jj % 4, g * 256 + cb * 128:g * 256 + cb * 128 + 128], rhs=bf(g),
                            start=(n == 0), stop=(n == len(srcs) - 1)),
                            reads=[('u', jj % 4), bkey], banks=[('Po', bank)])
            for bank in range(2):
                P.dve(lambda e, bank=bank: e.tensor_copy(
                    out=pT[:, bank * 4:(bank + 1) * 4, :].rearrange("p a b -> p (a b)"), in_=psPo[:, bank, :]),
                    banks=[('Po', bank)], writes=[('pT', bank)])
            yield
            for g in range(4):
                for db in range(2):
                    i8 = g * 2 + db
                    bank = i8 // 4
                    for cb in range(2):
                        P.pe(lambda e, g=g, db=db, cb=cb, i8=i8, bank=bank: e.matmul(
                            psHp[:, bank, (i8 % 4) * 128:(i8 % 4 + 1) * 128],
                            lhsT=wp[:, g, cb, db * 128:(db + 1) * 128], rhs=pT[:, g * 2 + cb, :],
                            start=(cb == 0), stop=(cb == 1)),
                            reads=['wp', ('pT', g // 2)], banks=[('Hp', bank)])
            for bank in range(2):
                P.dve(lambda e, bank=bank: e.tensor_copy(
                    out=mixT[:, s, 8 + bank * 4:8 + (bank + 1) * 4, :].rearrange("p a b -> p (a b)"),
                    in_=psHp[:, bank, :]),
                    banks=[('Hp', bank)], writes=[('mixT', s, 1 + bank)])
            yield
            MIX = [('mixT', s, i) for i in range(3)]
            for dg in range(4):
                b = dg % 2
                for kt in range(16):
                    P.pe(lambda e, dg=dg, b=b, kt=kt: e.matmul(
                        psW[:, b, :], lhsT=mixT[:, s, kt, :], rhs=wout[:, kt, dg * 512:(dg + 1) * 512],
                        start=(kt == 0), stop=(kt == 15)),
                        reads=MIX + WOUT, banks=[('W', b)])
                P.dve(lambda e, dg=dg, b=b: e.tensor_tensor(
                    out=R[:, rs, dg * 512:(dg + 1) * 512], in0=psW[:, b, :], in1=R[:, rs, dg * 512:(dg + 1) * 512],
                    op=ALU.add),
                    banks=[('W', b)], reads=[('R', rs)], writes=[('R', rs)])
                P.dve(lambda e, dg=dg: e.bn_stats(out=stats[:, dg, :], in_=R[:, rs, dg * 512:(dg + 1) * 512]),
                      reads=[('R', rs)], writes=['stats'])
                yield
            P.dve(lambda e: e.bn_aggr(out=mv[:], in_=stats[:].rearrange("p a b -> p (a b)")),
                  reads=['stats'], writes=['mv'])
            P.act(lambda e: e.activation(out=lnv[:], in_=mv[:, 1:2], func=AF.Ln, bias=LN_EPS), reads=['mv'],
                  writes=['lnv'])
            P.act(lambda e: e.activation(out=rstd[:], in_=lnv[:], func=AF.Exp, scale=-0.5), reads=['lnv'],
                  writes=['rstd'])
            P.dve(lambda e: e.scalar_tensor_tensor(out=nmr[:], in0=mv[:, 0:1], scalar=-1.0, in1=rstd[:],
                                                   op0=ALU.mult, op1=ALU.mult),
                  reads=['mv', 'rstd'], writes=['nmr'])
            P.act(lambda e: e.activation(out=R[:, rs, :], in_=R[:, rs, :], func=AF.Identity, scale=rstd[:, 0:1],
                                         bias=nmr[:, 0:1]),
                  reads=[('R', rs), 'rstd', 'nmr'], writes=[('R', rs)])
            P.dve(lambda e: e.tensor_tensor(out=R[:, rs, :], in0=R[:, rs, :], in1=g1[:], op=ALU.mult),
                  reads=[('R', rs), 'g1'], writes=[('R', rs)])
            P.pool(lambda e: e.tensor_tensor(out=R[:, rs, :], in0=R[:, rs, :], in1=b1[:], op=ALU.add),
                   reads=[('R', rs), 'b1'], writes=[('R', rs)])
            P.act(lambda e: e.copy(out=xb16[:, s, :], in_=R[:, rs, :]), reads=[('R', rs)], writes=[('xb16', s)])
            if c.debug:
                P.dma('sp', lambda e: e.dma_start(out=c.x1d_s[j * 128:(j + 1) * 128, :], in_=R[:, rs, :]),
                      reads=[('R', rs)])
            P.act(lambda e: e.activation(out=R[:, rs, :], in_=R[:, rs, :], func=AF.Copy, scale=ALPHA),
                  reads=[('R', rs)], writes=[('R', rs)])
            P.pool(lambda e: e.tensor_tensor(out=R[:, rs, :], in0=R[:, rs, :], in1=bdn[:], op=ALU.add),
                   reads=[('R', rs), 'bdn'], writes=[('R', rs)])
            P.dma('pool', lambda e: e.dma_start(out=c.x1_s[j * 128:(j + 1) * 128, :], in_=R[:, rs, :]),
                  reads=[('R', rs)])

        def adv(g):
            try:
                next(g)
            except StopIteration:
                pass

        load(0)
        if NCH > 1:
            load(1)
        gens = {}
        for it in range(NCH + 2):
            cur = prev = None
            if it + 2 < NCH:
                load(it + 2)
            if it < NCH:
                gens[it] = do_chunk(it)
                cur = gens[it]
            if 0 <= it - 1 < NCH:
                prev = gens[it - 1]
            if cur is not None:
                adv(cur)
            if prev is not None:
                adv(prev)
            if cur is not None:
                adv(cur)
            if prev is not None:
                adv(prev)
            if cur is not None:
                adv(cur)
            if prev is not None:
                adv(prev)
                adv(prev)
            if 0 <= it - 2 < NCH:
                xT_part(it - 2)
            if prev is not None:
                adv(prev)
        P.run()


NCONV = 52


def emit_weight_convert(P, c, idx):
    if idx < 44:
        g, gv = idx // 2, idx % 2
        src = c.w_up.rearrange("(kt p) n -> p kt n", p=128)[:, :, gv * DFF + g * 256:gv * DFF + (g + 1) * 256]
        dst = c.wu_t[g].rearrange("p (kt gv n) -> p kt gv n", kt=16, gv=2)[:, :, gv, :]
        P.dma('pool', lambda e: e.dma_start(out=dst, in_=src), writes=[('wu_t', g, gv)])
    elif idx < 52:
        g = idx - 44
        src = c.w_down.rearrange("(b p) n -> p b n", p=128)[:, :, g * 256:(g + 1) * 256]
        dst = c.wd_t[g].rearrange("p (b n) -> p b n", b=44)
        P.dma('pool', lambda e: e.dma_start(out=dst, in_=src), writes=[('wd_t', g)])


def phase5(nc, c):
    lay = c.lay
    NT = lay.NTILE
    HT = lay.iS
    TM = lay.tiles
    P = Prog(nc, "p5", c.sems)
    xh = c.xh
    with ExitStack() as es:
        sb, ps = _alloc(es, nc)
        wu = sb("p5_wu", [128, 2, 16, 2, 256], BF16)
        wd = sb("p5_wd", [128, 2, 44, 256], BF16)
        hT = sb("p5_hT", [128, 44, 512], BF16)
        x1t = sb("p5_x1t", [128, 16, 512], BF16)
        tg = sb("p5_tg", [128, 2, 512], F32)
        tv = sb("p5_tv", [128, 2, 512], F32)
        sg = sb("p5_sg", [128, 1, 512], F32)
        R = sb("p5_R", [128, 4, 2048], F32)
        g2 = sb("p5_g2", [128, 2048], F32)
        b2 = sb("p5_b2", [128, 2048], F32)
        Ah = sb("p5_Ah", [128, 88, 2 * NT], F32)
        wc = sb("p5_wc", [128, 3, 88], F32)
        bup = sb("p5_bup", [128, 88], F32)
        bcv = sb("p5_bcv", [128, 88], F32)
        cb = sb("p5_cb", [128, 88], F32)
        w0b = sb("p5_w0b", [128, 88], F32)
        w2b = sb("p5_w2b", [128, 88], F32)
        w0bl = sb("p5_w0bl", [128, 88], F32)
        w2bl = sb("p5_w2bl", [128, 88], F32)
        w0l = sb("p5_w0l", [128, 88], F32)
        w2l = sb("p5_w2l", [128, 88], F32)
        flg = sb("p5_flg", [128, 4], F32)
        stats = sb("p5_stats", [128, 4, 6], F32)
        mv = sb("p5_mv", [128, 2], F32)
        lnv = sb("p5_lnv", [128, 1], F32)
        rstd = sb("p5_rstd", [128, 1], F32)
        nmr = sb("p5_nmr", [128, 1], F32)
        psU = ps("p5_psU", [128, 4, 512])
        psH = ps("p5_psH", [128, 512])
        psD = ps("p5_psD", [128, 2, 512])

        P.dma('sp', lambda e: e.dma_start(out=wc[:].rearrange("p a b -> p (a b)"), in_=c.w_conv), writes=['wc'])
        P.dma('sp', lambda e: e.dma_start(out=bup[:], in_=c.b_up), writes=['bup'])
        P.dma('sp', lambda e: e.dma_start(out=bcv[:], in_=c.b_conv), writes=['bcv'])
        P.dma('sp', lambda e: e.dma_start(out=flg[:], in_=c.flags), writes=['flg'])
        P.dma('sp', lambda e: e.dma_start(out=g2[:], in_=c.ln2_g.to_broadcast([128, 2048])), writes=['g2'])
        P.dma('sp', lambda e: e.dma_start(out=b2[:], in_=c.ln2_b.to_broadcast([128, 2048])), writes=['b2'])
        P.dve(lambda e: e.tensor_tensor(out=cb[:], in0=wc[:, 0, :], in1=wc[:, 1, :], op=ALU.add), reads=['wc'],
              writes=['cb'])
        P.dve(lambda e: e.tensor_tensor(out=cb[:], in0=cb[:], in1=wc[:, 2, :], op=ALU.add), reads=['wc', 'cb'],
              writes=['cb'])
        P.dve(lambda e: e.tensor_tensor(out=cb[:], in0=cb[:], in1=bup[:], op=ALU.mult), reads=['bup', 'cb'],
              writes=['cb'])
        P.dve(lambda e: e.tensor_tensor(out=cb[:], in0=cb[:], in1=bcv[:], op=ALU.add), reads=['bcv', 'cb'],
              writes=['cb'])
        P.dve(lambda e: e.tensor_tensor(out=w0b[:], in0=wc[:, 0, :], in1=bup[:], op=ALU.mult), reads=['wc', 'bup'],
              writes=['w0b'])
        P.dve(lambda e: e.tensor_tensor(out=w2b[:], in0=wc[:, 2, :], in1=bup[:], op=ALU.mult), reads=['wc', 'bup'],
              writes=['w2b'])
        P.dve(lambda e: e.tensor_scalar(out=w0bl[:], in0=w0b[:], scalar1=flg[:, 1:2], scalar2=None, op0=ALU.mult),
              reads=['w0b', 'flg'], writes=['w0bl'])
        P.dve(lambda e: e.tensor_scalar(out=w2bl[:], in0=w2b[:], scalar1=flg[:, 1:2], scalar2=None, op0=ALU.mult),
              reads=['w2b', 'flg'], writes=['w2bl'])
        P.dve(lambda e: e.tensor_scalar(out=w0l[:], in0=wc[:, 0, :], scalar1=flg[:, 0:1], scalar2=None, op0=ALU.mult),
              reads=['wc', 'flg'], writes=['w0l'])
        P.dve(lambda e: e.tensor_scalar(out=w2l[:], in0=wc[:, 2, :], scalar1=flg[:, 0:1], scalar2=None, op0=ALU.mult),
              reads=['wc', 'flg'], writes=['w2l'])
        CONSTS = ['wc', 'cb', 'w0b', 'w2b', 'w0bl', 'w2bl', 'w0l', 'w2l']

        wu_cnt = [0]
        wd_cnt = [0]

        def load_wu(g):
            s = wu_cnt[0] % 2
            wu_cnt[0] += 1
            P.dma('sp', lambda e: e.dma_start(out=wu[:, s, :, :, :].rearrange("p a b c -> p (a b c)"), in_=c.wu_t[g]),
                  reads=[('wu_t', g)], writes=[('wu', s)])
            return s

        def load_wd(g):
            s = wd_cnt[0] % 2
            wd_cnt[0] += 1
            P.dma('sp', lambda e: e.dma_start(out=wd[:, s, :, :].rearrange("p a b -> p (a b)"), in_=c.wd_t[g]),
                  reads=[('wd_t', g)], writes=[('wd', s)])
            return s

        def conv_path(t, blk, b, gs, i, silu):
            w0 = wc[:, 0, blk:blk + 1]
            w1 = wc[:, 1, blk:blk + 1]
            w2 = wc[:, 2, blk:blk + 1]
            key = ('t', id(t), gs)
            P.act(lambda e: e.activation(out=t[:, gs, :], in_=psU[:, b, :], func=AF.Identity, scale=w1,
                                         bias=cb[:, blk:blk + 1]),
                  banks=[('U', b)], reads=CONSTS, writes=[key])
            P.dve(lambda e: e.scalar_tensor_tensor(out=t[:, gs, 1:512], in0=psU[:, b, 0:511], scalar=w0,
                                                   in1=t[:, gs, 1:512], op0=ALU.mult, op1=ALU.add),
                  banks=[('U', b)], reads=CONSTS + [key], writes=[key])
            P.dve(lambda e: e.scalar_tensor_tensor(out=t[:, gs, 0:511], in0=psU[:, b, 1:512], scalar=w2,
                                                   in1=t[:, gs, 0:511], op0=ALU.mult, op1=ALU.add),
                  banks=[('U', b)], reads=CONSTS + [key], writes=[key])
            if i == 0:
                P.dve(lambda e: e.tensor_scalar(out=t[:, gs, 0:1], in0=t[:, gs, 0:1], scalar1=w0b[:, blk:blk + 1],
                                                scalar2=None, op0=ALU.subtract),
                      reads=CONSTS + [key], writes=[key])
            else:
                wl = w0l[:, blk:blk + 1] if i == HT else w0
                P.dve(lambda e: e.scalar_tensor_tensor(out=t[:, gs, 0:1], in0=Ah[:, blk, 2 * i:2 * i + 1], scalar=wl,
                                                       in1=t[:, gs, 0:1], op0=ALU.mult, op1=ALU.add),
                      reads=CONSTS + [key, ('Ah', blk)], writes=[key])
                if i == HT:
                    P.dve(lambda e: e.tensor_scalar(out=t[:, gs, 0:1], in0=t[:, gs, 0:1],
                                                    scalar1=w0bl[:, blk:blk + 1], scalar2=None, op0=ALU.subtract),
                          reads=CONSTS + [key], writes=[key])
            P.dve(lambda e: e.scalar_tensor_tensor(out=t[:, gs, 511:512], in0=Ah[:, blk, 2 * i + 1:2 * i + 2],
                                                   scalar=w2, in1=t[:, gs, 511:512], op0=ALU.mult, op1=ALU.add),
                  reads=CONSTS + [key, ('Ah', blk)], writes=[key])
            if silu:
                P.act(lambda e: e.activation(out=sg[:, 0, :], in_=t[:, gs, :], func=AF.Silu), reads=[key],
                      writes=[('sg', 0)])
            return key

        pcnt = [0]

        def load_x1t(i):
            for cc in range(4):
                P.dma('sp', lambda e, cc=cc: e.dma_start(
                    out=x1t[:, :, cc * 128:(cc + 1) * 128], in_=c.x1T_s[TM[i] + cc].rearrange("p (k t) -> p k t", k=16)),
                    writes=[('x1t', cc)])

        def do_tile(i):
            if i == 0:
                load_x1t(0)
                tile_state['wd_next'] = load_wd(0)
            X1T = [('x1t', cc) for cc in range(4)]
            nxt = load_wu(0) if i == 0 else tile_state['wu_next']
            for g in range(22):
                if i > 0 and g == 3:
                    finish_pair(i - 1, 0)
                if i > 0 and g == 9:
                    finish_pair(i - 1, 1)
                s = nxt
                if g + 1 < 22:
                    nxt = load_wu(g + 1)
                elif i + 1 < NT:
                    nxt = load_wu(0)
                    tile_state['wu_next'] = nxt
                for pp in range(2):
                    p = g * 2 + pp
                    gs = pcnt[0] % 2
                    pcnt[0] += 1
                    for gv in range(2):
                        b = pp * 2 + gv
                        blk = p + 44 * gv
                        for kt in range(16):
                            P.pe(lambda e, b=b, kt=kt, gv=gv, pp=pp, s=s: e.matmul(
                                psU[:, b, :], lhsT=wu[:, s, kt, gv, pp * 128:(pp + 1) * 128], rhs=x1t[:, kt, :],
                                start=(kt == 0), stop=(kt == 15)),
                                reads=[('wu', s)] + X1T, banks=[('U', b)])
                        if i == 0:
                            for kt in range(16):
                                P.pe(lambda e, kt=kt, gv=gv, pp=pp, s=s: e.matmul(
                                    psH[:, 0:2 * NT], lhsT=wu[:, s, kt, gv, pp * 128:(pp + 1) * 128], rhs=xh[:, kt, :],
                                    start=(kt == 0), stop=(kt == 15)),
                                    reads=[('wu', s), 'xh'], banks=['H'])
                            P.act(lambda e, blk=blk: e.copy(out=Ah[:, blk, :], in_=psH[:, 0:2 * NT]), banks=['H'],
                                  writes=[('Ah', blk)])
                    kg = conv_path(tg, p, pp * 2 + 0, gs, i, True)
                    if c.debug and i == 0 and p == 0:
                        c.dbg(P, 'tg', tg[:, gs, :], [128, 512], F32, [kg])
                        c.dbg(P, 'sg', sg[:, 0, :], [128, 512], F32, [('sg', 0)])
                    kv = conv_path(tv, p + 44, pp * 2 + 1, gs, i, False)
                    P.pool(lambda e, p=p, gs=gs: e.tensor_tensor(out=hT[:, p, :], in0=sg[:, 0, :], in1=tv[:, gs, :],
                                                                 op=ALU.mult),
                           reads=[('sg', 0), kv], writes=[('hT', p)])
            for m in range(4):
                j = TM[i] + m
                P.dma('sp', lambda e, j=j, m=m: e.dma_start(out=R[:, m, :], in_=c.x1_s[j * 128:(j + 1) * 128, :]),
                      writes=[('R', m)])
            if i + 1 < NT:
                load_x1t(i + 1)
            HTK = [('hT', p) for p in range(44)]
            if i == 0:
                c.dbg(P, 'hT', hT[:].rearrange("p a b -> p (a b)"), [128, 44 * 512], BF16, HTK)
            for dg in range(8):
                sd = tile_state['wd_next']
                if dg + 1 < 8:
                    tile_state['wd_next'] = load_wd(dg + 1)
                elif i + 1 < NT:
                    tile_state['wd_next'] = load_wd(0)
                for m in range(4):
                    rs = m
                    b = m % 2
                    for blk in range(44):
                        P.pe(lambda e, b=b, blk=blk, m=m, sd=sd: e.matmul(
                            psD[:, b, 0:256], lhsT=hT[:, blk, m * 128:(m + 1) * 128], rhs=wd[:, sd, blk, :],
                            start=(blk == 0), stop=(blk == 43)),
                            reads=[('hT', blk), ('wd', sd)], banks=[('D', b)])
                    P.dve(lambda e, b=b, rs=rs, dg=dg: e.tensor_tensor(
                        out=R[:, rs, dg * 256:(dg + 1) * 256], in0=psD[:, b, 0:256],
                        in1=R[:, rs, dg * 256:(dg + 1) * 256], op=ALU.add),
                        banks=[('D', b)], reads=[('R', rs)], writes=[('R', rs)])
            if i == NT - 1:
                finish_pair(i, 0)
                finish_pair(i, 1)

        def finish_pair(i, mh):
            for m in (2 * mh, 2 * mh + 1):
                j = 4 * i + m
                rs = m
                for q in range(4):
                    P.dve(lambda e, q=q, rs=rs: e.bn_stats(out=stats[:, q, :], in_=R[:, rs, q * 512:(q + 1) * 512]),
                          reads=[('R', rs)], writes=['stats'])
                P.dve(lambda e: e.bn_aggr(out=mv[:], in_=stats[:].rearrange("p a b -> p (a b)")),
                      reads=['stats'], writes=['mv'])
                P.act(lambda e: e.activation(out=lnv[:], in_=mv[:, 1:2], func=AF.Ln, bias=LN_EPS), reads=['mv'],
                      writes=['lnv'])
                P.act(lambda e: e.activation(out=rstd[:], in_=lnv[:], func=AF.Exp, scale=-0.5), reads=['lnv'],
                      writes=['rstd'])
                P.dve(lambda e: e.scalar_tensor_tensor(out=nmr[:], in0=mv[:, 0:1], scalar=-1.0, in1=rstd[:],
                                                       op0=ALU.mult, op1=ALU.mult),
                      reads=['mv', 'rstd'], writes=['nmr'])
                P.act(lambda e, rs=rs: e.activation(out=R[:, rs, :], in_=R[:, rs, :], func=AF.Identity,
                                                    scale=rstd[:, 0:1], bias=nmr[:, 0:1]),
                      reads=[('R', rs), 'rstd', 'nmr'], writes=[('R', rs)])
                P.dve(lambda e, rs=rs: e.tensor_tensor(out=R[:, rs, :], in0=R[:, rs, :], in1=g2[:], op=ALU.mult),
                      reads=[('R', rs), 'g2'], writes=[('R', rs)])
                P.pool(lambda e, rs=rs: e.tensor_tensor(out=R[:, rs, :], in0=R[:, rs, :], in1=b2[:], op=ALU.add),
                       reads=[('R', rs), 'b2'], writes=[('R', rs)])
                P.dma('pool', lambda e, j=j, rs=rs: e.dma_start(out=c.y[j * 128:(j + 1) * 128, :], in_=R[:, rs, :]),
                      reads=[('R', rs)])

        tile_state = {}
        for i in range(NT):
            do_tile(i)
        P.run()


def build(lay_args=LAY_FULL, debug=False):
    nc = bass.Bass("TRN2", target_bir_lowering=False)
    lay = Lay(*lay_args)
    with ExitStack() as es:
        c = declare(nc, lay, debug, es)
        phase1(nc, c)
        phase2(nc, c)
        phase3(nc, c)
        phase4(nc, c)
        phase5(nc, c)
    return nc


GPERM = [0, 1, 2, 3, 8, 9, 10, 11, 4, 5, 6, 7, 12, 13, 14, 15]
GPERM_REV = [8, 9, 10, 11, 0, 1, 2, 3, 12, 13, 14, 15, 4, 5, 6, 7]


def shared_maps(inp):
    w_in = inp["w_in"][0]
    b_in = inp["b_in"][0]
    m = {}
    m["w_kv"] = np.ascontiguousarray(w_in[:, 1024:3072])
    m["w_qou"] = np.ascontiguousarray(np.concatenate([w_in[:, 0:1024], w_in[:, 3072:4096], w_in[:, 4112:5136]], axis=1))
    m["b_q"] = np.ascontiguousarray(b_in[0:1024].reshape(8, 128).T)
    m["b_ou"] = np.ascontiguousarray(np.concatenate([b_in[3072:4096], b_in[4112:5136]])[None, :])
    m["gain"] = np.ascontiguousarray(inp["mh_norm_g"][0][None, :])
    m["w_pool"] = np.ascontiguousarray(inp["w_pool"][0])
    m["pool_scale"] = np.ascontiguousarray(inp["pool_scale"][0][None, :])
    m["w_out"] = np.ascontiguousarray(inp["w_out"][0])
    m["b_out"] = np.ascontiguousarray(inp["b_out"][0][None, :])
    m["ln1_g"] = np.ascontiguousarray(inp["ln1_g"][0][None, :])
    m["ln1_b"] = np.ascontiguousarray(inp["ln1_b"][0][None, :])
    m["w_up"] = np.ascontiguousarray(inp["w_up"][0])
    m["b_up"] = np.ascontiguousarray(inp["b_up"][0].reshape(88, 128).T)
    m["b_conv"] = np.ascontiguousarray(inp["b_conv"][0].reshape(88, 128).T)
    m["w_down"] = np.ascontiguousarray(inp["w_down"][0])
    m["b_down"] = np.ascontiguousarray(inp["b_down"][0][None, :])
    m["ln2_g"] = np.ascontiguousarray(inp["ln2_g"][0][None, :])
    m["ln2_b"] = np.ascontiguousarray(inp["ln2_b"][0][None, :])
    var = {}
    for rev in (False, True):
        v = {}
        gp = GPERM_REV if rev else GPERM
        v["w_g"] = np.ascontiguousarray(w_in[:, 4096:4112][:, gp])
        v["b_kvg"] = np.ascontiguousarray(np.concatenate([b_in[1024:3072], b_in[4096:4112][gp]])[None, :])
        wc = inp["w_conv"][0]
        if rev:
            wc = wc[::-1]
        v["w_conv"] = np.ascontiguousarray(wc.reshape(3, 88, 128).transpose(2, 0, 1).reshape(128, 264))
        cst, pm = _consts(rev)
        v["consts"] = cst
        v["pmats"] = pm
        var[rev] = v
    return m, var


def _xT(chunks):
    n = chunks.shape[0]
    return np.ascontiguousarray(chunks.reshape(n, 128, 16, 128).transpose(0, 3, 2, 1).reshape(n, 128, 2048))


def core_map(shared, var, lay, Xp, Xs, stype, rev):
    m = dict(shared)
    m.update(var[bool(rev)])
    if rev:
        Xp = Xp[::-1]
        Xs = Xs[::-1]
    cp = Xp.reshape(-1, 128, D)
    cs = Xs.reshape(-1, 128, D)
    z = np.zeros((128, D), np.float32)
    slots = [cp[i] for i in range(lay.NPS)]
    nb = lay.S_bef + 1
    if stype == 0:
        seq = [None] * nb + list(range(0, lay.S_own + 1 + lay.S_aft))
    else:
        nreal = lay.S_aft - lay.S_pad
        seq = list(range(0, nb + lay.S_own + 1 + nreal)) + [None] * lay.S_pad
    assert len(seq) == lay.NSS
    slots += [z if i is None else cs[i] for i in seq]
    allc = np.stack(slots, axis=0)
    m["xT"] = _xT(allc)
    m["x"] = np.ascontiguousarray(allc[lay.M].reshape(lay.NM * 128, D))
    fl = np.zeros((128, 4), np.float32)
    fl[:, 0] = 1.0 if stype == 1 else 0.0
    fl[:, 1] = 1.0 - fl[:, 0]
    fl[:, 2] = 0.0 if (stype == 1 and lay.S_pad) else 1.0
    m["flags"] = fl
    return m


def place_outputs(lay, y, stype, rev, yp_out, ys_out):
    npo = lay.P_own * 128
    yp = y[:npo]
    ys = y[npo:]
    Sp = yp_out.shape[0]
    Ss = ys_out.shape[0]
    own0 = 0 if stype == 0 else (lay.S_bef + 1) * 128
    nso = lay.S_own * 128
    if not rev:
        yp_out[0:npo] = yp
        ys_out[own0:own0 + nso] = ys
    else:
        yp_out[Sp - npo:Sp] = yp[::-1]
        ys_out[Ss - own0 - nso:Ss - own0] = ys[::-1]


CORE_ASSIGN = [(0, 0, 0, 0), (0, 0, 0, 1), (1, 0, 1, 0), (1, 0, 1, 1),
               (2, 1, 0, 0), (2, 1, 0, 1), (3, 1, 1, 0), (3, 1, 1, 1)]

_NC_CACHE = {}


def kernel(**inputs):
    inp = {k: np.asarray(v) for k, v in inputs.items()}
    lay = Lay(*LAY_FULL)
    shared, var = shared_maps(inp)
    xp = inp["x_prompt"].astype(np.float32, copy=False)
    xs = inp["x_sample"].astype(np.float32, copy=False)
    maps = [core_map(shared, var, lay, xp[p], xs[s], st, rv) for (p, s, st, rv) in CORE_ASSIGN]
    if "full" not in _NC_CACHE:
        _NC_CACHE["full"] = build(LAY_FULL)
    nc = _NC_CACHE["full"]
    res = run_bass_kernel_spmd(nc, maps, core_ids=list(range(NCORES)))
    y_prompt = np.zeros(xp.shape, np.float32)
    y_sample = np.zeros(xs.shape, np.float32)
    for core, (p, s, st, rv) in enumerate(CORE_ASSIGN):
        place_outputs(lay, np.asarray(res.results[core]["y"], dtype=np.float32), st, rv, y_prompt[p], y_sample[s])
    return (y_prompt, y_sample)
```

```python
import numpy as np
from contextlib import ExitStack
import concourse.bass as bass
import concourse.mybir as mybir
from concourse.bass_utils import run_bass_kernel_spmd

F32 = mybir.dt.float32
BF16 = mybir.dt.bfloat16
AF = mybir.ActivationFunctionType
ALU = mybir.AluOpType

D = 2048
DM = 1024
H = 4
HD = 256
DFF = 5632
NCORES = 8
LN_EPS = 1e-5
ALPHA = 2.0 ** 0.25
POOL_W = (2, 4, 8, 16)

COMPUTE = ('pe', 'act', 'dve', 'pool')
import os
_STOP = int(os.environ.get('P3STOP', '0'))


class Instr:
    __slots__ = ('eng', 'fn', 'deps', 'is_dma', 'needs_inc', 'semval', 'dsem', 'dval', 'gid')


class Sems:
    def __init__(self, nc, es, ndma=12):
        self.ndma = ndma
        self.c = {e: es.enter_context(nc.semaphore("c_" + e)) for e in COMPUTE}
        self.d = {e: [es.enter_context(nc.semaphore("d_%s%d" % (e, i))) for i in range(ndma)]
                  for e in ('act', 'pool', 'sp')}
        self.cbase = {e: 0 for e in COMPUTE}
        self.dbase = {e: [0] * ndma for e in ('act', 'pool', 'sp')}


class Prog:
    def __init__(self, nc, name, sems):
        self.nc = nc
        self.name = name
        self.sems = sems
        ndma_sems = sems.ndma
        self.q = {e: [] for e in ('pe', 'act', 'dve', 'pool', 'sp')}
        self.bufs = {}
        self.gid = 0
        self.ndma_sems = ndma_sems
        self.dma_count = {e: 0 for e in ('act', 'pool', 'sp')}
        self.dma_hist = {e: [] for e in ('act', 'pool', 'sp')}

    def _add(self, eng, fn, reads, writes, is_dma, banks=()):
        writes = list(writes) + [('bank', b) for b in banks]
        ins = Instr()
        ins.eng = eng
        ins.fn = fn
        ins.is_dma = is_dma
        ins.needs_inc = False
        ins.semval = None
        ins.dsem = None
        ins.dval = None
        ins.gid = self.gid
        self.gid += 1
        deps = {}
        for b in reads:
            st = self.bufs.get(b)
            if st is not None and st[0] is not None:
                deps[id(st[0])] = (st[0], True)
        for b in writes:
            st = self.bufs.get(b)
            if st is not None:
                if st[0] is not None and id(st[0]) not in deps:
                    deps[id(st[0])] = (st[0], False)
                for r in st[1]:
                    if id(r) not in deps:
                        deps[id(r)] = (r, False)
        for b in reads:
            st = self.bufs.get(b)
            if st is None:
                st = [None, []]
                self.bufs[b] = st
            st[1].append(ins)
        for b in writes:
            self.bufs[b] = [ins, []]
        out = []
        for d, raw in deps.values():
            if d is ins:
                continue
            if (not is_dma) and (not d.is_dma) and d.eng == eng:
                if eng == 'pe':
                    continue
            out.append(d)
        if is_dma:
            k = self.dma_count[eng]
            self.dma_count[eng] += 1
            hist = self.dma_hist[eng]
            if k >= self.ndma_sems:
                out.append(hist[k - self.ndma_sems])
            hist.append(ins)
            ins.dsem = k % self.ndma_sems
            ins.dval = self.sems.dbase[eng][ins.dsem] + 16 * (k // self.ndma_sems + 1)
        for d in out:
            if not d.is_dma:
                d.needs_inc = True
        ins.deps = out
        self.q[eng].append(ins)
        return ins

    def pe(self, fn, reads=(), writes=(), banks=()):
        return self._add('pe', fn, reads, writes, False, banks)

    def act(self, fn, reads=(), writes=(), banks=()):
        return self._add('act', fn, reads, writes, False, banks)

    def dve(self, fn, reads=(), writes=(), banks=()):
        return self._add('dve', fn, reads, writes, False, banks)

    def pool(self, fn, reads=(), writes=(), banks=()):
        return self._add('pool', fn, reads, writes, False, banks)

    def dma(self, q, fn, reads=(), writes=()):
        return self._add(q, fn, reads, writes, True)

    def run(self):
        nc = self.nc
        sems = self.sems
        for e in COMPUTE:
            c = sems.cbase[e]
            for ins in self.q[e]:
                if (not ins.is_dma) and ins.needs_inc:
                    c += 1
                    ins.semval = c
            sems.cbase[e] = c
        for e in ('act', 'pool', 'sp'):
            for ins in self.q[e]:
                if ins.is_dma:
                    sems.dbase[e][ins.dsem] = ins.dval
        with ExitStack() as st:
            csem = sems.c
            dsem = sems.d
            block = st.enter_context(nc.Block())
            engmap = {'pe': (block.tensor, nc.tensor), 'act': (block.scalar, nc.scalar),
                      'dve': (block.vector, nc.vector), 'pool': (block.gpsimd, nc.gpsimd),
                      'sp': (block.sync, nc.sync)}

            def make(e):
                def body(eng):
                    seen = {}
                    for ins in self.q[e]:
                        for d in ins.deps:
                            if d.is_dma:
                                key = ('d', d.eng, d.dsem)
                                sem, val = dsem[d.eng][d.dsem], d.dval
                            else:
                                key = ('c', d.eng)
                                sem, val = csem[d.eng], d.semval
                            if seen.get(key, 0) >= val:
                                continue
                            seen[key] = val
                            eng.wait_ge(sem, val)
                        r = ins.fn(eng)
                        if ins.is_dma:
                            r.then_inc(dsem[e][ins.dsem], 16)
                        elif ins.needs_inc:
                            r.then_inc(csem[e], 1)
                    if e in dsem:
                        last = {}
                        for ins in self.q[e]:
                            if ins.is_dma:
                                last[ins.dsem] = ins.dval
                        for k, v in last.items():
                            if seen.get(('d', e, k), 0) < v:
                                eng.wait_ge(dsem[e][k], v)
                return body

            for e, (dec, _) in engmap.items():
                dec(make(e))


def _pool_mats(rev=False):
    mats = np.zeros((4, 5, 128, 128), np.float32)
    for g, w in enumerate(POOL_W):
        lo_off, hi_off = (w // 2, w - w // 2) if not rev else (w - w // 2 - 1, w // 2 + 1)
        for t in range(128):
            lo, hi = t - lo_off, t + hi_off
            for s in range(lo, hi):
                if s < 0:
                    mats[g, 0, s + 128, t] += 1.0 / w
                elif s >= 128:
                    mats[g, 2, s - 128, t] += 1.0 / w
                else:
                    mats[g, 1, s, t] += 1.0 / w
            mats[g, 1, t, t] -= 1.0
            lo_c = max(lo, 0)
            cnt = hi - lo_c if hi <= 128 else None
            for s in range(lo_c, min(hi, 128)):
                mats[g, 3, s, t] += 1.0 / (min(hi, 10 ** 9) - lo_c)
            mats[g, 3, t, t] -= 1.0
            hi_c = min(hi, 128)
            for s in range(max(lo, 0), hi_c):
                mats[g, 4, s, t] += 1.0 / (hi_c - lo)
            mats[g, 4, t, t] -= 1.0
    return mats


def _consts(rev=False):
    s = np.arange(128)[:, None]
    t = np.arange(128)[None, :]
    trif = (s <= t).astype(np.float32)
    trib = (s >= t).astype(np.float32)
    ones = np.ones((128, 128), np.float32)
    ident = np.eye(128, dtype=np.float32)
    c = np.concatenate([trif, trib, ones, ident], axis=1)
    pm = _pool_mats(rev).reshape(20, 128, 128).transpose(1, 0, 2).reshape(128, 20 * 128)
    return np.ascontiguousarray(c), np.ascontiguousarray(pm)


class Ctx:
    pass


class Lay:
    def __init__(self, P_own, P_ctx, S_bef, S_own, S_aft, S_pad):
        self.P_own, self.P_ctx, self.S_bef, self.S_own, self.S_aft, self.S_pad = P_own, P_ctx, S_bef, S_own, S_aft, S_pad
        self.NPS = P_own + 1 + P_ctx
        self.S0 = self.NPS
        self.NSS = S_bef + 1 + S_own + 1 + S_aft
        self.NX = self.NPS + self.NSS
        self.P_M = list(range(0, P_own + 1))
        self.S_M = list(range(self.S0 + S_bef, self.S0 + S_bef + S_own + 2))
        self.M = self.P_M + self.S_M
        self.NM = len(self.M)
        self.segs = [(0, len(self.P_M)), (len(self.P_M), self.NM)]
        self.M2 = list(self.M)
        extra = P_own + 1
        while len(self.M2) % 4:
            self.M2.append(extra)
            extra += 1
        self.tiles = [4 * i for i in range(P_own // 4)] + [len(self.P_M) + 1 + 4 * i for i in range(S_own // 4)]
        self.NTILE = len(self.tiles)
        self.iS = P_own // 4
        self.NF = P_own + S_own

    def seg_of(self, r):
        return 0 if r < len(self.P_M) else 1


LAY_FULL = (32, 31, 7, 8, 23, 8)


def _alloc(es, nc):
    def sb(name, shape, dt):
        return es.enter_context(nc.sbuf_tensor(name, shape, dt))

    def ps(name, shape, dt=F32):
        return es.enter_context(nc.psum_tensor(name, shape, dt))
    return sb, ps


def declare(nc, lay, debug, es):
    c = Ctx()
    c.lay = lay
    NCH = lay.NX
    c.NCH = NCH
    c.sems = Sems(nc, es)
    NT = lay.NM * 128

    def inp(name, shape, dt=F32):
        return nc.dram_tensor(name, shape, dt, kind="ExternalInput").ap()

    def scr(name, shape, dt):
        return nc.dram_tensor(name, shape, dt, kind="ExternalOutput" if debug else "Internal").ap()

    c.xT = inp("xT", [NCH, 128, 2048])
    c.x = inp("x", [NT, D])
    c.w_kv = inp("w_kv", [D, 2048])
    c.w_g = inp("w_g", [D, 16])
    c.w_qou = inp("w_qou", [D, 3072])
    c.b_kvg = inp("b_kvg", [1, 2064])
    c.b_q = inp("b_q", [128, 8])
    c.b_ou = inp("b_ou", [1, 2048])
    c.gain = inp("gain", [1, 1024])
    c.w_pool = inp("w_pool", [4, 256, 256])
    c.pool_scale = inp("pool_scale", [1, 1024])
    c.w_out = inp("w_out", [D, D])
    c.b_out = inp("b_out", [1, D])
    c.ln1_g = inp("ln1_g", [1, D])
    c.ln1_b = inp("ln1_b", [1, D])
    c.w_up = inp("w_up", [D, 2 * DFF])
    c.b_up = inp("b_up", [128, 88])
    c.w_conv = inp("w_conv", [128, 3 * 88])
    c.b_conv = inp("b_conv", [128, 88])
    c.w_down = inp("w_down", [DFF, D])
    c.b_down = inp("b_down", [1, D])
    c.ln2_g = inp("ln2_g", [1, D])
    c.ln2_b = inp("ln2_b", [1, D])
    c.flags = inp("flags", [128, 4])
    c.consts = inp("consts", [128, 512])
    c.pmats = inp("pmats", [128, 20 * 128])
    c.y = nc.dram_tensor("y", [lay.NF * 128, D], F32, kind="ExternalOutput").ap()
    c.debug = debug
    c.nc = nc

    def dbg(P, name, tile_ap, shape, dt, reads):
        if not debug:
            return
        t = nc.dram_tensor("dbg_" + name, shape, dt, kind="ExternalOutput").ap()
        P.dma('sp', lambda e: e.dma_start(out=t, in_=tile_ap), reads=reads)
    c.dbg = dbg
    c.k_s = scr("k_s", [NCH, 128, 1024], BF16)
    c.v_s = scr("v_s", [NCH, 128, 1024], BF16)
    c.sb_s = scr("sb_s", [NCH, 128, 8 * 257], BF16)
    c.g_s = scr("g_s", [128, NCH * 32], F32)
    c.qT_s = scr("qT_s", [NCH, 128, 1024], BF16)
    c.so_s = scr("so_s", [NCH, 128, 1024], BF16)
    c.u_s = scr("u_s", [NCH, 128, 1024], BF16)
    c.hm_s = scr("hm_s", [NCH, 128, 1024], BF16)
    c.x1_s = scr("x1_s", [NT, D], F32)
    if debug:
        c.x1d_s = scr("x1d_s", [NT, D], F32)
    c.x1T_s = scr("x1T_s", [lay.NM, 128, 2048], BF16)
    c.sf0_s = scr("sf0_s", [128, 8 * 257], F32)
    c.wu_t = scr("wu_t", [22, 128, 16 * 2 * 256], BF16)
    c.wd_t = scr("wd_t", [8, 128, 44 * 256], BF16)
    c.xh = es.enter_context(nc.sbuf_tensor("g_xh", [128, 16, 2 * lay.NTILE], BF16))
    return c


def phase1(nc, c):
    NCH = c.NCH
    P = Prog(nc, "p1", c.sems)
    with ExitStack() as es:
        sb, ps = _alloc(es, nc)
        wkv = sb("p1_wkv", [128, 16, 2048], BF16)
        wg = sb("p1_wg", [128, 16, 16], BF16)
        bias = sb("p1_bias", [128, 2064], F32)
        cst = sb("p1_cst", [128, 512], F32)
        flg = sb("p1_flg", [128, 4], F32)
        xt = sb("p1_xt", [128, 3, 2048], BF16)
        kb = sb("p1_kb", [128, 2, 1024], BF16)
        va = sb("p1_va", [128, 2, 4, 257], BF16)
        kd = sb("p1_kd", [128, 2, 1024], BF16)
        gs = sb("p1_gs", [128, 16], F32)
        e1 = sb("p1_e1", [128, 8], F32)
        Lt = sb("p1_L", [128, 8], F32)
        tmp8 = sb("p1_tmp8", [128, 8], F32)
        wv = sb("p1_wv", [128, 8], F32)
        G = sb("p1_G", [128, NCH, 32], F32)
        S = sb("p1_S", [128, 8, 257], F32)
        Sst = sb("p1_Sst", [128, 2, 8 * 257], BF16)
        psA = ps("p1_psA", [128, 2, 512])
        psG = ps("p1_psG", [128, 512])
        psS = ps("p1_psS", [128, 4, 512])

        wsrc = c.w_kv.rearrange("(kt p) n -> p kt n", p=128)
        for i in range(4):
            P.dma('pool', lambda e, i=i: e.dma_start(out=wkv[:, 4 * i:4 * i + 4, :], in_=wsrc[:, 4 * i:4 * i + 4, :]),
                  writes=[('wkv', i)])
        P.dma('pool', lambda e: e.dma_start(out=wg[:], in_=c.w_g.rearrange("(kt p) n -> p kt n", p=128)),
              writes=['wg'])
        P.dma('sp', lambda e: e.dma_start(out=bias[:], in_=c.b_kvg.to_broadcast([128, 2064])), writes=['bias'])
        P.dma('sp', lambda e: e.dma_start(out=cst[:], in_=c.consts), writes=['cst'])
        P.dma('sp', lambda e: e.dma_start(out=flg[:], in_=c.flags), writes=['flg'])
        P.dve(lambda e: e.memset(S[:], 0.0), writes=[('S', i) for i in range(8)])
        for s in range(2):
            P.pool(lambda e, s=s: e.memset(va[:, s, :, 256:257], 1.0), writes=[('va', s)])
        trif = cst[:, 0:128]
        trib = cst[:, 128:256]
        ones = cst[:, 256:384]
        WKV = [('wkv', i) for i in range(4)]

        lay = c.lay
        SK = [('S', i) for i in range(8)]
        Mset = set(lay.M)
        order = []
        for sl in range(lay.S0, lay.S0 + lay.S_bef):
            order.append([sl, 'f', None, None])
        if lay.S_bef:
            order[0][2] = 'zero'
            order[-1][3] = 'sf0'
        first = len(order)
        for sl in range(lay.NPS - 1, -1, -1):
            order.append([sl, 'b', None, None])
        order[first][2] = 'zero'
        first = len(order)
        for sl in range(lay.S0 + lay.NSS - 1, lay.S0 + lay.S_bef - 1, -1):
            order.append([sl, 'b', None, None])
        order[first][2] = 'zero'
        if lay.S_pad:
            order[first + lay.S_pad - 1][3] = 'fpad'
        NORD = len(order)

        def load_x(pos):
            s = pos % 3
            sl = order[pos][0]
            P.dma('pool', lambda e: e.dma_start(out=xt[:, s, :], in_=c.xT[sl]), writes=[('xt', s)])

        load_x(0)
        if NORD > 1:
            load_x(1)
        def part_a(idx, j):
            mode = order[idx][1]
            if idx + 2 < NORD:
                load_x(idx + 2)
            s3 = idx % 3
            s2 = idx % 2
            for kt in range(16):
                P.pe(lambda e, kt=kt: e.matmul(
                    psG[:, 0:16], lhsT=xt[:, s3, kt * 128:(kt + 1) * 128], rhs=wg[:, kt, :],
                    start=(kt == 0), stop=(kt == 15)),
                    reads=[('xt', s3), 'wg'], banks=['G'])
            P.dve(lambda e: e.tensor_tensor(out=gs[:], in0=psG[:, 0:16], in1=bias[:, 2048:2064], op=ALU.add),
                  banks=['G'], reads=['bias'], writes=['gs'])
            P.act(lambda e: e.activation(out=e1[:], in_=gs[:, 8:16], func=AF.Exp, scale=-1.0),
                  reads=['gs'], writes=['e1'])
            P.act(lambda e: e.activation(out=Lt[:], in_=e1[:], func=AF.Ln, bias=1.0),
                  reads=['e1'], writes=['L'])
            CGS = (0, 1)
            for cg in CGS:
                b = cg % 2
                for kt in range(16):
                    P.pe(lambda e, b=b, kt=kt, cg=cg: e.matmul(
                        psA[:, b, :], lhsT=xt[:, s3, kt * 128:(kt + 1) * 128],
                        rhs=wkv[:, kt, cg * 512:(cg + 1) * 512], start=(kt == 0), stop=(kt == 15)),
                        reads=[('xt', s3)] + WKV, banks=[('A', b)])
                if cg < 2:
                    P.dve(lambda e, b=b, cg=cg: e.tensor_tensor(
                        out=kb[:, s2, cg * 512:(cg + 1) * 512], in0=psA[:, b, :],
                        in1=bias[:, cg * 512:(cg + 1) * 512], op=ALU.add),
                        banks=[('A', b)], reads=['bias'], writes=[('kb', s2, cg)])
                else:
                    h0 = (cg - 2) * 2
                    P.dve(lambda e, b=b, cg=cg, h0=h0: e.tensor_tensor(
                        out=va[:, s2, h0:h0 + 2, 0:256],
                        in0=psA[:, b, :].rearrange("p (h d) -> p h d", h=2),
                        in1=bias[:, cg * 512:(cg + 1) * 512].rearrange("p (h d) -> p h d", h=2), op=ALU.add),
                        banks=[('A', b)], reads=['bias'], writes=[('va', s2)])
            P.pe(lambda e: e.matmul(psG[:, 32:36], lhsT=trif, rhs=Lt[:, 0:4], start=True, stop=True),
                 reads=['L', 'cst'], banks=['G'])
            P.pe(lambda e: e.matmul(psG[:, 36:40], lhsT=trib, rhs=Lt[:, 4:8], start=True, stop=True),
                 reads=['L', 'cst'], banks=['G'])
            P.pe(lambda e: e.matmul(psG[:, 40:48], lhsT=ones, rhs=Lt[:, 0:8], start=True, stop=True),
                 reads=['L', 'cst'], banks=['G'])
            P.dve(lambda e: e.tensor_tensor(out=tmp8[:], in0=psG[:, 32:40], in1=gs[:, 0:8], op=ALU.add),
                  banks=['G'], reads=['gs'], writes=['tmp8'])
            P.act(lambda e: e.activation(out=wv[:], in_=tmp8[:], func=AF.Exp), reads=['tmp8'], writes=['wv'])
            P.act(lambda e, j=j: e.activation(out=G[:, j, 8:16], in_=psG[:, 32:40], func=AF.Exp),
                  banks=['G'], writes=[('G', j, 1)])
            P.act(lambda e, j=j: e.activation(out=G[:, j, 16:24], in_=psG[:, 40:48], func=AF.Exp, scale=-1.0),
                  banks=['G'], writes=[('G', j, 2)])
            P.dve(lambda e, j=j: e.tensor_scalar(out=G[:, j, 0:8], in0=wv[:], scalar1=1.0 / 16.0, scalar2=None,
                                                 op0=ALU.mult),
                  reads=['wv'], writes=[('G', j, 0)])
            P.dve(lambda e, j=j: e.tensor_tensor(out=G[:, j, 24:32], in0=G[:, j, 0:8], in1=G[:, j, 16:24],
                                                 op=ALU.mult),
                  reads=[('G', j, 0), ('G', j, 2)], writes=[('G', j, 3)])
            CGS = (2, 3)
            for cg in CGS:
                b = cg % 2
                for kt in range(16):
                    P.pe(lambda e, b=b, kt=kt, cg=cg: e.matmul(
                        psA[:, b, :], lhsT=xt[:, s3, kt * 128:(kt + 1) * 128],
                        rhs=wkv[:, kt, cg * 512:(cg + 1) * 512], start=(kt == 0), stop=(kt == 15)),
                        reads=[('xt', s3)] + WKV, banks=[('A', b)])
                if cg < 2:
                    P.dve(lambda e, b=b, cg=cg: e.tensor_tensor(
                        out=kb[:, s2, cg * 512:(cg + 1) * 512], in0=psA[:, b, :],
                        in1=bias[:, cg * 512:(cg + 1) * 512], op=ALU.add),
                        banks=[('A', b)], reads=['bias'], writes=[('kb', s2, cg)])
                else:
                    h0 = (cg - 2) * 2
                    P.dve(lambda e, b=b, cg=cg, h0=h0: e.tensor_tensor(
                        out=va[:, s2, h0:h0 + 2, 0:256],
                        in0=psA[:, b, :].rearrange("p (h d) -> p h d", h=2),
                        in1=bias[:, cg * 512:(cg + 1) * 512].rearrange("p (h d) -> p h d", h=2), op=ALU.add),
                        banks=[('A', b)], reads=['bias'], writes=[('va', s2)])
            if j in Mset and mode == 'b':
                P.dma('sp', lambda e, j=j: e.dma_start(out=c.k_s[j], in_=kb[:, s2, :]),
                      reads=[('kb', s2, 0), ('kb', s2, 1)])
                P.dma('sp', lambda e, j=j: e.dma_start(out=c.v_s[j].rearrange("p (h d) -> p h d", h=4),
                                                       in_=va[:, s2, :, 0:256]), reads=[('va', s2)])

        def part_b1(idx, j):
            mode, pre, post = order[idx][1], order[idx][2], order[idx][3]
            kcol = 28 if mode == 'b' else 24
            dcol = 20 if mode == 'b' else 16
            s2 = idx % 2
            if pre == 'zero':
                P.dve(lambda e: e.memset(S[:], 0.0), reads=SK, writes=SK)
            if j in Mset and mode == 'b':
                P.act(lambda e: e.copy(out=Sst[:, s2, :], in_=S[:].rearrange("p a b -> p (a b)")),
                      reads=[('S', i) for i in range(8)], writes=[('Sst', s2)])
                P.dma('sp', lambda e, j=j: e.dma_start(out=c.sb_s[j], in_=Sst[:, s2, :]), reads=[('Sst', s2)])
            for h in range(4):
                P.act(lambda e, h=h, j=j: e.activation(
                    out=kd[:, s2, h * 256:(h + 1) * 256], in_=kb[:, s2, h * 256:(h + 1) * 256],
                    func=AF.Copy, scale=G[:, j, kcol + h:kcol + 1 + h]),
                    reads=[('kb', s2, h // 2), ('G', j, 3)], writes=[('kd', s2, h)])

        def part_b2(idx, j):
            mode, pre, post = order[idx][1], order[idx][2], order[idx][3]
            kcol = 28 if mode == 'b' else 24
            dcol = 20 if mode == 'b' else 16
            s2 = idx % 2
            for h in range(4):
                for db in range(2):
                    i8 = h * 2 + db
                    bk = i8 % 4
                    P.pe(lambda e, h=h, db=db, bk=bk: e.matmul(
                        psS[:, bk, 0:257], lhsT=kd[:, s2, h * 256 + db * 128:h * 256 + db * 128 + 128],
                        rhs=va[:, s2, h, :], start=True, stop=True),
                        reads=[('kd', s2, h), ('va', s2)], banks=[('S', bk)])
                    P.dve(lambda e, h=h, i8=i8, bk=bk, j=j: e.scalar_tensor_tensor(
                        out=S[:, i8, :], in0=S[:, i8, :], scalar=G[:, j, dcol + h:dcol + 1 + h], in1=psS[:, bk, 0:257],
                        op0=ALU.mult, op1=ALU.add),
                        banks=[('S', bk)], reads=[('S', i8), ('G', j, 2)], writes=[('S', i8)])
            if post == 'fpad':
                P.dve(lambda e: e.tensor_scalar(out=S[:], in0=S[:], scalar1=flg[:, 2:3], scalar2=None, op0=ALU.mult),
                      reads=SK + ['flg'], writes=SK)
            if post == 'sf0':
                P.dve(lambda e: e.tensor_scalar(out=S[:], in0=S[:], scalar1=flg[:, 0:1], scalar2=None, op0=ALU.mult),
                      reads=SK + ['flg'], writes=SK)
                P.dma('sp', lambda e: e.dma_start(out=c.sf0_s, in_=S[:].rearrange("p a b -> p (a b)")), reads=SK)
        for idx in range(NORD):
            if idx > 0:
                part_b1(idx - 1, order[idx - 1][0])
            part_a(idx, order[idx][0])
            if idx > 0:
                part_b2(idx - 1, order[idx - 1][0])
        part_b1(NORD - 1, order[NORD - 1][0])
        part_b2(NORD - 1, order[NORD - 1][0])
        P.dma('sp', lambda e: e.dma_start(out=c.g_s, in_=G[:].rearrange("p a b -> p (a b)")),
              reads=[('G', j, i) for j in range(NCH) for i in range(4)])
        P.run()


def phase2(nc, c):
    M2 = c.lay.M2
    NST = len(M2) // 4
    P = Prog(nc, "p2", c.sems)
    with ExitStack() as es:
        sb, ps = _alloc(es, nc)
        w = sb("p2_w", [128, 16, 3072], BF16)
        bq = sb("p2_bq", [128, 8], F32)
        bou = sb("p2_bou", [128, 2048], F32)
        gn = sb("p2_gn", [128, 1024], F32)
        xt = sb("p2_xt", [128, 2, 4, 2048], BF16)
        qst = sb("p2_qst", [128, 2, 8, 512], BF16)
        ot = sb("p2_ot", [128, 2, 512], F32)
        ot2 = sb("p2_ot2", [128, 2, 512], F32)
        so = sb("p2_so", [128, 2, 1024], BF16)
        ub = sb("p2_ub", [128, 2, 1024], BF16)
        psQ = ps("p2_psQ", [128, 2, 512])
        psO = ps("p2_psO", [128, 4, 512])

        wsrc = c.w_qou.rearrange("(kt p) n -> p kt n", p=128)
        for i in range(4):
            for cgp in range(3):
                P.dma('pool', lambda e, i=i, cgp=cgp: e.dma_start(
                    out=w[:, 4 * i:4 * i + 4, cgp * 1024:(cgp + 1) * 1024],
                    in_=wsrc[:, 4 * i:4 * i + 4, cgp * 1024:(cgp + 1) * 1024]), writes=[('w', i, cgp)])
        WQ = [('w', i, 0) for i in range(4)]
        WO = [('w', i, 1) for i in range(4)]
        WU = [('w', i, 2) for i in range(4)]
        P.dma('sp', lambda e: e.dma_start(out=bq[:], in_=c.b_q), writes=['bq'])
        P.dma('sp', lambda e: e.dma_start(out=bou[:], in_=c.b_ou.to_broadcast([128, 2048])), writes=['bou'])
        P.dma('sp', lambda e: e.dma_start(out=gn[:], in_=c.gain.to_broadcast([128, 1024])), writes=['gn'])

        def load_x(st):
            s = st % 2
            for cc in range(4):
                P.dma('pool', lambda e, cc=cc: e.dma_start(out=xt[:, s, cc, :], in_=c.xT[M2[st * 4 + cc]]),
                      writes=[('xt', s, cc)])

        cnt = [0]

        def do_st(st):
            if st + 1 < NST:
                load_x(st + 1)
            s = st % 2
            XT = [('xt', s, cc) for cc in range(4)]
            for blk in range(8):
                b = blk % 2
                for kt in range(16):
                    P.pe(lambda e, blk=blk, b=b, kt=kt: e.matmul(
                        psQ[:, b, :], lhsT=w[:, kt, blk * 128:(blk + 1) * 128],
                        rhs=xt[:, s, :, kt * 128:(kt + 1) * 128], start=(kt == 0), stop=(kt == 15)),
                        reads=XT + WQ, banks=[('Q', b)])
                P.act(lambda e, blk=blk, b=b: e.activation(
                    out=qst[:, s, blk, :], in_=psQ[:, b, :], func=AF.Identity, bias=bq[:, blk:blk + 1]),
                    banks=[('Q', b)], reads=['bq'], writes=[('qst', s, blk)])
            for cc in range(4):
                j = M2[st * 4 + cc]
                P.dma('sp', lambda e, cc=cc, j=j: e.dma_start(
                    out=c.qT_s[j].rearrange("p (b t) -> p b t", b=8), in_=qst[:, s, :, cc * 128:(cc + 1) * 128]),
                    reads=[('qst', s, blk) for blk in range(8)])
            for cc in range(4):
                j = M2[st * 4 + cc]
                s2 = cnt[0] % 2
                cnt[0] += 1
                for cg in range(4):
                    b = cg
                    for kt in range(16):
                        P.pe(lambda e, cc=cc, cg=cg, b=b, kt=kt: e.matmul(
                            psO[:, b, :], lhsT=xt[:, s, cc, kt * 128:(kt + 1) * 128],
                            rhs=w[:, kt, 1024 + cg * 512:1024 + (cg + 1) * 512], start=(kt == 0), stop=(kt == 15)),
                            reads=[('xt', s, cc)] + (WO if cg < 2 else WU), banks=[('O', b)])
                    if cg < 2:
                        P.dve(lambda e, cg=cg, b=b: e.tensor_tensor(
                            out=ot[:, cg, :], in0=psO[:, b, :], in1=bou[:, cg * 512:(cg + 1) * 512], op=ALU.add),
                            banks=[('O', b)], reads=['bou'], writes=[('ot', cg)])
                        P.act(lambda e, cg=cg: e.activation(out=ot2[:, cg, :], in_=ot[:, cg, :], func=AF.Sigmoid),
                              reads=[('ot', cg)], writes=[('ot2', cg)])
                        P.pool(lambda e, cg=cg, s2=s2: e.tensor_tensor(
                            out=so[:, s2, cg * 512:(cg + 1) * 512], in0=ot2[:, cg, :],
                            in1=gn[:, cg * 512:(cg + 1) * 512], op=ALU.mult),
                            reads=[('ot2', cg), 'gn'], writes=[('so', s2, cg)])
                    else:
                        P.dve(lambda e, cg=cg, b=b, s2=s2: e.tensor_tensor(
                            out=ub[:, s2, (cg - 2) * 512:(cg - 1) * 512], in0=psO[:, b, :],
                            in1=bou[:, cg * 512:(cg + 1) * 512], op=ALU.add),
                            banks=[('O', b)], reads=['bou'], writes=[('ub', s2, cg)])
                P.dma('sp', lambda e, j=j, s2=s2: e.dma_start(out=c.so_s[j], in_=so[:, s2, :]),
                      reads=[('so', s2, 0), ('so', s2, 1)])
                P.dma('sp', lambda e, j=j, s2=s2: e.dma_start(out=c.u_s[j], in_=ub[:, s2, :]),
                      reads=[('ub', s2, 2), ('ub', s2, 3)])

        load_x(0)
        for st in range(NST):
            do_st(st)
        P.run()


def phase3(nc, c):
    NCH = c.NCH
    lay = c.lay
    M = lay.M
    NM = lay.NM
    RS = lay.segs[1][0]
    P = Prog(nc, "p3", c.sems)
    with ExitStack() as es:
        sb, ps = _alloc(es, nc)
        G = sb("p3_G", [128, NCH, 32], F32)
        cst = sb("p3_cst", [128, 512], F32)
        identb = sb("p3_identb", [128, 128], BF16)
        flg = sb("p3_flg", [128, 4], F32)
        qT = sb("p3_qT", [128, 3, 1024], BF16)
        kb = sb("p3_kb", [128, 3, 1024], BF16)
        va = sb("p3_va", [128, 3, 4, 257], BF16)
        so = sb("p3_so", [128, 3, 1024], BF16)
        sbs = sb("p3_sbs", [128, 3, 8, 257], BF16)
        Sf = sb("p3_Sf", [128, 8, 257], F32)
        Sfb = sb("p3_Sfb", [128, 2, 8, 257], BF16)
        kT = sb("p3_kT", [128, 2, 1024], BF16)
        kdf = sb("p3_kdf", [128, 2, 1024], BF16)
        STf = sb("p3_STf", [128, 2, 4, 128], BF16)
        STb = sb("p3_STb", [128, 2, 4, 128], BF16)
        hf32 = sb("p3_hf32", [128, 2, 1024], F32)
        h32 = sb("p3_h32", [128, 2, 1024], F32)
        stats = sb("p3_stats", [128, 4, 6], F32)
        mv = sb("p3_mv", [128, 4, 2], F32)
        den = sb("p3_den", [128, 4, 2], F32)
        rden = sb("p3_rden", [128, 4, 2], F32)
        lnv = sb("p3_lnv", [128, 4], F32)
        rstd = sb("p3_rstd", [128, 4], F32)
        nmr = sb("p3_nmr", [128, 4], F32)
        hm = sb("p3_hm", [128, 2, 1024], BF16)
        psT = ps("p3_psT", [128, 512])
        psSc = ps("p3_psSc", [128, 4, 128])
        psP = ps("p3_psP", [128, 4, 512])
        psD = ps("p3_psD", [128, 2, 512])
        psTb = psT[:].bitcast(BF16)

        P.dma('sp', lambda e: e.dma_start(out=G[:].rearrange("p a b -> p (a b)"), in_=c.g_s), writes=['G'])
        P.dma('sp', lambda e: e.dma_start(out=cst[:], in_=c.consts), writes=['cst'])
        P.dma('sp', lambda e: e.dma_start(out=flg[:], in_=c.flags), writes=['flg'])
        P.dve(lambda e: e.tensor_copy(out=identb[:], in_=cst[:, 384:512]), reads=['cst'], writes=['identb'])
        P.dve(lambda e: e.memset(Sf[:], 0.0), writes=[('Sf', i) for i in range(8)])
        P.pool(lambda e: e.memset(Sfb[:, 0, :, :], 0.0), writes=[('Sfb', 0)])
        for l in range(3):
            P.pool(lambda e, l=l: e.memset(va[:, l, :, 256:257], 1.0), writes=[('va', l)])
        trif = cst[:, 0:128]
        trib = cst[:, 128:256]

        def load(r):
            j = M[r]
            l = r % 3
            P.dma('sp', lambda e: e.dma_start(out=qT[:, l, :], in_=c.qT_s[j]), writes=[('qT', l)])
            P.dma('sp', lambda e: e.dma_start(out=kb[:, l, :], in_=c.k_s[j]), writes=[('kb', l)])
            P.dma('sp', lambda e: e.dma_start(out=va[:, l, :, 0:256],
                                              in_=c.v_s[j].rearrange("p (h d) -> p h d", h=4)), writes=[('va', l)])
            P.dma('sp', lambda e: e.dma_start(out=so[:, l, :], in_=c.so_s[j]), writes=[('so', l)])
            P.dma('sp', lambda e: e.dma_start(out=sbs[:, l, :, :].rearrange("p a b -> p (a b)"), in_=c.sb_s[j]),
                  writes=[('sbs', l)])

        def ctx(r):
            return M[r], r % 2, (r + 1) % 2, r % 3

        def st_init(r):
            j, s, nx, l = ctx(r)
            if r == RS:
                P.dma('sp', lambda e: e.dma_start(out=Sf[:].rearrange("p a b -> p (a b)"), in_=c.sf0_s),
                      writes=[('Sf', i) for i in range(8)])
                P.act(lambda e: e.copy(out=Sfb[:, s, :, :], in_=Sf[:]), reads=[('Sf', i) for i in range(8)],
                      writes=[('Sfb', s)])

        def st_x1(r):
            j, s, nx, l = ctx(r)
            for blk in range(8):
                P.pe(lambda e, blk=blk: e.transpose(
                    out=psTb[:, blk * 128:(blk + 1) * 128], in_=kb[:, l, blk * 128:(blk + 1) * 128],
                    identity=identb[:]),
                    reads=[('kb', l), 'identb'], banks=['T'])
            P.dve(lambda e: e.tensor_copy(out=kT[:, s, :], in_=psTb), banks=['T'], writes=[('kT', s)])
            for h in range(4):
                P.pool(lambda e, h=h: e.tensor_scalar(
                    out=kdf[:, s, h * 256:(h + 1) * 256], in0=kb[:, l, h * 256:(h + 1) * 256],
                    scalar1=G[:, j, 24 + h:25 + h], scalar2=0.0, op0=ALU.mult, op1=ALU.add),
                    reads=[('kb', l), 'G'], writes=[('kdf', s, h)])

        def st_x2(r):
            j, s, nx, l = ctx(r)
            for h in range(4):
                for blk in range(2):
                    cb = (2 * h + blk) * 128
                    P.pe(lambda e, h=h, blk=blk, cb=cb: e.matmul(
                        psSc[:, h, :], lhsT=kT[:, s, cb:cb + 128], rhs=qT[:, l, cb:cb + 128],
                        start=(blk == 0), stop=(blk == 1)),
                        reads=[('kT', s), ('qT', l)], banks=['Sc'])
            for h in range(4):
                P.dve(lambda e, h=h: e.scalar_tensor_tensor(
                    out=STf[:, s, h, :], in0=psSc[:, h, :], scalar=G[:, j, h:h + 1], in1=trif,
                    op0=ALU.mult, op1=ALU.mult),
                    banks=['Sc'], reads=['G', 'cst'], writes=[('STf', s, h)])
                P.dve(lambda e, h=h: e.scalar_tensor_tensor(
                    out=STb[:, s, h, :], in0=psSc[:, h, :], scalar=G[:, j, 4 + h:5 + h], in1=trib,
                    op0=ALU.mult, op1=ALU.mult),
                    banks=['Sc'], reads=['G', 'cst'], writes=[('STb', s, h)])

        def st_u(r):
            j, s, nx, l = ctx(r)
            for h in range(4):
                for db in range(2):
                    i8 = 2 * h + db
                    bk = i8 % 2
                    P.pe(lambda e, h=h, db=db, bk=bk: e.matmul(
                        psD[:, bk, 0:257], lhsT=kdf[:, s, h * 256 + db * 128:h * 256 + db * 128 + 128],
                        rhs=va[:, l, h, :], start=True, stop=True),
                        reads=[('kdf', s, h), ('va', l)], banks=[('D', bk)])
                    P.dve(lambda e, h=h, i8=i8, bk=bk: e.scalar_tensor_tensor(
                        out=Sf[:, i8, :], in0=Sf[:, i8, :], scalar=G[:, j, 16 + h:17 + h], in1=psD[:, bk, 0:257],
                        op0=ALU.mult, op1=ALU.add),
                        banks=[('D', bk)], reads=[('Sf', i8), 'G'], writes=[('Sf', i8)])
            SFK = [('Sf', i) for i in range(8)]
            if r == RS:
                P.dve(lambda e: e.tensor_scalar(out=Sf[:], in0=Sf[:], scalar1=flg[:, 0:1], scalar2=None,
                                                op0=ALU.mult), reads=SFK + ['flg'], writes=SFK)
            P.act(lambda e: e.copy(out=Sfb[:, nx, :, :], in_=Sf[:]), reads=SFK, writes=[('Sfb', nx)])

        def st_y(r):
            j, s, nx, l = ctx(r)
            for hp in range(2):
                hs = (2 * hp, 2 * hp + 1)
                for h in hs:
                    bf_ = 2 * (h % 2)
                    bb_ = bf_ + 1
                    c0 = (2 * h) * 128
                    c1 = (2 * h + 1) * 128
                    P.pe(lambda e, h=h, bf_=bf_: e.matmul(psP[:, bf_, 0:257], lhsT=STf[:, s, h, :], rhs=va[:, l, h, :],
                                                          start=True, stop=False),
                         reads=[('STf', s, h), ('va', l)], banks=[('P', bf_)])
                    P.pe(lambda e, h=h, bf_=bf_, c0=c0: e.matmul(psP[:, bf_, 0:257], lhsT=qT[:, l, c0:c0 + 128],
                                                                 rhs=Sfb[:, s, 2 * h, :], start=False, stop=False),
                         reads=[('qT', l), ('Sfb', s)], banks=[('P', bf_)])
                    P.pe(lambda e, h=h, bf_=bf_, c1=c1: e.matmul(psP[:, bf_, 0:257], lhsT=qT[:, l, c1:c1 + 128],
                                                                 rhs=Sfb[:, s, 2 * h + 1, :], start=False, stop=True),
                         reads=[('qT', l), ('Sfb', s)], banks=[('P', bf_)])
                    P.pe(lambda e, h=h, bb_=bb_: e.matmul(psP[:, bb_, 0:257], lhsT=STb[:, s, h, :], rhs=va[:, l, h, :],
                                                          start=True, stop=False),
                         reads=[('STb', s, h), ('va', l)], banks=[('P', bb_)])
                    P.pe(lambda e, h=h, bb_=bb_, c0=c0: e.matmul(psP[:, bb_, 0:257], lhsT=qT[:, l, c0:c0 + 128],
                                                                 rhs=sbs[:, l, 2 * h, :], start=False, stop=False),
                         reads=[('qT', l), ('sbs', l)], banks=[('P', bb_)])
                    P.pe(lambda e, h=h, bb_=bb_, c1=c1: e.matmul(psP[:, bb_, 0:257], lhsT=qT[:, l, c1:c1 + 128],
                                                                 rhs=sbs[:, l, 2 * h + 1, :], start=False, stop=True),
                         reads=[('qT', l), ('sbs', l)], banks=[('P', bb_)])
                for h in hs:
                    bf_ = 2 * (h % 2)
                    bb_ = bf_ + 1
                    c0 = (2 * h) * 128
                    c1 = (2 * h + 1) * 128
                    P.act(lambda e, h=h, bf_=bf_: e.activation(out=den[:, h, 0:1], in_=psP[:, bf_, 256:257], func=AF.Abs),
                          banks=[('P', bf_)], reads=[], writes=[('den', h)])
                    P.act(lambda e, h=h, bb_=bb_: e.activation(out=den[:, h, 1:2], in_=psP[:, bb_, 256:257], func=AF.Abs),
                          banks=[('P', bb_)], reads=[], writes=[('den', h)])
                h0 = 2 * hp
                P.dve(lambda e, h0=h0: e.tensor_tensor(
                    out=den[:, h0:h0 + 2, :], in0=den[:, h0:h0 + 2, :],
                    in1=G[:, j, 8:16].rearrange("p (d h) -> p h d", d=2)[:, h0:h0 + 2, :], op=ALU.max),
                    reads=[('den', h0), ('den', h0 + 1), 'G'], writes=[('den', h0), ('den', h0 + 1)])
                P.dve(lambda e, h0=h0: e.reciprocal(out=rden[:, h0:h0 + 2, :], in_=den[:, h0:h0 + 2, :]),
                      reads=[('den', h0), ('den', h0 + 1)], writes=[('rden', h0), ('rden', h0 + 1)])
                for h in hs:
                    bf_ = 2 * (h % 2)
                    bb_ = bf_ + 1
                    c0 = (2 * h) * 128
                    c1 = (2 * h + 1) * 128
                    P.act(lambda e, h=h, bf_=bf_: e.activation(
                        out=hf32[:, s, h * 256:(h + 1) * 256], in_=psP[:, bf_, 0:256], func=AF.Copy,
                        scale=rden[:, h, 0:1]),
                        banks=[('P', bf_)], reads=[('rden', h)], writes=[('hf32', s, h)])
                for h in hs:
                    bf_ = 2 * (h % 2)
                    bb_ = bf_ + 1
                    c0 = (2 * h) * 128
                    c1 = (2 * h + 1) * 128
                    P.dve(lambda e, h=h, bb_=bb_: e.scalar_tensor_tensor(
                        out=h32[:, s, h * 256:(h + 1) * 256], in0=psP[:, bb_, 0:256], scalar=rden[:, h, 1:2],
                        in1=hf32[:, s, h * 256:(h + 1) * 256], op0=ALU.mult, op1=ALU.add),
                        banks=[('P', bb_)], reads=[('rden', h), ('hf32', s, h)], writes=[('h32', s, h)])
                for h in hs:
                    P.dve(lambda e, h=h: e.bn_stats(out=stats[:, h, :], in_=h32[:, s, h * 256:(h + 1) * 256]),
                          reads=[('h32', s, h)], writes=[('stats', h)])


        def st_z(r):
            j, s, nx, l = ctx(r)
            SFK = [('Sf', i) for i in range(8)]
            for h in range(4):
                P.dve(lambda e, h=h: e.bn_aggr(out=mv[:, h, :], in_=stats[:, h, :]),
                      reads=[('stats', h)], writes=[('mv', h)])
            MV = [('mv', h) for h in range(4)]
            P.act(lambda e: e.activation(out=lnv[:], in_=mv[:, :, 1], func=AF.Ln, bias=LN_EPS),
                  reads=MV, writes=['lnv'])
            P.act(lambda e: e.activation(out=rstd[:], in_=lnv[:], func=AF.Exp, scale=-0.5),
                  reads=['lnv'], writes=['rstd'])
            P.dve(lambda e: e.scalar_tensor_tensor(out=nmr[:], in0=mv[:, :, 0], scalar=-1.0, in1=rstd[:],
                                                   op0=ALU.mult, op1=ALU.mult),
                  reads=MV + ['rstd'], writes=['nmr'])
            for h in range(4):
                P.act(lambda e, h=h: e.activation(
                    out=hf32[:, s, h * 256:(h + 1) * 256], in_=h32[:, s, h * 256:(h + 1) * 256], func=AF.Identity,
                    scale=rstd[:, h:h + 1], bias=nmr[:, h:h + 1]),
                    reads=[('h32', s, h), 'rstd', 'nmr'], writes=[('hf32', s, h)])
            P.pool(lambda e: e.tensor_tensor(out=hm[:, s, :], in0=hf32[:, s, :], in1=so[:, l, :], op=ALU.mult),
                   reads=[('hf32', s, h) for h in range(4)] + [('so', l)], writes=[('hm', s)])
            P.dma('pool', lambda e: e.dma_start(out=c.hm_s[j], in_=hm[:, s, :]), reads=[('hm', s)])
            if r == 0:
                H4 = list(range(4))
                c.dbg(P, 'kT', kT[:, s, :], [128, 1024], BF16, [('kT', s)])
                c.dbg(P, 'identb', identb[:], [128, 128], BF16, ['identb'])
                c.dbg(P, 'kb', kb[:, l, :], [128, 1024], BF16, [('kb', l)])
                c.dbg(P, 'cst', cst[:], [128, 512], F32, ['cst'])
                c.dbg(P, 'STf', STf[:, s, :, :].rearrange("p a b -> p (a b)"), [128, 512], BF16, [('STf', s, h) for h in H4])
                c.dbg(P, 'STb', STb[:, s, :, :].rearrange("p a b -> p (a b)"), [128, 512], BF16, [('STb', s, h) for h in H4])
                c.dbg(P, 'h32', h32[:, s, :], [128, 1024], F32, [('h32', s, h) for h in H4])
                c.dbg(P, 'hn', hf32[:, s, :], [128, 1024], F32, [('hf32', s, h) for h in H4])
                c.dbg(P, 'den', den[:].rearrange("p a b -> p (a b)"), [128, 8], F32, [('den', h) for h in H4])
                c.dbg(P, 'rden', rden[:].rearrange("p a b -> p (a b)"), [128, 8], F32, [('rden', h) for h in H4])
                c.dbg(P, 'mv', mv[:].rearrange("p a b -> p (a b)"), [128, 8], F32, [('mv', h) for h in H4])
                c.dbg(P, 'rstd', rstd[:], [128, 4], F32, ['rstd'])
                c.dbg(P, 'nmr', nmr[:], [128, 4], F32, ['nmr'])
                c.dbg(P, 'kdf', kdf[:, s, :], [128, 1024], BF16, [('kdf', s, h) for h in H4])
                c.dbg(P, 'Sf', Sf[:].rearrange("p a b -> p (a b)"), [128, 8 * 257], F32, SFK)

        load(0)
        if NM > 1:
            load(1)
        st_x1(0)
        st_x2(0)
        nconv = 0
        for r in range(NM):
            if r + 2 < NM:
                load(r + 2)
            if r + 1 < NM:
                st_x1(r + 1)
            st_init(r)
            st_u(r)
            if r + 1 < NM:
                st_x2(r + 1)
            st_y(r)
            st_z(r)
            want = min(NCONV, ((r + 1) * NCONV + NM - 1) // NM)
            while nconv < want:
                emit_weight_convert(P, c, nconv)
                nconv += 1
        P.run()


def phase4(nc, c):
    lay = c.lay
    M = lay.M
    NCH = lay.NM
    FIRST = [a for a, b in lay.segs]
    LAST = [b - 1 for a, b in lay.segs]
    SECOND_S = lay.segs[1][0] + 1
    P = Prog(nc, "p4", c.sems)
    with ExitStack() as es:
        sb, ps = _alloc(es, nc)
        wout = sb("p4_wout", [128, 16, 2048], BF16)
        wpr = sb("p4_wpr", [128, 4, 2, 256], F32)
        wp = sb("p4_wp", [128, 4, 2, 256], BF16)
        psc = sb("p4_psc", [128, 1024], F32)
        pm = sb("p4_pm", [128, 20, 128], F32)
        Bm = sb("p4_Bm", [128, 20, 128], BF16)
        Bx = sb("p4_Bx", [128, 4, 4, 128], BF16)
        tmpb = sb("p4_tmpb", [128, 4, 128], F32)
        flg = sb("p4_flg", [128, 4], F32)
        cst = sb("p4_cst", [128, 512], F32)
        identb = sb("p4_identb", [128, 128], BF16)
        bo = sb("p4_bo", [128, 2048], F32)
        g1 = sb("p4_g1", [128, 2048], F32)
        b1 = sb("p4_b1", [128, 2048], F32)
        bdn = sb("p4_bdn", [128, 2048], F32)
        u = sb("p4_u", [128, 4, 1024], BF16)
        hm = sb("p4_hm", [128, 3, 1024], BF16)
        R = sb("p4_R", [128, 4, 2048], F32)
        xb16 = sb("p4_xb16", [128, 2, 2048], BF16)
        mixT = sb("p4_mixT", [128, 2, 16, 128], BF16)
        pT = sb("p4_pT", [128, 8, 128], BF16)
        xTs = sb("p4_xTs", [128, 1, 2048], BF16)
        stats = sb("p4_stats", [128, 4, 6], F32)
        mv = sb("p4_mv", [128, 2], F32)
        lnv = sb("p4_lnv", [128, 1], F32)
        rstd = sb("p4_rstd", [128, 1], F32)
        nmr = sb("p4_nmr", [128, 1], F32)
        psPo = ps("p4_psPo", [128, 2, 512])
        psHp = ps("p4_psHp", [128, 2, 512])
        psX = ps("p4_psX", [128, 2, 512])
        psW = ps("p4_psW", [128, 2, 512])
        psXb = psX[:].rearrange("p a b -> p (a b)").bitcast(BF16)

        wsrc = c.w_out.rearrange("(kt p) n -> p kt n", p=128)
        for i in range(4):
            P.dma('pool', lambda e, i=i: e.dma_start(out=wout[:, 4 * i:4 * i + 4, :], in_=wsrc[:, 4 * i:4 * i + 4, :]),
                  writes=[('wout', i)])
        WOUT = [('wout', i) for i in range(4)]
        for g in range(4):
            P.dma('sp', lambda e, g=g: e.dma_start(out=wpr[:, g, :, :],
                                                   in_=c.w_pool[g].rearrange("(cb p) d -> p cb d", p=128)),
                  writes=[('wpr', g)])
        P.dma('sp', lambda e: e.dma_start(out=psc[:], in_=c.pool_scale.to_broadcast([128, 1024])), writes=['psc'])
        P.dma('sp', lambda e: e.dma_start(out=pm[:].rearrange("p a b -> p (a b)"), in_=c.pmats), writes=['pm'])
        P.dma('sp', lambda e: e.dma_start(out=flg[:], in_=c.flags), writes=['flg'])
        P.dma('sp', lambda e: e.dma_start(out=cst[:], in_=c.consts), writes=['cst'])
        P.dma('sp', lambda e: e.dma_start(out=bo[:], in_=c.b_out.to_broadcast([128, 2048])), writes=['bo'])
        P.dma('sp', lambda e: e.dma_start(out=g1[:], in_=c.ln1_g.to_broadcast([128, 2048])), writes=['g1'])
        P.dma('sp', lambda e: e.dma_start(out=b1[:], in_=c.ln1_b.to_broadcast([128, 2048])), writes=['b1'])
        P.dma('sp', lambda e: e.dma_start(out=bdn[:], in_=c.b_down.to_broadcast([128, 2048])), writes=['bdn'])
        P.dve(lambda e: e.tensor_copy(out=identb[:], in_=cst[:, 384:512]), reads=['cst'], writes=['identb'])
        for cb in range(2):
            P.dve(lambda e, cb=cb: e.tensor_tensor(out=wp[:, :, cb, :], in0=wpr[:, :, cb, :],
                                                   in1=psc[:].rearrange("p (g d) -> p g d", g=4), op=ALU.mult),
                  reads=[('wpr', g) for g in range(4)] + ['psc'], writes=['wp'])
        P.dve(lambda e: e.tensor_copy(out=Bm[:], in_=pm[:]), reads=['pm'], writes=['Bm'])
        P.pool(lambda e: e.memset(c.xh[:], 0.0), writes=['xh'])
        pm4 = pm[:].rearrange("p (g k) t -> p g k t", k=5)
        for idx, (ka, kb_) in enumerate([(1, 4), (2, None), (1, 3), (0, None)]):
            if kb_ is None:
                P.dve(lambda e, idx=idx, ka=ka: e.tensor_scalar(out=Bx[:, idx, :, :], in0=pm4[:, :, ka, :],
                                                                scalar1=flg[:, 0:1], scalar2=None, op0=ALU.mult),
                      reads=['pm', 'flg'], writes=[('Bx', idx)])
            else:
                P.dve(lambda e, ka=ka: e.tensor_scalar(out=tmpb[:], in0=pm4[:, :, ka, :], scalar1=flg[:, 0:1],
                                                       scalar2=None, op0=ALU.mult),
                      reads=['pm', 'flg'], writes=['tmpb'])
                P.dve(lambda e, idx=idx, kb_=kb_: e.scalar_tensor_tensor(
                    out=Bx[:, idx, :, :], in0=pm4[:, :, kb_, :], scalar=flg[:, 1:2], in1=tmpb[:],
                    op0=ALU.mult, op1=ALU.add),
                    reads=['pm', 'flg', 'tmpb'], writes=[('Bx', idx)])
        Bm4 = Bm[:].rearrange("p (g k) t -> p g k t", k=5)

        def load_uh(j):
            P.dma('sp', lambda e: e.dma_start(out=u[:, j % 4, :], in_=c.u_s[M[j]]), writes=[('u', j % 4)])
            P.dma('sp', lambda e: e.dma_start(out=hm[:, j % 3, :], in_=c.hm_s[M[j]]), writes=[('hm', j % 3)])

        def load_r(j):
            P.dma('sp', lambda e: e.dma_start(out=R[:, j % 4, :], in_=c.x[j * 128:(j + 1) * 128, :]),
                  writes=[('R', j % 4)])

        def load(j):
            load_uh(j)
            load_r(j)

        def xT_part(j):
            s = j % 2
            for kt in range(16):
                P.pe(lambda e, kt=kt: e.transpose(out=psXb[:, kt * 128:(kt + 1) * 128],
                                                  in_=xb16[:, s, kt * 128:(kt + 1) * 128], identity=identb[:]),
                     reads=[('xb16', s), 'identb'], banks=['X0', 'X1'])
            P.dve(lambda e: e.tensor_copy(out=xTs[:, 0, :], in_=psXb), banks=['X0', 'X1'], writes=[('xTs', 0)])
            P.dma('act', lambda e: e.dma_start(out=c.x1T_s[j], in_=xTs[:, 0, :]), reads=[('xTs', 0)])
            xv = xTs[:, 0, :].rearrange("p (k t) -> p k t", k=16)
            for ti, m0 in enumerate(lay.tiles):
                if j == m0 - 1 and lay.seg_of(j) == lay.seg_of(m0):
                    P.pool(lambda e, ti=ti: e.tensor_copy(out=c.xh[:, :, 2 * ti:2 * ti + 1], in_=xv[:, :, 127:128]),
                           reads=[('xTs', 0)], writes=['xh'])
                if j == m0 + 4:
                    P.pool(lambda e, ti=ti: e.tensor_copy(out=c.xh[:, :, 2 * ti + 1:2 * ti + 2], in_=xv[:, :, 0:1]),
                           reads=[('xTs', 0)], writes=['xh'])

        def do_chunk(j):
            s = j % 2
            rs = j % 4
            hs3 = j % 3
            P.act(lambda e: e.activation(out=R[:, rs, :], in_=R[:, rs, :], func=AF.Copy, scale=ALPHA),
                  reads=[('R', rs)], writes=[('R', rs)])
            P.pool(lambda e: e.tensor_tensor(out=R[:, rs, :], in0=R[:, rs, :], in1=bo[:], op=ALU.add),
                   reads=[('R', rs), 'bo'], writes=[('R', rs)])
            for blk in range(8):
                P.pe(lambda e, blk=blk: e.transpose(out=psXb[:, blk * 128:(blk + 1) * 128],
                                                    in_=hm[:, hs3, blk * 128:(blk + 1) * 128], identity=identb[:]),
                     reads=[('hm', hs3), 'identb'], banks=['X0'])
            P.dve(lambda e: e.tensor_copy(out=mixT[:, s, 0:8, :].rearrange("p a b -> p (a b)"), in_=psXb[:, 0:1024]),
                  banks=['X0'], writes=[('mixT', s, 0)])
            yield
            srcs = []
            is_first = j in FIRST
            is_last = j in LAST
            if j == 0:
                srcs.append((j, lambda g: Bm4[:, g, 3, :], 'Bm'))
            elif j == SECOND_S:
                srcs.append((j - 1, lambda g: Bx[:, 3, g, :], ('Bx', 3)))
                srcs.append((j, lambda g: Bx[:, 2, g, :], ('Bx', 2)))
            else:
                if not is_first:
                    srcs.append((j - 1, lambda g: Bm4[:, g, 0, :], 'Bm'))
                srcs.append((j, lambda g: Bm4[:, g, 1, :], 'Bm'))
            if not is_last:
                srcs.append((j + 1, lambda g: Bm4[:, g, 2, :], 'Bm'))
            for g in range(4):
                for cb in range(2):
                    i8 = g * 2 + cb
                    bank = i8 // 4
                    for n, (jj, bf, bkey) in enumerate(srcs):
                        P.pe(lambda e, g=g, cb=cb, i8=i8, bank=bank, jj=jj, bf=bf, n=n: e.matmul(
                            psPo[:, bank, (i8 % 4) * 128:(i8 % 4 + 1) * 128],
                            lhsT=u[:, jj % 4, g * 256 + cb * 128:g * 256 + cb * 128 + 128], rhs=bf(g),
                            start=(n == 0), stop=(n == len(srcs) - 1)),
                            reads=[('u', jj % 4), bkey], banks=[('Po', bank)])
            for bank in range(2):
                P.dve(lambda e, bank=bank: e.tensor_copy(
                    out=pT[:, bank * 4:(bank + 1) * 4, :].rearrange("p a b -> p (a b)"), in_=psPo[:, bank, :]),
                    banks=[('Po', bank)], writes=[('pT', bank)])
            yield
            for g in range(4):
                for db in range(2):
                    i8 = g * 2 + db
                    bank = i8 // 4
                    for cb in range(2):
                        P.pe(lambda e, g=g, db=db, cb=cb, i8=i8, bank=bank: e.matmul(
                            psHp[:, bank, (i8 % 4) * 128:(i8 % 4 + 1) * 128],
                            lhsT=wp[:, g, cb, db * 128:(db + 1) * 128], rhs=pT[:, g * 2 + cb, :],
                            start=(cb == 0), stop=(cb == 1)),
                            reads=['wp', ('pT', g // 2)], banks=[('Hp', bank)])
            for bank in range(2):
                P.dve(lambda e, bank=bank: e.tensor_copy(
                    out=mixT[:, s, 8 + bank * 4:8 + (bank + 1) * 4, :].rearrange("p a b -> p (a b)"),
                    in_=psHp[:, bank, :]),
                    banks=[('Hp', bank)], writes=[('mixT', s, 1 + bank)])
            yield
            MIX = [('mixT', s, i) for i in range(3)]
            for dg in range(4):
                b = dg % 2
                for kt in range(16):
                    P.pe(lambda e, dg=dg, b=b, kt=kt: e.matmul(
                        psW[:, b, :], lhsT=mixT[:, s, kt, :], rhs=wout[:, kt, dg * 512:(dg + 1) * 512],
                        start=(kt == 0), stop=(kt == 15)),
                        reads=MIX + WOUT, banks=[('W', b)])
                P.dve(lambda e, dg=dg, b=b: e.tensor_tensor(
                    out=R[:, rs, dg * 512:(dg + 1) * 512], in0=psW[:, b, :], in1=R[:, rs, dg * 512:(dg + 1) * 512],
                    op=ALU.add),
                    banks=[('W', b)], reads=[('R', rs)], writes=[('R', rs)])
                P.dve(lambda e, dg=dg: e.bn_stats(out=stats[:, dg, :], in_=R[:, rs, dg * 512:(dg + 1) * 512]),
                      reads=[('R', rs)], writes=['stats'])
                yield
            P.dve(lambda e: e.bn_aggr(out=mv[:], in_=stats[:].rearrange("p a b -> p (a b)")),
                  reads=['stats'], writes=['mv'])
            P.act(lambda e: e.activation(out=lnv[:], in_=mv[:, 1:2], func=AF.Ln, bias=LN_EPS), reads=['mv'],
                  writes=['lnv'])
            P.act(lambda e: e.activation(out=rstd[:], in_=lnv[:], func=AF.Exp, scale=-0.5), reads=['lnv'],
                  writes=['rstd'])
            P.dve(lambda e: e.scalar_tensor_tensor(out=nmr[:], in0=mv[:, 0:1], scalar=-1.0, in1=rstd[:],
                                                   op0=ALU.mult, op1=ALU.mult),
                  reads=['mv', 'rstd'], writes=['nmr'])
            P.act(lambda e: e.activation(out=R[:, rs, :], in_=R[:, rs, :], func=AF.Identity, scale=rstd[:, 0:1],
                                         bias=nmr[:, 0:1]),
                  reads=[('R', rs), 'rstd', 'nmr'], writes=[('R', rs)])
            P.dve(lambda e: e.tensor_tensor(out=R[:, rs, :], in0=R[:, rs, :], in1=g1[:], op=ALU.mult),
                  reads=[('R', rs), 'g1'], writes=[('R', rs)])
            P.pool(lambda e: e.tensor_tensor(out=R[:, rs, :], in0=R[:, rs, :], in1=b1[:], op=ALU.add),
                   reads=[('R', rs), 'b1'], writes=[('R', rs)])
            P.act(lambda e: e.copy(out=xb16[:, s, :], in_=R[:, rs, :]), reads=[('R', rs)], writes=[('xb16', s)])
            if c.debug:
                P.dma('sp', lambda e: e.dma_start(out=c.x1d_s[j * 128:(j + 1) * 128, :], in_=R[:, rs, :]),
                      reads=[('R', rs)])
            P.act(lambda e: e.activation(out=R[:, rs, :], in_=R[:, rs, :], func=AF.Copy, scale=ALPHA),
                  reads=[('R', rs)], writes=[('R', rs)])
            P.pool(lambda e: e.tensor_tensor(out=R[:, rs, :], in0=R[:, rs, :], in1=bdn[:], op=ALU.add),
                   reads=[('R', rs), 'bdn'], writes=[('R', rs)])
            P.dma('pool', lambda e: e.dma_start(out=c.x1_s[j * 128:(j + 1) * 128, :], in_=R[:, rs, :]),
                  reads=[('R', rs)])

        def adv(g):
            try:
                next(g)
            except StopIteration:
                pass

        load(0)
        if NCH > 1:
            load(1)
        gens = {}
        for it in range(NCH + 2):
            cur = prev = None
            if it + 2 < NCH:
                load(it + 2)
            if it < NCH:
                gens[it] = do_chunk(it)
                cur = gens[it]
            if 0 <= it - 1 < NCH:
                prev = gens[it - 1]
            if cur is not None:
                adv(cur)
            if prev is not None:
                adv(prev)
            if cur is not None:
                adv(cur)
            if prev is not None:
                adv(prev)
            if cur is not None:
                adv(cur)
            if prev is not None:
                adv(prev)
                adv(prev)
            if 0 <= it - 2 < NCH:
                xT_part(it - 2)
            if prev is not None:
                adv(prev)
        P.run()


NCONV = 52


def emit_weight_convert(P, c, idx):
    if idx < 44:
        g, gv = idx // 2, idx % 2
        src = c.w_up.rearrange("(kt p) n -> p kt n", p=128)[:, :, gv * DFF + g * 256:gv * DFF + (g + 1) * 256]
        dst = c.wu_t[g].rearrange("p (kt gv n) -> p kt gv n", kt=16, gv=2)[:, :, gv, :]
        P.dma('pool', lambda e: e.dma_start(out=dst, in_=src), writes=[('wu_t', g, gv)])
    elif idx < 52:
        g = idx - 44
        src = c.w_down.rearrange("(b p) n -> p b n", p=128)[:, :, g * 256:(g + 1) * 256]
        dst = c.wd_t[g].rearrange("p (b n) -> p b n", b=44)
        P.dma('pool', lambda e: e.dma_start(out=dst, in_=src), writes=[('wd_t', g)])


def phase5(nc, c):
    lay = c.lay
    NT = lay.NTILE
    HT = lay.iS
    TM = lay.tiles
    P = Prog(nc, "p5", c.sems)
    xh = c.xh
    with ExitStack() as es:
        sb, ps = _alloc(es, nc)
        wu = sb("p5_wu", [128, 2, 16, 2, 256], BF16)
        wd = sb("p5_wd", [128, 2, 44, 256], BF16)
        hT = sb("p5_hT", [128, 44, 512], BF16)
        x1t = sb("p5_x1t", [128, 16, 512], BF16)
        tg = sb("p5_tg", [128, 2, 512], F32)
        tv = sb("p5_tv", [128, 2, 512], F32)
        sg = sb("p5_sg", [128, 1, 512], F32)
        R = sb("p5_R", [128, 4, 2048], F32)
        g2 = sb("p5_g2", [128, 2048], F32)
        b2 = sb("p5_b2", [128, 2048], F32)
        Ah = sb("p5_Ah", [128, 88, 2 * NT], F32)
        wc = sb("p5_wc", [128, 3, 88], F32)
        bup = sb("p5_bup", [128, 88], F32)
        bcv = sb("p5_bcv", [128, 88], F32)
        cb = sb("p5_cb", [128, 88], F32)
        w0b = sb("p5_w0b", [128, 88], F32)
        w2b = sb("p5_w2b", [128, 88], F32)
        w0bl = sb("p5_w0bl", [128, 88], F32)
        w2bl = sb("p5_w2bl", [128, 88], F32)
        w0l = sb("p5_w0l", [128, 88], F32)
        w2l = sb("p5_w2l", [128, 88], F32)
        flg = sb("p5_flg", [128, 4], F32)
        stats = sb("p5_stats", [128, 4, 6], F32)
        mv = sb("p5_mv", [128, 2], F32)
        lnv = sb("p5_lnv", [128, 1], F32)
        rstd = sb("p5_rstd", [128, 1], F32)
        nmr = sb("p5_nmr", [128, 1], F32)
        negh = sb("p5_negh", [128, 1], F32)
        psU = ps("p5_psU", [128, 4, 512])
        psH = ps("p5_psH", [128, 512])
        psD = ps("p5_psD", [128, 2, 512])

        P.dma('sp', lambda e: e.dma_start(out=wc[:].rearrange("p a b -> p (a b)"), in_=c.w_conv), writes=['wc'])
        P.dma('sp', lambda e: e.dma_start(out=bup[:], in_=c.b_up), writes=['bup'])
        P.dma('sp', lambda e: e.dma_start(out=bcv[:], in_=c.b_conv), writes=['bcv'])
        P.dma('sp', lambda e: e.dma_start(out=flg[:], in_=c.flags), writes=['flg'])
        P.dma('sp', lambda e: e.dma_start(out=g2[:], in_=c.ln2_g.to_broadcast([128, 2048])), writes=['g2'])
        P.dma('sp', lambda e: e.dma_start(out=b2[:], in_=c.ln2_b.to_broadcast([128, 2048])), writes=['b2'])
        P.dve(lambda e: e.tensor_tensor(out=cb[:], in0=wc[:, 0, :], in1=wc[:, 1, :], op=ALU.add), reads=['wc'],
              writes=['cb'])
        P.dve(lambda e: e.tensor_tensor(out=cb[:], in0=cb[:], in1=wc[:, 2, :], op=ALU.add), reads=['wc', 'cb'],
              writes=['cb'])
        P.dve(lambda e: e.tensor_tensor(out=cb[:], in0=cb[:], in1=bup[:], op=ALU.mult), reads=['bup', 'cb'],
              writes=['cb'])
        P.dve(lambda e: e.tensor_tensor(out=cb[:], in0=cb[:], in1=bcv[:], op=ALU.add), reads=['bcv', 'cb'],
              writes=['cb'])
        P.dve(lambda e: e.tensor_tensor(out=w0b[:], in0=wc[:, 0, :], in1=bup[:], op=ALU.mult), reads=['wc', 'bup'],
              writes=['w0b'])
        P.dve(lambda e: e.tensor_tensor(out=w2b[:], in0=wc[:, 2, :], in1=bup[:], op=ALU.mult), reads=['wc', 'bup'],
              writes=['w2b'])
        P.dve(lambda e: e.tensor_scalar(out=w0bl[:], in0=w0b[:], scalar1=flg[:, 1:2], scalar2=None, op0=ALU.mult),
              reads=['w0b', 'flg'], writes=['w0bl'])
        P.dve(lambda e: e.tensor_scalar(out=w2bl[:], in0=w2b[:], scalar1=flg[:, 1:2], scalar2=None, op0=ALU.mult),
              reads=['w2b', 'flg'], writes=['w2bl'])
        P.dve(lambda e: e.tensor_scalar(out=w0l[:], in0=wc[:, 0, :], scalar1=flg[:, 0:1], scalar2=None, op0=ALU.mult),
              reads=['wc', 'flg'], writes=['w0l'])
        P.dve(lambda e: e.tensor_scalar(out=w2l[:], in0=wc[:, 2, :], scalar1=flg[:, 0:1], scalar2=None, op0=ALU.mult),
              reads=['wc', 'flg'], writes=['w2l'])
        CONSTS = ['wc', 'cb', 'w0b', 'w2b', 'w0bl', 'w2bl', 'w0l', 'w2l']
        P.pool(lambda e: e.memset(negh[:], -0.5), writes=['negh'])

        wu_cnt = [0]
        wd_cnt = [0]

        def load_wu(g):
            s = wu_cnt[0] % 2
            wu_cnt[0] += 1
            P.dma('sp', lambda e: e.dma_start(out=wu[:, s, :, :, :].rearrange("p a b c -> p (a b c)"), in_=c.wu_t[g]),
                  reads=[('wu_t', g)], writes=[('wu', s)])
            return s

        def load_wd(g):
            s = wd_cnt[0] % 2
            wd_cnt[0] += 1
            P.dma('sp', lambda e: e.dma_start(out=wd[:, s, :, :].rearrange("p a b -> p (a b)"), in_=c.wd_t[g]),
                  reads=[('wd_t', g)], writes=[('wd', s)])
            return s

        def conv_path(t, blk, b, gs, i, silu):
            w0 = wc[:, 0, blk:blk + 1]
            w1 = wc[:, 1, blk:blk + 1]
            w2 = wc[:, 2, blk:blk + 1]
            key = ('t', id(t), gs)
            P.act(lambda e: e.activation(out=t[:, gs, :], in_=psU[:, b, :], func=AF.Identity, scale=w1,
                                         bias=cb[:, blk:blk + 1]),
                  banks=[('U', b)], reads=CONSTS, writes=[key])
            P.dve(lambda e: e.scalar_tensor_tensor(out=t[:, gs, 1:512], in0=psU[:, b, 0:511], scalar=w0,
                                                   in1=t[:, gs, 1:512], op0=ALU.mult, op1=ALU.add),
                  banks=[('U', b)], reads=CONSTS + [key], writes=[key])
            P.dve(lambda e: e.scalar_tensor_tensor(out=t[:, gs, 0:511], in0=psU[:, b, 1:512], scalar=w2,
                                                   in1=t[:, gs, 0:511], op0=ALU.mult, op1=ALU.add),
                  banks=[('U', b)], reads=CONSTS + [key], writes=[key])
            if i == 0:
                P.dve(lambda e: e.tensor_scalar(out=t[:, gs, 0:1], in0=t[:, gs, 0:1], scalar1=w0b[:, blk:blk + 1],
                                                scalar2=None, op0=ALU.subtract),
                      reads=CONSTS + [key], writes=[key])
            else:
                wl = w0l[:, blk:blk + 1] if i == HT else w0
                P.dve(lambda e: e.scalar_tensor_tensor(out=t[:, gs, 0:1], in0=Ah[:, blk, 2 * i:2 * i + 1], scalar=wl,
                                                       in1=t[:, gs, 0:1], op0=ALU.mult, op1=ALU.add),
                      reads=CONSTS + [key, ('Ah', blk)], writes=[key])
                if i == HT:
                    P.dve(lambda e: e.tensor_scalar(out=t[:, gs, 0:1], in0=t[:, gs, 0:1],
                                                    scalar1=w0bl[:, blk:blk + 1], scalar2=None, op0=ALU.subtract),
                          reads=CONSTS + [key], writes=[key])
            P.dve(lambda e: e.scalar_tensor_tensor(out=t[:, gs, 511:512], in0=Ah[:, blk, 2 * i + 1:2 * i + 2],
                                                   scalar=w2, in1=t[:, gs, 511:512], op0=ALU.mult, op1=ALU.add),
                  reads=CONSTS + [key, ('Ah', blk)], writes=[key])
            if silu:
                P.act(lambda e: e.activation(out=sg[:, 0, :], in_=t[:, gs, :], func=AF.Silu), reads=[key],
                      writes=[('sg', 0)])
            return key

        pcnt = [0]

        def load_x1t(i):
            for cc in range(4):
                P.dma('sp', lambda e, cc=cc: e.dma_start(
                    out=x1t[:, :, cc * 128:(cc + 1) * 128], in_=c.x1T_s[TM[i] + cc].rearrange("p (k t) -> p k t", k=16)),
                    writes=[('x1t', cc)])

        def do_tile(i):
            if i == 0:
                load_x1t(0)
                tile_state['wd_next'] = load_wd(0)
            X1T = [('x1t', cc) for cc in range(4)]
            nxt = load_wu(0) if i == 0 else tile_state['wu_next']
            for g in range(22):
                if i > 0 and g in (2, 7, 12, 17):
                    finish_pair(i - 1, 0, ms=((g - 2) // 5,))
                s = nxt
                if g + 1 < 22:
                    nxt = load_wu(g + 1)
                elif i + 1 < NT:
                    nxt = load_wu(0)
                    tile_state['wu_next'] = nxt
                for pp in range(2):
                    p = g * 2 + pp
                    gs = pcnt[0] % 2
                    pcnt[0] += 1
                    for gv in range(2):
                        b = pp * 2 + gv
                        blk = p + 44 * gv
                        for kt in range(16):
                            P.pe(lambda e, b=b, kt=kt, gv=gv, pp=pp, s=s: e.matmul(
                                psU[:, b, :], lhsT=wu[:, s, kt, gv, pp * 128:(pp + 1) * 128], rhs=x1t[:, kt, :],
                                start=(kt == 0), stop=(kt == 15)),
                                reads=[('wu', s)] + X1T, banks=[('U', b)])
                        if i == 0:
                            for kt in range(16):
                                P.pe(lambda e, kt=kt, gv=gv, pp=pp, s=s: e.matmul(
                                    psH[:, 0:2 * NT], lhsT=wu[:, s, kt, gv, pp * 128:(pp + 1) * 128], rhs=xh[:, kt, :],
                                    start=(kt == 0), stop=(kt == 15)),
                                    reads=[('wu', s), 'xh'], banks=['H'])
                            P.act(lambda e, blk=blk: e.copy(out=Ah[:, blk, :], in_=psH[:, 0:2 * NT]), banks=['H'],
                                  writes=[('Ah', blk)])
                    kg = conv_path(tg, p, pp * 2 + 0, gs, i, True)
                    if c.debug and i == 0 and p == 0:
                        c.dbg(P, 'tg', tg[:, gs, :], [128, 512], F32, [kg])
                        c.dbg(P, 'sg', sg[:, 0, :], [128, 512], F32, [('sg', 0)])
                    kv = conv_path(tv, p + 44, pp * 2 + 1, gs, i, False)
                    P.pool(lambda e, p=p, gs=gs: e.tensor_tensor(out=hT[:, p, :], in0=sg[:, 0, :], in1=tv[:, gs, :],
                                                                 op=ALU.mult),
                           reads=[('sg', 0), kv], writes=[('hT', p)])
            for m in range(4):
                j = TM[i] + m
                P.dma('sp', lambda e, j=j, m=m: e.dma_start(out=R[:, m, :], in_=c.x1_s[j * 128:(j + 1) * 128, :]),
                      writes=[('R', m)])
            if i + 1 < NT:
                load_x1t(i + 1)
            HTK = [('hT', p) for p in range(44)]
            if i == 0:
                c.dbg(P, 'hT', hT[:].rearrange("p a b -> p (a b)"), [128, 44 * 512], BF16, HTK)
            for dg in range(8):
                sd = tile_state['wd_next']
                if dg + 1 < 8:
                    tile_state['wd_next'] = load_wd(dg + 1)
                elif i + 1 < NT:
                    tile_state['wd_next'] = load_wd(0)
                for m in range(4):
                    rs = m
                    b = m % 2
                    for blk in range(44):
                        P.pe(lambda e, b=b, blk=blk, m=m, sd=sd: e.matmul(
                            psD[:, b, 0:256], lhsT=hT[:, blk, m * 128:(m + 1) * 128], rhs=wd[:, sd, blk, :],
                            start=(blk == 0), stop=(blk == 43)),
                            reads=[('hT', blk), ('wd', sd)], banks=[('D', b)])
                    P.dve(lambda e, b=b, rs=rs, dg=dg: e.tensor_tensor(
                        out=R[:, rs, dg * 256:(dg + 1) * 256], in0=psD[:, b, 0:256],
                        in1=R[:, rs, dg * 256:(dg + 1) * 256], op=ALU.add),
                        banks=[('D', b)], reads=[('R', rs)], writes=[('R', rs)])
            if i == NT - 1:
                finish_pair(i, 0)
                finish_pair(i, 1)

        def finish_pair(i, mh, ms=None):
            for m in (ms if ms is not None else (2 * mh, 2 * mh + 1)):
                j = 4 * i + m
                rs = m
                for q in range(4):
                    P.dve(lambda e, q=q, rs=rs: e.bn_stats(out=stats[:, q, :], in_=R[:, rs, q * 512:(q + 1) * 512]),
                          reads=[('R', rs)], writes=['stats'])
                P.dve(lambda e: e.bn_aggr(out=mv[:], in_=stats[:].rearrange("p a b -> p (a b)")),
                      reads=['stats'], writes=['mv'])
                P.pool(lambda e: e.tensor_scalar(out=lnv[:], in0=mv[:, 1:2], scalar1=LN_EPS, scalar2=1.0,
                                                 op0=ALU.add, op1=ALU.mult), reads=['mv'], writes=['lnv'])
                P.pool(lambda e: e.tensor_tensor(out=rstd[:], in0=lnv[:], in1=negh[:], op=ALU.pow),
                       reads=['lnv', 'negh'], writes=['rstd'])
                P.dve(lambda e: e.scalar_tensor_tensor(out=nmr[:], in0=mv[:, 0:1], scalar=-1.0, in1=rstd[:],
                                                       op0=ALU.mult, op1=ALU.mult),
                      reads=['mv', 'rstd'], writes=['nmr'])
                P.dve(lambda e, rs=rs: e.tensor_scalar(out=R[:, rs, :], in0=R[:, rs, :], scalar1=rstd[:, 0:1],
                                                       scalar2=nmr[:, 0:1], op0=ALU.mult, op1=ALU.add),
                      reads=[('R', rs), 'rstd', 'nmr'], writes=[('R', rs)])
                P.dve(lambda e, rs=rs: e.tensor_tensor(out=R[:, rs, :], in0=R[:, rs, :], in1=g2[:], op=ALU.mult),
                      reads=[('R', rs), 'g2'], writes=[('R', rs)])
                P.pool(lambda e, rs=rs: e.tensor_tensor(out=R[:, rs, :], in0=R[:, rs, :], in1=b2[:], op=ALU.add),
                       reads=[('R', rs), 'b2'], writes=[('R', rs)])
                P.dma('pool', lambda e, j=j, rs=rs: e.dma_start(out=c.y[j * 128:(j + 1) * 128, :], in_=R[:, rs, :]),
                      reads=[('R', rs)])

        tile_state = {}
        for i in range(NT):
            do_tile(i)
        P.run()


def build(lay_args=LAY_FULL, debug=False):
    nc = bass.Bass("TRN2", target_bir_lowering=False)
    lay = Lay(*lay_args)
    with ExitStack() as es:
        c = declare(nc, lay, debug, es)
        phase1(nc, c)
        phase2(nc, c)
        phase3(nc, c)
        phase4(nc, c)
        phase5(nc, c)
    return nc


GPERM = [0, 1, 2, 3, 8, 9, 10, 11, 4, 5, 6, 7, 12, 13, 14, 15]
GPERM_REV = [8, 9, 10, 11, 0, 1, 2, 3, 12, 13, 14, 15, 4, 5, 6, 7]


def shared_maps(inp):
    w_in = inp["w_in"][0]
    b_in = inp["b_in"][0]
    m = {}
    m["w_kv"] = np.ascontiguousarray(w_in[:, 1024:3072])
    m["w_qou"] = np.ascontiguousarray(np.concatenate([w_in[:, 0:1024], w_in[:, 3072:4096], w_in[:, 4112:5136]], axis=1))
    m["b_q"] = np.ascontiguousarray(b_in[0:1024].reshape(8, 128).T)
    m["b_ou"] = np.ascontiguousarray(np.concatenate([b_in[3072:4096], b_in[4112:5136]])[None, :])
    m["gain"] = np.ascontiguousarray(inp["mh_norm_g"][0][None, :])
    m["w_pool"] = np.ascontiguousarray(inp["w_pool"][0])
    m["pool_scale"] = np.ascontiguousarray(inp["pool_scale"][0][None, :])
    m["w_out"] = np.ascontiguousarray(inp["w_out"][0])
    m["b_out"] = np.ascontiguousarray(inp["b_out"][0][None, :])
    m["ln1_g"] = np.ascontiguousarray(inp["ln1_g"][0][None, :])
    m["ln1_b"] = np.ascontiguousarray(inp["ln1_b"][0][None, :])
    m["w_up"] = np.ascontiguousarray(inp["w_up"][0])
    m["b_up"] = np.ascontiguousarray(inp["b_up"][0].reshape(88, 128).T)
    m["b_conv"] = np.ascontiguousarray(inp["b_conv"][0].reshape(88, 128).T)
    m["w_down"] = np.ascontiguousarray(inp["w_down"][0])
    m["b_down"] = np.ascontiguousarray(inp["b_down"][0][None, :])
    m["ln2_g"] = np.ascontiguousarray(inp["ln2_g"][0][None, :])
    m["ln2_b"] = np.ascontiguousarray(inp["ln2_b"][0][None, :])
    var = {}
    for rev in (False, True):
        v = {}
        gp = GPERM_REV if rev else GPERM
        v["w_g"] = np.ascontiguousarray(w_in[:, 4096:4112][:, gp])
        v["b_kvg"] = np.ascontiguousarray(np.concatenate([b_in[1024:3072], b_in[4096:4112][gp]])[None, :])
        wc = inp["w_conv"][0]
        if rev:
            wc = wc[::-1]
        v["w_conv"] = np.ascontiguousarray(wc.reshape(3, 88, 128).transpose(2, 0, 1).reshape(128, 264))
        cst, pm = _consts(rev)
        v["consts"] = cst
        v["pmats"] = pm
        var[rev] = v
    return m, var


def _xT(chunks):
    n = chunks.shape[0]
    return np.ascontiguousarray(chunks.reshape(n, 128, 16, 128).transpose(0, 3, 2, 1).reshape(n, 128, 2048))


def core_map(shared, var, lay, Xp, Xs, stype, rev):
    m = dict(shared)
    m.update(var[bool(rev)])
    if rev:
        Xp = Xp[::-1]
        Xs = Xs[::-1]
    cp = Xp.reshape(-1, 128, D)
    cs = Xs.reshape(-1, 128, D)
    z = np.zeros((128, D), np.float32)
    slots = [cp[i] for i in range(lay.NPS)]
    nb = lay.S_bef + 1
    if stype == 0:
        seq = [None] * nb + list(range(0, lay.S_own + 1 + lay.S_aft))
    else:
        nreal = lay.S_aft - lay.S_pad
        seq = list(range(0, nb + lay.S_own + 1 + nreal)) + [None] * lay.S_pad
    assert len(seq) == lay.NSS
    slots += [z if i is None else cs[i] for i in seq]
    allc = np.stack(slots, axis=0)
    m["xT"] = _xT(allc)
    m["x"] = np.ascontiguousarray(allc[lay.M].reshape(lay.NM * 128, D))
    fl = np.zeros((128, 4), np.float32)
    fl[:, 0] = 1.0 if stype == 1 else 0.0
    fl[:, 1] = 1.0 - fl[:, 0]
    fl[:, 2] = 0.0 if (stype == 1 and lay.S_pad) else 1.0
    m["flags"] = fl
    return m


def place_outputs(lay, y, stype, rev, yp_out, ys_out):
    npo = lay.P_own * 128
    yp = y[:npo]
    ys = y[npo:]
    Sp = yp_out.shape[0]
    Ss = ys_out.shape[0]
    own0 = 0 if stype == 0 else (lay.S_bef + 1) * 128
    nso = lay.S_own * 128
    if not rev:
        yp_out[0:npo] = yp
        ys_out[own0:own0 + nso] = ys
    else:
        yp_out[Sp - npo:Sp] = yp[::-1]
        ys_out[Ss - own0 - nso:Ss - own0] = ys[::-1]


CORE_ASSIGN = [(0, 0, 0, 0), (0, 0, 0, 1), (1, 0, 1, 0), (1, 0, 1, 1),
               (2, 1, 0, 0), (2, 1, 0, 1), (3, 1, 1, 0), (3, 1, 1, 1)]

_NC_CACHE = {}


def kernel(**inputs):
    inp = {k: np.asarray(v) for k, v in inputs.items()}
    lay = Lay(*LAY_FULL)
    shared, var = shared_maps(inp)
    xp = inp["x_prompt"].astype(np.float32, copy=False)
    xs = inp["x_sample"].astype(np.float32, copy=False)
    maps = [core_map(shared, var, lay, xp[p], xs[s], st, rv) for (p, s, st, rv) in CORE_ASSIGN]
    if "full" not in _NC_CACHE:
        _NC_CACHE["full"] = build(LAY_FULL)
    nc = _NC_CACHE["full"]
    res = run_bass_kernel_spmd(nc, maps, core_ids=list(range(NCORES)))
    y_prompt = np.zeros(xp.shape, np.float32)
    y_sample = np.zeros(xs.shape, np.float32)
    for core, (p, s, st, rv) in enumerate(CORE_ASSIGN):
        place_outputs(lay, np.asarray(res.results[core]["y"], dtype=np.float32), st, rv, y_prompt[p], y_sample[s])
    return (y_prompt, y_sample)
```

```python
import numpy as np
from contextlib import ExitStack
import concourse.bass as bass
import concourse.mybir as mybir
from concourse.bass_utils import run_bass_kernel_spmd

F32 = mybir.dt.float32
BF16 = mybir.dt.bfloat16
AF = mybir.ActivationFunctionType
ALU = mybir.AluOpType

D = 2048
DM = 1024
H = 4
HD = 256
DFF = 5632
NCORES = 8
LN_EPS = 1e-5
ALPHA = 2.0 ** 0.25
POOL_W = (2, 4, 8, 16)

COMPUTE = ('pe', 'act', 'dve', 'pool')
import os
_STOP = int(os.environ.get('P3STOP', '0'))


class Instr:
    __slots__ = ('eng', 'fn', 'deps', 'is_dma', 'needs_inc', 'semval', 'dsem', 'dval', 'gid')


class Sems:
    def __init__(self, nc, es, ndma=12):
        self.ndma = ndma
        self.c = {e: es.enter_context(nc.semaphore("c_" + e)) for e in COMPUTE}
        self.d = {e: [es.enter_context(nc.semaphore("d_%s%d" % (e, i))) for i in range(ndma)]
                  for e in ('act', 'pool', 'sp')}
        self.cbase = {e: 0 for e in COMPUTE}
        self.dbase = {e: [0] * ndma for e in ('act', 'pool', 'sp')}


class Prog:
    def __init__(self, nc, name, sems):
        self.nc = nc
        self.name = name
        self.sems = sems
        ndma_sems = sems.ndma
        self.q = {e: [] for e in ('pe', 'act', 'dve', 'pool', 'sp')}
        self.bufs = {}
        self.gid = 0
        self.ndma_sems = ndma_sems
        self.dma_count = {e: 0 for e in ('act', 'pool', 'sp')}
        self.dma_hist = {e: [] for e in ('act', 'pool', 'sp')}

    def _add(self, eng, fn, reads, writes, is_dma, banks=()):
        writes = list(writes) + [('bank', b) for b in banks]
        ins = Instr()
        ins.eng = eng
        ins.fn = fn
        ins.is_dma = is_dma
        ins.needs_inc = False
        ins.semval = None
        ins.dsem = None
        ins.dval = None
        ins.gid = self.gid
        self.gid += 1
        deps = {}
        for b in reads:
            st = self.bufs.get(b)
            if st is not None and st[0] is not None:
                deps[id(st[0])] = (st[0], True)
        for b in writes:
            st = self.bufs.get(b)
            if st is not None:
                if st[0] is not None and id(st[0]) not in deps:
                    deps[id(st[0])] = (st[0], False)
                for r in st[1]:
                    if id(r) not in deps:
                        deps[id(r)] = (r, False)
        for b in reads:
            st = self.bufs.get(b)
            if st is None:
                st = [None, []]
                self.bufs[b] = st
            st[1].append(ins)
        for b in writes:
            self.bufs[b] = [ins, []]
        out = []
        for d, raw in deps.values():
            if d is ins:
                continue
            if (not is_dma) and (not d.is_dma) and d.eng == eng:
                if eng == 'pe':
                    continue
            out.append(d)
        if is_dma:
            k = self.dma_count[eng]
            self.dma_count[eng] += 1
            hist = self.dma_hist[eng]
            if k >= self.ndma_sems:
                out.append(hist[k - self.ndma_sems])
            hist.append(ins)
            ins.dsem = k % self.ndma_sems
            ins.dval = self.sems.dbase[eng][ins.dsem] + 16 * (k // self.ndma_sems + 1)
        for d in out:
            if not d.is_dma:
                d.needs_inc = True
        ins.deps = out
        self.q[eng].append(ins)
        return ins

    def pe(self, fn, reads=(), writes=(), banks=()):
        return self._add('pe', fn, reads, writes, False, banks)

    def act(self, fn, reads=(), writes=(), banks=()):
        return self._add('act', fn, reads, writes, False, banks)

    def dve(self, fn, reads=(), writes=(), banks=()):
        return self._add('dve', fn, reads, writes, False, banks)

    def pool(self, fn, reads=(), writes=(), banks=()):
        return self._add('pool', fn, reads, writes, False, banks)

    def dma(self, q, fn, reads=(), writes=()):
        return self._add(q, fn, reads, writes, True)

    def run(self):
        nc = self.nc
        sems = self.sems
        for e in COMPUTE:
            c = sems.cbase[e]
            for ins in self.q[e]:
                if (not ins.is_dma) and ins.needs_inc:
                    c += 1
                    ins.semval = c
            sems.cbase[e] = c
        for e in ('act', 'pool', 'sp'):
            for ins in self.q[e]:
                if ins.is_dma:
                    sems.dbase[e][ins.dsem] = ins.dval
        with ExitStack() as st:
            csem = sems.c
            dsem = sems.d
            block = st.enter_context(nc.Block())
            engmap = {'pe': (block.tensor, nc.tensor), 'act': (block.scalar, nc.scalar),
                      'dve': (block.vector, nc.vector), 'pool': (block.gpsimd, nc.gpsimd),
                      'sp': (block.sync, nc.sync)}

            def make(e):
                def body(eng):
                    seen = {}
                    for ins in self.q[e]:
                        for d in ins.deps:
                            if d.is_dma:
                                key = ('d', d.eng, d.dsem)
                                sem, val = dsem[d.eng][d.dsem], d.dval
                            else:
                                key = ('c', d.eng)
                                sem, val = csem[d.eng], d.semval
                            if seen.get(key, 0) >= val:
                                continue
                            seen[key] = val
                            eng.wait_ge(sem, val)
                        r = ins.fn(eng)
                        if ins.is_dma:
                            r.then_inc(dsem[e][ins.dsem], 16)
                        elif ins.needs_inc:
                            r.then_inc(csem[e], 1)
                    if e in dsem:
                        last = {}
                        for ins in self.q[e]:
                            if ins.is_dma:
                                last[ins.dsem] = ins.dval
                        for k, v in last.items():
                            if seen.get(('d', e, k), 0) < v:
                                eng.wait_ge(dsem[e][k], v)
                return body

            for e, (dec, _) in engmap.items():
                dec(make(e))


def _pool_mats(rev=False):
    mats = np.zeros((4, 5, 128, 128), np.float32)
    for g, w in enumerate(POOL_W):
        lo_off, hi_off = (w // 2, w - w // 2) if not rev else (w - w // 2 - 1, w // 2 + 1)
        for t in range(128):
            lo, hi = t - lo_off, t + hi_off
            for s in range(lo, hi):
                if s < 0:
                    mats[g, 0, s + 128, t] += 1.0 / w
                elif s >= 128:
                    mats[g, 2, s - 128, t] += 1.0 / w
                else:
                    mats[g, 1, s, t] += 1.0 / w
            mats[g, 1, t, t] -= 1.0
            lo_c = max(lo, 0)
            cnt = hi - lo_c if hi <= 128 else None
            for s in range(lo_c, min(hi, 128)):
                mats[g, 3, s, t] += 1.0 / (min(hi, 10 ** 9) - lo_c)
            mats[g, 3, t, t] -= 1.0
            hi_c = min(hi, 128)
            for s in range(max(lo, 0), hi_c):
                mats[g, 4, s, t] += 1.0 / (hi_c - lo)
            mats[g, 4, t, t] -= 1.0
    return mats


def _consts(rev=False):
    s = np.arange(128)[:, None]
    t = np.arange(128)[None, :]
    trif = (s <= t).astype(np.float32)
    trib = (s >= t).astype(np.float32)
    ones = np.ones((128, 128), np.float32)
    ident = np.eye(128, dtype=np.float32)
    c = np.concatenate([trif, trib, ones, ident], axis=1)
    pm = _pool_mats(rev).reshape(20, 128, 128).transpose(1, 0, 2).reshape(128, 20 * 128)
    return np.ascontiguousarray(c), np.ascontiguousarray(pm)


class Ctx:
    pass


class Lay:
    def __init__(self, P_own, P_ctx, S_bef, S_own, S_aft, S_pad):
        self.P_own, self.P_ctx, self.S_bef, self.S_own, self.S_aft, self.S_pad = P_own, P_ctx, S_bef, S_own, S_aft, S_pad
        self.NPS = P_own + 1 + P_ctx
        self.S0 = self.NPS
        self.NSS = S_bef + 1 + S_own + 1 + S_aft
        self.NX = self.NPS + self.NSS
        self.P_M = list(range(0, P_own + 1))
        self.S_M = list(range(self.S0 + S_bef, self.S0 + S_bef + S_own + 2))
        self.M = self.P_M + self.S_M
        self.NM = len(self.M)
        self.segs = [(0, len(self.P_M)), (len(self.P_M), self.NM)]
        self.M2 = list(self.M)
        extra = P_own + 1
        while len(self.M2) % 4:
            self.M2.append(extra)
            extra += 1
        self.tiles = [4 * i for i in range(P_own // 4)] + [len(self.P_M) + 1 + 4 * i for i in range(S_own // 4)]
        self.NTILE = len(self.tiles)
        self.iS = P_own // 4
        self.NF = P_own + S_own

    def seg_of(self, r):
        return 0 if r < len(self.P_M) else 1


LAY_FULL = (32, 31, 7, 8, 23, 8)


def _alloc(es, nc):
    def sb(name, shape, dt):
        return es.enter_context(nc.sbuf_tensor(name, shape, dt))

    def ps(name, shape, dt=F32):
        return es.enter_context(nc.psum_tensor(name, shape, dt))
    return sb, ps


def declare(nc, lay, debug, es):
    c = Ctx()
    c.lay = lay
    NCH = lay.NX
    c.NCH = NCH
    c.sems = Sems(nc, es)
    NT = lay.NM * 128

    def inp(name, shape, dt=F32):
        return nc.dram_tensor(name, shape, dt, kind="ExternalInput").ap()

    def scr(name, shape, dt):
        return nc.dram_tensor(name, shape, dt, kind="ExternalOutput" if debug else "Internal").ap()

    c.xT = inp("xT", [NCH, 128, 2048])
    c.x = inp("x", [NT, D])
    c.w_kv = inp("w_kv", [D, 2048])
    c.w_g = inp("w_g", [D, 16])
    c.w_qou = inp("w_qou", [D, 3072])
    c.b_kvg = inp("b_kvg", [1, 2064])
    c.b_q = inp("b_q", [128, 8])
    c.b_ou = inp("b_ou", [1, 2048])
    c.gain = inp("gain", [1, 1024])
    c.w_pool = inp("w_pool", [4, 256, 256])
    c.pool_scale = inp("pool_scale", [1, 1024])
    c.w_out = inp("w_out", [D, D])
    c.b_out = inp("b_out", [1, D])
    c.ln1_g = inp("ln1_g", [1, D])
    c.ln1_b = inp("ln1_b", [1, D])
    c.w_up = inp("w_up", [D, 2 * DFF])
    c.b_up = inp("b_up", [128, 88])
    c.w_conv = inp("w_conv", [128, 3 * 88])
    c.b_conv = inp("b_conv", [128, 88])
    c.w_down = inp("w_down", [DFF, D])
    c.b_down = inp("b_down", [1, D])
    c.ln2_g = inp("ln2_g", [1, D])
    c.ln2_b = inp("ln2_b", [1, D])
    c.flags = inp("flags", [128, 4])
    c.consts = inp("consts", [128, 512])
    c.pmats = inp("pmats", [128, 20 * 128])
    c.y = nc.dram_tensor("y", [lay.NF * 128, D], F32, kind="ExternalOutput").ap()
    c.debug = debug
    c.nc = nc

    def dbg(P, name, tile_ap, shape, dt, reads):
        if not debug:
            return
        t = nc.dram_tensor("dbg_" + name, shape, dt, kind="ExternalOutput").ap()
        P.dma('sp', lambda e: e.dma_start(out=t, in_=tile_ap), reads=reads)
    c.dbg = dbg
    c.k_s = scr("k_s", [NCH, 128, 1024], BF16)
    c.v_s = scr("v_s", [NCH, 128, 1024], BF16)
    c.sb_s = scr("sb_s", [NCH, 128, 8 * 257], BF16)
    c.g_s = scr("g_s", [128, NCH * 32], F32)
    c.qT_s = scr("qT_s", [NCH, 128, 1024], BF16)
    c.so_s = scr("so_s", [NCH, 128, 1024], BF16)
    c.u_s = scr("u_s", [NCH, 128, 1024], BF16)
    c.hm_s = scr("hm_s", [NCH, 128, 1024], BF16)
    c.x1_s = scr("x1_s", [NT, D], F32)
    if debug:
        c.x1d_s = scr("x1d_s", [NT, D], F32)
    c.x1T_s = scr("x1T_s", [lay.NM, 128, 2048], BF16)
    c.sf0_s = scr("sf0_s", [128, 8 * 257], F32)
    c.wu_t = scr("wu_t", [22, 128, 16 * 2 * 256], BF16)
    c.wd_t = scr("wd_t", [8, 128, 44 * 256], BF16)
    c.xh = es.enter_context(nc.sbuf_tensor("g_xh", [128, 16, 2 * lay.NTILE], BF16))
    return c


def phase1(nc, c):
    NCH = c.NCH
    P = Prog(nc, "p1", c.sems)
    with ExitStack() as es:
        sb, ps = _alloc(es, nc)
        wkv = sb("p1_wkv", [128, 16, 2048], BF16)
        wg = sb("p1_wg", [128, 16, 16], BF16)
        bias = sb("p1_bias", [128, 2064], F32)
        cst = sb("p1_cst", [128, 512], F32)
        flg = sb("p1_flg", [128, 4], F32)
        xt = sb("p1_xt", [128, 3, 2048], BF16)
        kb = sb("p1_kb", [128, 2, 1024], BF16)
        va = sb("p1_va", [128, 2, 4, 257], BF16)
        kd = sb("p1_kd", [128, 2, 1024], BF16)
        gs = sb("p1_gs", [128, 16], F32)
        e1 = sb("p1_e1", [128, 8], F32)
        Lt = sb("p1_L", [128, 8], F32)
        tmp8 = sb("p1_tmp8", [128, 8], F32)
        wv = sb("p1_wv", [128, 8], F32)
        G = sb("p1_G", [128, NCH, 32], F32)
        S = sb("p1_S", [128, 8, 257], F32)
        Sst = sb("p1_Sst", [128, 2, 8 * 257], BF16)
        psA = ps("p1_psA", [128, 2, 512])
        psG = ps("p1_psG", [128, 512])
        psS = ps("p1_psS", [128, 4, 512])

        wsrc = c.w_kv.rearrange("(kt p) n -> p kt n", p=128)
        for i in range(4):
            P.dma('pool', lambda e, i=i: e.dma_start(out=wkv[:, 4 * i:4 * i + 4, :], in_=wsrc[:, 4 * i:4 * i + 4, :]),
                  writes=[('wkv', i)])
        P.dma('pool', lambda e: e.dma_start(out=wg[:], in_=c.w_g.rearrange("(kt p) n -> p kt n", p=128)),
              writes=['wg'])
        P.dma('sp', lambda e: e.dma_start(out=bias[:], in_=c.b_kvg.to_broadcast([128, 2064])), writes=['bias'])
        P.dma('sp', lambda e: e.dma_start(out=cst[:], in_=c.consts), writes=['cst'])
        P.dma('sp', lambda e: e.dma_start(out=flg[:], in_=c.flags), writes=['flg'])
        P.dve(lambda e: e.memset(S[:], 0.0), writes=[('S', i) for i in range(8)])
        for s in range(2):
            P.pool(lambda e, s=s: e.memset(va[:, s, :, 256:257], 1.0), writes=[('va', s)])
        trif = cst[:, 0:128]
        trib = cst[:, 128:256]
        ones = cst[:, 256:384]
        WKV = [('wkv', i) for i in range(4)]

        lay = c.lay
        SK = [('S', i) for i in range(8)]
        Mset = set(lay.M)
        order = []
        for sl in range(lay.S0, lay.S0 + lay.S_bef):
            order.append([sl, 'f', None, None])
        if lay.S_bef:
            order[0][2] = 'zero'
            order[-1][3] = 'sf0'
        first = len(order)
        for sl in range(lay.NPS - 1, -1, -1):
            order.append([sl, 'b', None, None])
        order[first][2] = 'zero'
        first = len(order)
        for sl in range(lay.S0 + lay.NSS - 1, lay.S0 + lay.S_bef - 1, -1):
            order.append([sl, 'b', None, None])
        order[first][2] = 'zero'
        if lay.S_pad:
            order[first + lay.S_pad - 1][3] = 'fpad'
        NORD = len(order)

        def load_x(pos):
            s = pos % 3
            sl = order[pos][0]
            P.dma('pool', lambda e: e.dma_start(out=xt[:, s, :], in_=c.xT[sl]), writes=[('xt', s)])

        load_x(0)
        if NORD > 1:
            load_x(1)
        def part_a(idx, j):
            mode = order[idx][1]
            if idx + 2 < NORD:
                load_x(idx + 2)
            s3 = idx % 3
            s2 = idx % 2
            for kt in range(16):
                P.pe(lambda e, kt=kt: e.matmul(
                    psG[:, 0:16], lhsT=xt[:, s3, kt * 128:(kt + 1) * 128], rhs=wg[:, kt, :],
                    start=(kt == 0), stop=(kt == 15)),
                    reads=[('xt', s3), 'wg'], banks=['G'])
            P.dve(lambda e: e.tensor_tensor(out=gs[:], in0=psG[:, 0:16], in1=bias[:, 2048:2064], op=ALU.add),
                  banks=['G'], reads=['bias'], writes=['gs'])
            P.act(lambda e: e.activation(out=e1[:], in_=gs[:, 8:16], func=AF.Exp, scale=-1.0),
                  reads=['gs'], writes=['e1'])
            P.act(lambda e: e.activation(out=Lt[:], in_=e1[:], func=AF.Ln, bias=1.0),
                  reads=['e1'], writes=['L'])
            CGS = (0, 1)
            for cg in CGS:
                b = cg % 2
                for kt in range(16):
                    P.pe(lambda e, b=b, kt=kt, cg=cg: e.matmul(
                        psA[:, b, :], lhsT=xt[:, s3, kt * 128:(kt + 1) * 128],
                        rhs=wkv[:, kt, cg * 512:(cg + 1) * 512], start=(kt == 0), stop=(kt == 15)),
                        reads=[('xt', s3)] + WKV, banks=[('A', b)])
                if cg < 2:
                    P.dve(lambda e, b=b, cg=cg: e.tensor_tensor(
                        out=kb[:, s2, cg * 512:(cg + 1) * 512], in0=psA[:, b, :],
                        in1=bias[:, cg * 512:(cg + 1) * 512], op=ALU.add),
                        banks=[('A', b)], reads=['bias'], writes=[('kb', s2, cg)])
                else:
                    h0 = (cg - 2) * 2
                    P.dve(lambda e, b=b, cg=cg, h0=h0: e.tensor_tensor(
                        out=va[:, s2, h0:h0 + 2, 0:256],
                        in0=psA[:, b, :].rearrange("p (h d) -> p h d", h=2),
                        in1=bias[:, cg * 512:(cg + 1) * 512].rearrange("p (h d) -> p h d", h=2), op=ALU.add),
                        banks=[('A', b)], reads=['bias'], writes=[('va', s2)])
            P.pe(lambda e: e.matmul(psG[:, 32:36], lhsT=trif, rhs=Lt[:, 0:4], start=True, stop=True),
                 reads=['L', 'cst'], banks=['G'])
            P.pe(lambda e: e.matmul(psG[:, 36:40], lhsT=trib, rhs=Lt[:, 4:8], start=True, stop=True),
                 reads=['L', 'cst'], banks=['G'])
            P.pe(lambda e: e.matmul(psG[:, 40:48], lhsT=ones, rhs=Lt[:, 0:8], start=True, stop=True),
                 reads=['L', 'cst'], banks=['G'])
            P.dve(lambda e: e.tensor_tensor(out=tmp8[:], in0=psG[:, 32:40], in1=gs[:, 0:8], op=ALU.add),
                  banks=['G'], reads=['gs'], writes=['tmp8'])
            P.act(lambda e: e.activation(out=wv[:], in_=tmp8[:], func=AF.Exp), reads=['tmp8'], writes=['wv'])
            P.act(lambda e, j=j: e.activation(out=G[:, j, 8:16], in_=psG[:, 32:40], func=AF.Exp),
                  banks=['G'], writes=[('G', j, 1)])
            P.act(lambda e, j=j: e.activation(out=G[:, j, 16:24], in_=psG[:, 40:48], func=AF.Exp, scale=-1.0),
                  banks=['G'], writes=[('G', j, 2)])
            P.dve(lambda e, j=j: e.tensor_scalar(out=G[:, j, 0:8], in0=wv[:], scalar1=1.0 / 16.0, scalar2=None,
                                                 op0=ALU.mult),
                  reads=['wv'], writes=[('G', j, 0)])
            P.dve(lambda e, j=j: e.tensor_tensor(out=G[:, j, 24:32], in0=G[:, j, 0:8], in1=G[:, j, 16:24],
                                                 op=ALU.mult),
                  reads=[('G', j, 0), ('G', j, 2)], writes=[('G', j, 3)])
            CGS = (2, 3)
            for cg in CGS:
                b = cg % 2
                for kt in range(16):
                    P.pe(lambda e, b=b, kt=kt, cg=cg: e.matmul(
                        psA[:, b, :], lhsT=xt[:, s3, kt * 128:(kt + 1) * 128],
                        rhs=wkv[:, kt, cg * 512:(cg + 1) * 512], start=(kt == 0), stop=(kt == 15)),
                        reads=[('xt', s3)] + WKV, banks=[('A', b)])
                if cg < 2:
                    P.dve(lambda e, b=b, cg=cg: e.tensor_tensor(
                        out=kb[:, s2, cg * 512:(cg + 1) * 512], in0=psA[:, b, :],
                        in1=bias[:, cg * 512:(cg + 1) * 512], op=ALU.add),
                        banks=[('A', b)], reads=['bias'], writes=[('kb', s2, cg)])
                else:
                    h0 = (cg - 2) * 2
                    P.dve(lambda e, b=b, cg=cg, h0=h0: e.tensor_tensor(
                        out=va[:, s2, h0:h0 + 2, 0:256],
                        in0=psA[:, b, :].rearrange("p (h d) -> p h d", h=2),
                        in1=bias[:, cg * 512:(cg + 1) * 512].rearrange("p (h d) -> p h d", h=2), op=ALU.add),
                        banks=[('A', b)], reads=['bias'], writes=[('va', s2)])
            if j in Mset and mode == 'b':
                P.dma('sp', lambda e, j=j: e.dma_start(out=c.k_s[j], in_=kb[:, s2, :]),
                      reads=[('kb', s2, 0), ('kb', s2, 1)])
                P.dma('sp', lambda e, j=j: e.dma_start(out=c.v_s[j].rearrange("p (h d) -> p h d", h=4),
                                                       in_=va[:, s2, :, 0:256]), reads=[('va', s2)])

        def part_b1(idx, j):
            mode, pre, post = order[idx][1], order[idx][2], order[idx][3]
            kcol = 28 if mode == 'b' else 24
            dcol = 20 if mode == 'b' else 16
            s2 = idx % 2
            if pre == 'zero':
                P.dve(lambda e: e.memset(S[:], 0.0), reads=SK, writes=SK)
            if j in Mset and mode == 'b':
                P.act(lambda e: e.copy(out=Sst[:, s2, :], in_=S[:].rearrange("p a b -> p (a b)")),
                      reads=[('S', i) for i in range(8)], writes=[('Sst', s2)])
                P.dma('sp', lambda e, j=j: e.dma_start(out=c.sb_s[j], in_=Sst[:, s2, :]), reads=[('Sst', s2)])
            for h in range(4):
                P.act(lambda e, h=h, j=j: e.activation(
                    out=kd[:, s2, h * 256:(h + 1) * 256], in_=kb[:, s2, h * 256:(h + 1) * 256],
                    func=AF.Copy, scale=G[:, j, kcol + h:kcol + 1 + h]),
                    reads=[('kb', s2, h // 2), ('G', j, 3)], writes=[('kd', s2, h)])

        def part_b2(idx, j):
            mode, pre, post = order[idx][1], order[idx][2], order[idx][3]
            kcol = 28 if mode == 'b' else 24
            dcol = 20 if mode == 'b' else 16
            s2 = idx % 2
            for h in range(4):
                for db in range(2):
                    i8 = h * 2 + db
                    bk = i8 % 4
                    P.pe(lambda e, h=h, db=db, bk=bk: e.matmul(
                        psS[:, bk, 0:257], lhsT=kd[:, s2, h * 256 + db * 128:h * 256 + db * 128 + 128],
                        rhs=va[:, s2, h, :], start=True, stop=True),
                        reads=[('kd', s2, h), ('va', s2)], banks=[('S', bk)])
                    P.dve(lambda e, h=h, i8=i8, bk=bk, j=j: e.scalar_tensor_tensor(
                        out=S[:, i8, :], in0=S[:, i8, :], scalar=G[:, j, dcol + h:dcol + 1 + h], in1=psS[:, bk, 0:257],
                        op0=ALU.mult, op1=ALU.add),
                        banks=[('S', bk)], reads=[('S', i8), ('G', j, 2)], writes=[('S', i8)])
            if post == 'fpad':
                P.dve(lambda e: e.tensor_scalar(out=S[:], in0=S[:], scalar1=flg[:, 2:3], scalar2=None, op0=ALU.mult),
                      reads=SK + ['flg'], writes=SK)
            if post == 'sf0':
                P.dve(lambda e: e.tensor_scalar(out=S[:], in0=S[:], scalar1=flg[:, 0:1], scalar2=None, op0=ALU.mult),
                      reads=SK + ['flg'], writes=SK)
                P.dma('sp', lambda e: e.dma_start(out=c.sf0_s, in_=S[:].rearrange("p a b -> p (a b)")), reads=SK)
        for idx in range(NORD):
            if idx > 0:
                part_b1(idx - 1, order[idx - 1][0])
            part_a(idx, order[idx][0])
            if idx > 0:
                part_b2(idx - 1, order[idx - 1][0])
        part_b1(NORD - 1, order[NORD - 1][0])
        part_b2(NORD - 1, order[NORD - 1][0])
        P.dma('sp', lambda e: e.dma_start(out=c.g_s, in_=G[:].rearrange("p a b -> p (a b)")),
              reads=[('G', j, i) for j in range(NCH) for i in range(4)])
        P.run()


def phase2(nc, c):
    M2 = c.lay.M2
    NST = len(M2) // 4
    P = Prog(nc, "p2", c.sems)
    with ExitStack() as es:
        sb, ps = _alloc(es, nc)
        w = sb("p2_w", [128, 16, 3072], BF16)
        bq = sb("p2_bq", [128, 8], F32)
        bou = sb("p2_bou", [128, 2048], F32)
        gn = sb("p2_gn", [128, 1024], F32)
        xt = sb("p2_xt", [128, 2, 4, 2048], BF16)
        qst = sb("p2_qst", [128, 2, 8, 512], BF16)
        ot = sb("p2_ot", [128, 2, 512], F32)
        ot2 = sb("p2_ot2", [128, 2, 512], F32)
        so = sb("p2_so", [128, 2, 1024], BF16)
        ub = sb("p2_ub", [128, 2, 1024], BF16)
        psQ = ps("p2_psQ", [128, 2, 512])
        psO = ps("p2_psO", [128, 4, 512])

        wsrc = c.w_qou.rearrange("(kt p) n -> p kt n", p=128)
        for i in range(4):
            for cgp in range(3):
                P.dma('pool', lambda e, i=i, cgp=cgp: e.dma_start(
                    out=w[:, 4 * i:4 * i + 4, cgp * 1024:(cgp + 1) * 1024],
                    in_=wsrc[:, 4 * i:4 * i + 4, cgp * 1024:(cgp + 1) * 1024]), writes=[('w', i, cgp)])
        WQ = [('w', i, 0) for i in range(4)]
        WO = [('w', i, 1) for i in range(4)]
        WU = [('w', i, 2) for i in range(4)]
        P.dma('sp', lambda e: e.dma_start(out=bq[:], in_=c.b_q), writes=['bq'])
        P.dma('sp', lambda e: e.dma_start(out=bou[:], in_=c.b_ou.to_broadcast([128, 2048])), writes=['bou'])
        P.dma('sp', lambda e: e.dma_start(out=gn[:], in_=c.gain.to_broadcast([128, 1024])), writes=['gn'])

        def load_x(st):
            s = st % 2
            for cc in range(4):
                P.dma('pool', lambda e, cc=cc: e.dma_start(out=xt[:, s, cc, :], in_=c.xT[M2[st * 4 + cc]]),
                      writes=[('xt', s, cc)])

        cnt = [0]

        def do_st(st):
            if st + 1 < NST:
                load_x(st + 1)
            s = st % 2
            XT = [('xt', s, cc) for cc in range(4)]
            for blk in range(8):
                b = blk % 2
                for kt in range(16):
                    P.pe(lambda e, blk=blk, b=b, kt=kt: e.matmul(
                        psQ[:, b, :], lhsT=w[:, kt, blk * 128:(blk + 1) * 128],
                        rhs=xt[:, s, :, kt * 128:(kt + 1) * 128], start=(kt == 0), stop=(kt == 15)),
                        reads=XT + WQ, banks=[('Q', b)])
                P.act(lambda e, blk=blk, b=b: e.activation(
                    out=qst[:, s, blk, :], in_=psQ[:, b, :], func=AF.Identity, bias=bq[:, blk:blk + 1]),
                    banks=[('Q', b)], reads=['bq'], writes=[('qst', s, blk)])
            for cc in range(4):
                j = M2[st * 4 + cc]
                P.dma('sp', lambda e, cc=cc, j=j: e.dma_start(
                    out=c.qT_s[j].rearrange("p (b t) -> p b t", b=8), in_=qst[:, s, :, cc * 128:(cc + 1) * 128]),
                    reads=[('qst', s, blk) for blk in range(8)])
            for cc in range(4):
                j = M2[st * 4 + cc]
                s2 = cnt[0] % 2
                cnt[0] += 1
                for cg in range(4):
                    b = cg
                    for kt in range(16):
                        P.pe(lambda e, cc=cc, cg=cg, b=b, kt=kt: e.matmul(
                            psO[:, b, :], lhsT=xt[:, s, cc, kt * 128:(kt + 1) * 128],
                            rhs=w[:, kt, 1024 + cg * 512:1024 + (cg + 1) * 512], start=(kt == 0), stop=(kt == 15)),
                            reads=[('xt', s, cc)] + (WO if cg < 2 else WU), banks=[('O', b)])
                    if cg < 2:
                        P.dve(lambda e, cg=cg, b=b: e.tensor_tensor(
                            out=ot[:, cg, :], in0=psO[:, b, :], in1=bou[:, cg * 512:(cg + 1) * 512], op=ALU.add),
                            banks=[('O', b)], reads=['bou'], writes=[('ot', cg)])
                        P.act(lambda e, cg=cg: e.activation(out=ot2[:, cg, :], in_=ot[:, cg, :], func=AF.Sigmoid),
                              reads=[('ot', cg)], writes=[('ot2', cg)])
                        P.pool(lambda e, cg=cg, s2=s2: e.tensor_tensor(
                            out=so[:, s2, cg * 512:(cg + 1) * 512], in0=ot2[:, cg, :],
                            in1=gn[:, cg * 512:(cg + 1) * 512], op=ALU.mult),
                            reads=[('ot2', cg), 'gn'], writes=[('so', s2, cg)])
                    else:
                        P.dve(lambda e, cg=cg, b=b, s2=s2: e.tensor_tensor(
                            out=ub[:, s2, (cg - 2) * 512:(cg - 1) * 512], in0=psO[:, b, :],
                            in1=bou[:, cg * 512:(cg + 1) * 512], op=ALU.add),
                            banks=[('O', b)], reads=['bou'], writes=[('ub', s2, cg)])
                P.dma('sp', lambda e, j=j, s2=s2: e.dma_start(out=c.so_s[j], in_=so[:, s2, :]),
                      reads=[('so', s2, 0), ('so', s2, 1)])
                P.dma('sp', lambda e, j=j, s2=s2: e.dma_start(out=c.u_s[j], in_=ub[:, s2, :]),
                      reads=[('ub', s2, 2), ('ub', s2, 3)])

        load_x(0)
        for st in range(NST):
            do_st(st)
        P.run()


def phase3(nc, c):
    NCH = c.NCH
    lay = c.lay
    M = lay.M
    NM = lay.NM
    RS = lay.segs[1][0]
    P = Prog(nc, "p3", c.sems)
    with ExitStack() as es:
        sb, ps = _alloc(es, nc)
        G = sb("p3_G", [128, NCH, 32], F32)
        cst = sb("p3_cst", [128, 512], F32)
        identb = sb("p3_identb", [128, 128], BF16)
        flg = sb("p3_flg", [128, 4], F32)
        qT = sb("p3_qT", [128, 3, 1024], BF16)
        kb = sb("p3_kb", [128, 3, 1024], BF16)
        va = sb("p3_va", [128, 3, 4, 257], BF16)
        so = sb("p3_so", [128, 3, 1024], BF16)
        sbs = sb("p3_sbs", [128, 3, 8, 257], BF16)
        Sf = sb("p3_Sf", [128, 8, 257], F32)
        Sfb = sb("p3_Sfb", [128, 2, 8, 257], BF16)
        kT = sb("p3_kT", [128, 2, 1024], BF16)
        kdf = sb("p3_kdf", [128, 2, 1024], BF16)
        STf = sb("p3_STf", [128, 2, 4, 128], BF16)
        STb = sb("p3_STb", [128, 2, 4, 128], BF16)
        hf32 = sb("p3_hf32", [128, 2, 1024], F32)
        h32 = sb("p3_h32", [128, 2, 1024], F32)
        stats = sb("p3_stats", [128, 4, 6], F32)
        mv = sb("p3_mv", [128, 4, 2], F32)
        den = sb("p3_den", [128, 4, 2], F32)
        rden = sb("p3_rden", [128, 4, 2], F32)
        lnv = sb("p3_lnv", [128, 4], F32)
        rstd = sb("p3_rstd", [128, 4], F32)
        nmr = sb("p3_nmr", [128, 4], F32)
        hm = sb("p3_hm", [128, 2, 1024], BF16)
        psT = ps("p3_psT", [128, 512])
        psSc = ps("p3_psSc", [128, 4, 128])
        psP = ps("p3_psP", [128, 4, 512])
        psD = ps("p3_psD", [128, 2, 512])
        psTb = psT[:].bitcast(BF16)

        P.dma('sp', lambda e: e.dma_start(out=G[:].rearrange("p a b -> p (a b)"), in_=c.g_s), writes=['G'])
        wsrc_o = c.w_out.rearrange("(kt p) n -> p kt n", p=128)
        for i in range(4):
            P.dma('pool', lambda e, i=i: e.dma_start(out=c.wout_pre[:, 4 * i:4 * i + 4, :],
                                                     in_=wsrc_o[:, 4 * i:4 * i + 4, :]), writes=[('wout_pre', i)])
        P.dma('sp', lambda e: e.dma_start(out=cst[:], in_=c.consts), writes=['cst'])
        P.dma('sp', lambda e: e.dma_start(out=flg[:], in_=c.flags), writes=['flg'])
        P.dve(lambda e: e.tensor_copy(out=identb[:], in_=cst[:, 384:512]), reads=['cst'], writes=['identb'])
        P.dve(lambda e: e.memset(Sf[:], 0.0), writes=[('Sf', i) for i in range(8)])
        P.pool(lambda e: e.memset(Sfb[:, 0, :, :], 0.0), writes=[('Sfb', 0)])
        for l in range(3):
            P.pool(lambda e, l=l: e.memset(va[:, l, :, 256:257], 1.0), writes=[('va', l)])
        trif = cst[:, 0:128]
        trib = cst[:, 128:256]

        def load(r):
            j = M[r]
            l = r % 3
            P.dma('sp', lambda e: e.dma_start(out=qT[:, l, :], in_=c.qT_s[j]), writes=[('qT', l)])
            P.dma('sp', lambda e: e.dma_start(out=kb[:, l, :], in_=c.k_s[j]), writes=[('kb', l)])
            P.dma('sp', lambda e: e.dma_start(out=va[:, l, :, 0:256],
                                              in_=c.v_s[j].rearrange("p (h d) -> p h d", h=4)), writes=[('va', l)])
            P.dma('sp', lambda e: e.dma_start(out=so[:, l, :], in_=c.so_s[j]), writes=[('so', l)])
            P.dma('sp', lambda e: e.dma_start(out=sbs[:, l, :, :].rearrange("p a b -> p (a b)"), in_=c.sb_s[j]),
                  writes=[('sbs', l)])

        def ctx(r):
            return M[r], r % 2, (r + 1) % 2, r % 3

        def st_init(r):
            j, s, nx, l = ctx(r)
            if r == RS:
                P.dma('sp', lambda e: e.dma_start(out=Sf[:].rearrange("p a b -> p (a b)"), in_=c.sf0_s),
                      writes=[('Sf', i) for i in range(8)])
                P.act(lambda e: e.copy(out=Sfb[:, s, :, :], in_=Sf[:]), reads=[('Sf', i) for i in range(8)],
                      writes=[('Sfb', s)])

        def st_x1(r):
            j, s, nx, l = ctx(r)
            for blk in range(8):
                P.pe(lambda e, blk=blk: e.transpose(
                    out=psTb[:, blk * 128:(blk + 1) * 128], in_=kb[:, l, blk * 128:(blk + 1) * 128],
                    identity=identb[:]),
                    reads=[('kb', l), 'identb'], banks=['T'])
            P.dve(lambda e: e.tensor_copy(out=kT[:, s, :], in_=psTb), banks=['T'], writes=[('kT', s)])
            for h in range(4):
                P.pool(lambda e, h=h: e.tensor_scalar(
                    out=kdf[:, s, h * 256:(h + 1) * 256], in0=kb[:, l, h * 256:(h + 1) * 256],
                    scalar1=G[:, j, 24 + h:25 + h], scalar2=0.0, op0=ALU.mult, op1=ALU.add),
                    reads=[('kb', l), 'G'], writes=[('kdf', s, h)])

        def st_x2(r):
            j, s, nx, l = ctx(r)
            for h in range(4):
                for blk in range(2):
                    cb = (2 * h + blk) * 128
                    P.pe(lambda e, h=h, blk=blk, cb=cb: e.matmul(
                        psSc[:, h, :], lhsT=kT[:, s, cb:cb + 128], rhs=qT[:, l, cb:cb + 128],
                        start=(blk == 0), stop=(blk == 1)),
                        reads=[('kT', s), ('qT', l)], banks=['Sc'])
            for h in range(4):
                P.dve(lambda e, h=h: e.scalar_tensor_tensor(
                    out=STf[:, s, h, :], in0=psSc[:, h, :], scalar=G[:, j, h:h + 1], in1=trif,
                    op0=ALU.mult, op1=ALU.mult),
                    banks=['Sc'], reads=['G', 'cst'], writes=[('STf', s, h)])
                P.dve(lambda e, h=h: e.scalar_tensor_tensor(
                    out=STb[:, s, h, :], in0=psSc[:, h, :], scalar=G[:, j, 4 + h:5 + h], in1=trib,
                    op0=ALU.mult, op1=ALU.mult),
                    banks=['Sc'], reads=['G', 'cst'], writes=[('STb', s, h)])

        def st_u(r):
            j, s, nx, l = ctx(r)
            for h in range(4):
                for db in range(2):
                    i8 = 2 * h + db
                    bk = i8 % 2
                    P.pe(lambda e, h=h, db=db, bk=bk: e.matmul(
                        psD[:, bk, 0:257], lhsT=kdf[:, s, h * 256 + db * 128:h * 256 + db * 128 + 128],
                        rhs=va[:, l, h, :], start=True, stop=True),
                        reads=[('kdf', s, h), ('va', l)], banks=[('D', bk)])
                    P.dve(lambda e, h=h, i8=i8, bk=bk: e.scalar_tensor_tensor(
                        out=Sf[:, i8, :], in0=Sf[:, i8, :], scalar=G[:, j, 16 + h:17 + h], in1=psD[:, bk, 0:257],
                        op0=ALU.mult, op1=ALU.add),
                        banks=[('D', bk)], reads=[('Sf', i8), 'G'], writes=[('Sf', i8)])
            SFK = [('Sf', i) for i in range(8)]
            if r == RS:
                P.dve(lambda e: e.tensor_scalar(out=Sf[:], in0=Sf[:], scalar1=flg[:, 0:1], scalar2=None,
                                                op0=ALU.mult), reads=SFK + ['flg'], writes=SFK)
            P.act(lambda e: e.copy(out=Sfb[:, nx, :, :], in_=Sf[:]), reads=SFK, writes=[('Sfb', nx)])

        def st_y(r):
            j, s, nx, l = ctx(r)
            for hp in range(2):
                hs = (2 * hp, 2 * hp + 1)
                for h in hs:
                    bf_ = 2 * (h % 2)
                    bb_ = bf_ + 1
                    c0 = (2 * h) * 128
                    c1 = (2 * h + 1) * 128
                    P.pe(lambda e, h=h, bf_=bf_: e.matmul(psP[:, bf_, 0:257], lhsT=STf[:, s, h, :], rhs=va[:, l, h, :],
                                                          start=True, stop=False),
                         reads=[('STf', s, h), ('va', l)], banks=[('P', bf_)])
                    P.pe(lambda e, h=h, bf_=bf_, c0=c0: e.matmul(psP[:, bf_, 0:257], lhsT=qT[:, l, c0:c0 + 128],
                                                                 rhs=Sfb[:, s, 2 * h, :], start=False, stop=False),
                         reads=[('qT', l), ('Sfb', s)], banks=[('P', bf_)])
                    P.pe(lambda e, h=h, bf_=bf_, c1=c1: e.matmul(psP[:, bf_, 0:257], lhsT=qT[:, l, c1:c1 + 128],
                                                                 rhs=Sfb[:, s, 2 * h + 1, :], start=False, stop=True),
                         reads=[('qT', l), ('Sfb', s)], banks=[('P', bf_)])
                    P.pe(lambda e, h=h, bb_=bb_: e.matmul(psP[:, bb_, 0:257], lhsT=STb[:, s, h, :], rhs=va[:, l, h, :],
                                                          start=True, stop=False),
                         reads=[('STb', s, h), ('va', l)], banks=[('P', bb_)])
                    P.pe(lambda e, h=h, bb_=bb_, c0=c0: e.matmul(psP[:, bb_, 0:257], lhsT=qT[:, l, c0:c0 + 128],
                                                                 rhs=sbs[:, l, 2 * h, :], start=False, stop=False),
                         reads=[('qT', l), ('sbs', l)], banks=[('P', bb_)])
                    P.pe(lambda e, h=h, bb_=bb_, c1=c1: e.matmul(psP[:, bb_, 0:257], lhsT=qT[:, l, c1:c1 + 128],
                                                                 rhs=sbs[:, l, 2 * h + 1, :], start=False, stop=True),
                         reads=[('qT', l), ('sbs', l)], banks=[('P', bb_)])
                for h in hs:
                    bf_ = 2 * (h % 2)
                    bb_ = bf_ + 1
                    c0 = (2 * h) * 128
                    c1 = (2 * h + 1) * 128
                    P.act(lambda e, h=h, bf_=bf_: e.activation(out=den[:, h, 0:1], in_=psP[:, bf_, 256:257], func=AF.Abs),
                          banks=[('P', bf_)], reads=[], writes=[('den', h)])
                    P.act(lambda e, h=h, bb_=bb_: e.activation(out=den[:, h, 1:2], in_=psP[:, bb_, 256:257], func=AF.Abs),
                          banks=[('P', bb_)], reads=[], writes=[('den', h)])
                h0 = 2 * hp
                P.dve(lambda e, h0=h0: e.tensor_tensor(
                    out=den[:, h0:h0 + 2, :], in0=den[:, h0:h0 + 2, :],
                    in1=G[:, j, 8:16].rearrange("p (d h) -> p h d", d=2)[:, h0:h0 + 2, :], op=ALU.max),
                    reads=[('den', h0), ('den', h0 + 1), 'G'], writes=[('den', h0), ('den', h0 + 1)])
                P.dve(lambda e, h0=h0: e.reciprocal(out=rden[:, h0:h0 + 2, :], in_=den[:, h0:h0 + 2, :]),
                      reads=[('den', h0), ('den', h0 + 1)], writes=[('rden', h0), ('rden', h0 + 1)])
                for h in hs:
                    bf_ = 2 * (h % 2)
                    bb_ = bf_ + 1
                    c0 = (2 * h) * 128
                    c1 = (2 * h + 1) * 128
                    P.act(lambda e, h=h, bf_=bf_: e.activation(
                        out=hf32[:, s, h * 256:(h + 1) * 256], in_=psP[:, bf_, 0:256], func=AF.Copy,
                        scale=rden[:, h, 0:1]),
                        banks=[('P', bf_)], reads=[('rden', h)], writes=[('hf32', s, h)])
                for h in hs:
                    bf_ = 2 * (h % 2)
                    bb_ = bf_ + 1
                    c0 = (2 * h) * 128
                    c1 = (2 * h + 1) * 128
                    P.dve(lambda e, h=h, bb_=bb_: e.scalar_tensor_tensor(
                        out=h32[:, s, h * 256:(h + 1) * 256], in0=psP[:, bb_, 0:256], scalar=rden[:, h, 1:2],
                        in1=hf32[:, s, h * 256:(h + 1) * 256], op0=ALU.mult, op1=ALU.add),
                        banks=[('P', bb_)], reads=[('rden', h), ('hf32', s, h)], writes=[('h32', s, h)])
                for h in hs:
                    P.dve(lambda e, h=h: e.bn_stats(out=stats[:, h, :], in_=h32[:, s, h * 256:(h + 1) * 256]),
                          reads=[('h32', s, h)], writes=[('stats', h)])


        def st_z(r):
            j, s, nx, l = ctx(r)
            SFK = [('Sf', i) for i in range(8)]
            for h in range(4):
                P.dve(lambda e, h=h: e.bn_aggr(out=mv[:, h, :], in_=stats[:, h, :]),
                      reads=[('stats', h)], writes=[('mv', h)])
            MV = [('mv', h) for h in range(4)]
            P.act(lambda e: e.activation(out=lnv[:], in_=mv[:, :, 1], func=AF.Ln, bias=LN_EPS),
                  reads=MV, writes=['lnv'])
            P.act(lambda e: e.activation(out=rstd[:], in_=lnv[:], func=AF.Exp, scale=-0.5),
                  reads=['lnv'], writes=['rstd'])
            P.dve(lambda e: e.scalar_tensor_tensor(out=nmr[:], in0=mv[:, :, 0], scalar=-1.0, in1=rstd[:],
                                                   op0=ALU.mult, op1=ALU.mult),
                  reads=MV + ['rstd'], writes=['nmr'])
            for h in range(4):
                P.act(lambda e, h=h: e.activation(
                    out=hf32[:, s, h * 256:(h + 1) * 256], in_=h32[:, s, h * 256:(h + 1) * 256], func=AF.Identity,
                    scale=rstd[:, h:h + 1], bias=nmr[:, h:h + 1]),
                    reads=[('h32', s, h), 'rstd', 'nmr'], writes=[('hf32', s, h)])
            P.pool(lambda e: e.tensor_tensor(out=hm[:, s, :], in0=hf32[:, s, :], in1=so[:, l, :], op=ALU.mult),
                   reads=[('hf32', s, h) for h in range(4)] + [('so', l)], writes=[('hm', s)])
            P.dma('pool', lambda e: e.dma_start(out=c.hm_s[j], in_=hm[:, s, :]), reads=[('hm', s)])
            if r == 0:
                H4 = list(range(4))
                c.dbg(P, 'kT', kT[:, s, :], [128, 1024], BF16, [('kT', s)])
                c.dbg(P, 'identb', identb[:], [128, 128], BF16, ['identb'])
                c.dbg(P, 'kb', kb[:, l, :], [128, 1024], BF16, [('kb', l)])
                c.dbg(P, 'cst', cst[:], [128, 512], F32, ['cst'])
                c.dbg(P, 'STf', STf[:, s, :, :].rearrange("p a b -> p (a b)"), [128, 512], BF16, [('STf', s, h) for h in H4])
                c.dbg(P, 'STb', STb[:, s, :, :].rearrange("p a b -> p (a b)"), [128, 512], BF16, [('STb', s, h) for h in H4])
                c.dbg(P, 'h32', h32[:, s, :], [128, 1024], F32, [('h32', s, h) for h in H4])
                c.dbg(P, 'hn', hf32[:, s, :], [128, 1024], F32, [('hf32', s, h) for h in H4])
                c.dbg(P, 'den', den[:].rearrange("p a b -> p (a b)"), [128, 8], F32, [('den', h) for h in H4])
                c.dbg(P, 'rden', rden[:].rearrange("p a b -> p (a b)"), [128, 8], F32, [('rden', h) for h in H4])
                c.dbg(P, 'mv', mv[:].rearrange("p a b -> p (a b)"), [128, 8], F32, [('mv', h) for h in H4])
                c.dbg(P, 'rstd', rstd[:], [128, 4], F32, ['rstd'])
                c.dbg(P, 'nmr', nmr[:], [128, 4], F32, ['nmr'])
                c.dbg(P, 'kdf', kdf[:, s, :], [128, 1024], BF16, [('kdf', s, h) for h in H4])
                c.dbg(P, 'Sf', Sf[:].rearrange("p a b -> p (a b)"), [128, 8 * 257], F32, SFK)

        load(0)
        if NM > 1:
            load(1)
        st_x1(0)
        st_x2(0)
        nconv = 0
        for r in range(NM):
            if r + 2 < NM:
                load(r + 2)
            if r + 1 < NM:
                st_x1(r + 1)
            st_init(r)
            st_u(r)
            if r + 1 < NM:
                st_x2(r + 1)
            st_y(r)
            st_z(r)
            want = min(NCONV, ((r + 1) * NCONV + NM - 1) // NM)
            while nconv < want:
                emit_weight_convert(P, c, nconv)
                nconv += 1
        P.run()


def phase4(nc, c):
    lay = c.lay
    M = lay.M
    NCH = lay.NM
    FIRST = [a for a, b in lay.segs]
    LAST = [b - 1 for a, b in lay.segs]
    SECOND_S = lay.segs[1][0] + 1
    P = Prog(nc, "p4", c.sems)
    with ExitStack() as es:
        sb, ps = _alloc(es, nc)
        wout = c.wout_pre
        wpr = sb("p4_wpr", [128, 4, 2, 256], F32)
        wp = sb("p4_wp", [128, 4, 2, 256], BF16)
        psc = sb("p4_psc", [128, 1024], F32)
        pm = sb("p4_pm", [128, 20, 128], F32)
        Bm = sb("p4_Bm", [128, 20, 128], BF16)
        Bx = sb("p4_Bx", [128, 4, 4, 128], BF16)
        tmpb = sb("p4_tmpb", [128, 4, 128], F32)
        flg = sb("p4_flg", [128, 4], F32)
        cst = sb("p4_cst", [128, 512], F32)
        identb = sb("p4_identb", [128, 128], BF16)
        bo = sb("p4_bo", [128, 2048], F32)
        g1 = sb("p4_g1", [128, 2048], F32)
        b1 = sb("p4_b1", [128, 2048], F32)
        bdn = sb("p4_bdn", [128, 2048], F32)
        u = sb("p4_u", [128, 4, 1024], BF16)
        hm = sb("p4_hm", [128, 3, 1024], BF16)
        R = sb("p4_R", [128, 4, 2048], F32)
        xb16 = sb("p4_xb16", [128, 2, 2048], BF16)
        mixT = sb("p4_mixT", [128, 2, 16, 128], BF16)
        pT = sb("p4_pT", [128, 8, 128], BF16)
        xTs = sb("p4_xTs", [128, 1, 2048], BF16)
        stats = sb("p4_stats", [128, 4, 6], F32)
        mv = sb("p4_mv", [128, 2], F32)
        lnv = sb("p4_lnv", [128, 1], F32)
        rstd = sb("p4_rstd", [128, 1], F32)
        nmr = sb("p4_nmr", [128, 1], F32)
        psPo = ps("p4_psPo", [128, 2, 512])
        psHp = ps("p4_psHp", [128, 2, 512])
        psX = ps("p4_psX", [128, 2, 512])
        psW = ps("p4_psW", [128, 2, 512])
        psXb = psX[:].rearrange("p a b -> p (a b)").bitcast(BF16)

        WOUT = [('wout', i) for i in range(4)]
        for g in range(4):
            P.dma('sp', lambda e, g=g: e.dma_start(out=wpr[:, g, :, :],
                                                   in_=c.w_pool[g].rearrange("(cb p) d -> p cb d", p=128)),
                  writes=[('wpr', g)])
        P.dma('sp', lambda e: e.dma_start(out=psc[:], in_=c.pool_scale.to_broadcast([128, 1024])), writes=['psc'])
        P.dma('sp', lambda e: e.dma_start(out=pm[:].rearrange("p a b -> p (a b)"), in_=c.pmats), writes=['pm'])
        P.dma('sp', lambda e: e.dma_start(out=flg[:], in_=c.flags), writes=['flg'])
        P.dma('sp', lambda e: e.dma_start(out=cst[:], in_=c.consts), writes=['cst'])
        P.dma('sp', lambda e: e.dma_start(out=bo[:], in_=c.b_out.to_broadcast([128, 2048])), writes=['bo'])
        P.dma('sp', lambda e: e.dma_start(out=g1[:], in_=c.ln1_g.to_broadcast([128, 2048])), writes=['g1'])
        P.dma('sp', lambda e: e.dma_start(out=b1[:], in_=c.ln1_b.to_broadcast([128, 2048])), writes=['b1'])
        P.dma('sp', lambda e: e.dma_start(out=bdn[:], in_=c.b_down.to_broadcast([128, 2048])), writes=['bdn'])
        P.dve(lambda e: e.tensor_copy(out=identb[:], in_=cst[:, 384:512]), reads=['cst'], writes=['identb'])
        for cb in range(2):
            P.dve(lambda e, cb=cb: e.tensor_tensor(out=wp[:, :, cb, :], in0=wpr[:, :, cb, :],
                                                   in1=psc[:].rearrange("p (g d) -> p g d", g=4), op=ALU.mult),
                  reads=[('wpr', g) for g in range(4)] + ['psc'], writes=['wp'])
        P.dve(lambda e: e.tensor_copy(out=Bm[:], in_=pm[:]), reads=['pm'], writes=['Bm'])
        P.pool(lambda e: e.memset(c.xh[:], 0.0), writes=['xh'])
        pm4 = pm[:].rearrange("p (g k) t -> p g k t", k=5)
        for idx, (ka, kb_) in enumerate([(1, 4), (2, None), (1, 3), (0, None)]):
            if kb_ is None:
                P.dve(lambda e, idx=idx, ka=ka: e.tensor_scalar(out=Bx[:, idx, :, :], in0=pm4[:, :, ka, :],
                                                                scalar1=flg[:, 0:1], scalar2=None, op0=ALU.mult),
                      reads=['pm', 'flg'], writes=[('Bx', idx)])
            else:
                P.dve(lambda e, ka=ka: e.tensor_scalar(out=tmpb[:], in0=pm4[:, :, ka, :], scalar1=flg[:, 0:1],
                                                       scalar2=None, op0=ALU.mult),
                      reads=['pm', 'flg'], writes=['tmpb'])
                P.dve(lambda e, idx=idx, kb_=kb_: e.scalar_tensor_tensor(
                    out=Bx[:, idx, :, :], in0=pm4[:, :, kb_, :], scalar=flg[:, 1:2], in1=tmpb[:],
                    op0=ALU.mult, op1=ALU.add),
                    reads=['pm', 'flg', 'tmpb'], writes=[('Bx', idx)])
        Bm4 = Bm[:].rearrange("p (g k) t -> p g k t", k=5)

        def load_uh(j):
            P.dma('sp', lambda e: e.dma_start(out=u[:, j % 4, :], in_=c.u_s[M[j]]), writes=[('u', j % 4)])
            P.dma('sp', lambda e: e.dma_start(out=hm[:, j % 3, :], in_=c.hm_s[M[j]]), writes=[('hm', j % 3)])

        def load_r(j):
            P.dma('sp', lambda e: e.dma_start(out=R[:, j % 4, :], in_=c.x[j * 128:(j + 1) * 128, :]),
                  writes=[('R', j % 4)])

        def load(j):
            load_uh(j)
            load_r(j)

        def xT_part(j):
            s = j % 2
            for kt in range(16):
                P.pe(lambda e, kt=kt: e.transpose(out=psXb[:, kt * 128:(kt + 1) * 128],
                                                  in_=xb16[:, s, kt * 128:(kt + 1) * 128], identity=identb[:]),
                     reads=[('xb16', s), 'identb'], banks=['X0', 'X1'])
            P.dve(lambda e: e.tensor_copy(out=xTs[:, 0, :], in_=psXb), banks=['X0', 'X1'], writes=[('xTs', 0)])
            P.dma('act', lambda e: e.dma_start(out=c.x1T_s[j], in_=xTs[:, 0, :]), reads=[('xTs', 0)])
            xv = xTs[:, 0, :].rearrange("p (k t) -> p k t", k=16)
            for ti, m0 in enumerate(lay.tiles):
                if j == m0 - 1 and lay.seg_of(j) == lay.seg_of(m0):
                    P.pool(lambda e, ti=ti: e.tensor_copy(out=c.xh[:, :, 2 * ti:2 * ti + 1], in_=xv[:, :, 127:128]),
                           reads=[('xTs', 0)], writes=['xh'])
                if j == m0 + 4:
                    P.pool(lambda e, ti=ti: e.tensor_copy(out=c.xh[:, :, 2 * ti + 1:2 * ti + 2], in_=xv[:, :, 0:1]),
                           reads=[('xTs', 0)], writes=['xh'])

        def do_chunk(j):
            s = j % 2
            rs = j % 4
            hs3 = j % 3
            P.act(lambda e: e.activation(out=R[:, rs, :], in_=R[:, rs, :], func=AF.Copy, scale=ALPHA),
                  reads=[('R', rs)], writes=[('R', rs)])
            P.pool(lambda e: e.tensor_tensor(out=R[:, rs, :], in0=R[:, rs, :], in1=bo[:], op=ALU.add),
                   reads=[('R', rs), 'bo'], writes=[('R', rs)])
            for blk in range(8):
                P.pe(lambda e, blk=blk: e.transpose(out=psXb[:, blk * 128:(blk + 1) * 128],
                                                    in_=hm[:, hs3, blk * 128:(blk + 1) * 128], identity=identb[:]),
                     reads=[('hm', hs3), 'identb'], banks=['X0'])
            P.dve(lambda e: e.tensor_copy(out=mixT[:, s, 0:8, :].rearrange("p a b -> p (a b)"), in_=psXb[:, 0:1024]),
                  banks=['X0'], writes=[('mixT', s, 0)])
            yield
            srcs = []
            is_first = j in FIRST
            is_last = j in LAST
            if j == 0:
                srcs.append((j, lambda g: Bm4[:, g, 3, :], 'Bm'))
            elif j == SECOND_S:
                srcs.append((j - 1, lambda g: Bx[:, 3, g, :], ('Bx', 3)))
                srcs.append((j, lambda g: Bx[:, 2, g, :], ('Bx', 2)))
            else:
                if not is_first:
                    srcs.append((j - 1, lambda g: Bm4[:, g, 0, :], 'Bm'))
                srcs.append((j, lambda g: Bm4[:, g, 1, :], 'Bm'))
            if not is_last:
                srcs.append((j + 1, lambda g: Bm4[:, g, 2, :], 'Bm'))
            for g in range(4):
                for cb in range(2):
                    i8 = g * 2 + cb
                    bank = i8 // 4
                    for n, (jj, bf, bkey) in enumerate(srcs):
                        P.pe(lambda e, g=g, cb=cb, i8=i8, bank=bank, jj=jj, bf=bf, n=n: e.matmul(
                            psPo[:, bank, (i8 % 4) * 128:(i8 % 4 + 1) * 128],
                            lhsT=u[:, jj % 4, g * 256 + cb * 128:g * 256 + cb * 128 + 128], rhs=bf(g),
                            start=(n == 0), stop=(n == len(srcs) - 1)),
                            reads=[('u', jj % 4), bkey], banks=[('Po', bank)])
            for bank in range(2):
                P.dve(lambda e, bank=bank: e.tensor_copy(
                    out=pT[:, bank * 4:(bank + 1) * 4, :].rearrange("p a b -> p (a b)"), in_=psPo[:, bank, :]),
                    banks=[('Po', bank)], writes=[('pT', bank)])
            yield
            for g in range(4):
                for db in range(2):
                    i8 = g * 2 + db
                    bank = i8 // 4
                    for cb in range(2):
                        P.pe(lambda e, g=g, db=db, cb=cb, i8=i8, bank=bank: e.matmul(
                            psHp[:, bank, (i8 % 4) * 128:(i8 % 4 + 1) * 128],
                            lhsT=wp[:, g, cb, db * 128:(db + 1) * 128], rhs=pT[:, g * 2 + cb, :],
                            start=(cb == 0), stop=(cb == 1)),
                            reads=['wp', ('pT', g // 2)], banks=[('Hp', bank)])
            for bank in range(2):
                P.dve(lambda e, bank=bank: e.tensor_copy(
                    out=mixT[:, s, 8 + bank * 4:8 + (bank + 1) * 4, :].rearrange("p a b -> p (a b)"),
                    in_=psHp[:, bank, :]),
                    banks=[('Hp', bank)], writes=[('mixT', s, 1 + bank)])
            yield
            MIX = [('mixT', s, i) for i in range(3)]
            for dg in range(4):
                b = dg % 2
                for kt in range(16):
                    P.pe(lambda e, dg=dg, b=b, kt=kt: e.matmul(
                        psW[:, b, :], lhsT=mixT[:, s, kt, :], rhs=wout[:, kt, dg * 512:(dg + 1) * 512],
                        start=(kt == 0), stop=(kt == 15)),
                        reads=MIX + WOUT, banks=[('W', b)])
                P.dve(lambda e, dg=dg, b=b: e.tensor_tensor(
                    out=R[:, rs, dg * 512:(dg + 1) * 512], in0=psW[:, b, :], in1=R[:, rs, dg * 512:(dg + 1) * 512],
                    op=ALU.add),
                    banks=[('W', b)], reads=[('R', rs)], writes=[('R', rs)])
                P.dve(lambda e, dg=dg: e.bn_stats(out=stats[:, dg, :], in_=R[:, rs, dg * 512:(dg + 1) * 512]),
                      reads=[('R', rs)], writes=['stats'])
                yield
            P.dve(lambda e: e.bn_aggr(out=mv[:], in_=stats[:].rearrange("p a b -> p (a b)")),
                  reads=['stats'], writes=['mv'])
            P.act(lambda e: e.activation(out=lnv[:], in_=mv[:, 1:2], func=AF.Ln, bias=LN_EPS), reads=['mv'],
                  writes=['lnv'])
            P.act(lambda e: e.activation(out=rstd[:], in_=lnv[:], func=AF.Exp, scale=-0.5), reads=['lnv'],
                  writes=['rstd'])
            P.dve(lambda e: e.scalar_tensor_tensor(out=nmr[:], in0=mv[:, 0:1], scalar=-1.0, in1=rstd[:],
                                                   op0=ALU.mult, op1=ALU.mult),
                  reads=['mv', 'rstd'], writes=['nmr'])
            P.act(lambda e: e.activation(out=R[:, rs, :], in_=R[:, rs, :], func=AF.Identity, scale=rstd[:, 0:1],
                                         bias=nmr[:, 0:1]),
                  reads=[('R', rs), 'rstd', 'nmr'], writes=[('R', rs)])
            P.dve(lambda e: e.tensor_tensor(out=R[:, rs, :], in0=R[:, rs, :], in1=g1[:], op=ALU.mult),
                  reads=[('R', rs), 'g1'], writes=[('R', rs)])
            P.pool(lambda e: e.tensor_tensor(out=R[:, rs, :], in0=R[:, rs, :], in1=b1[:], op=ALU.add),
                   reads=[('R', rs), 'b1'], writes=[('R', rs)])
            P.act(lambda e: e.copy(out=xb16[:, s, :], in_=R[:, rs, :]), reads=[('R', rs)], writes=[('xb16', s)])
            if c.debug:
                P.dma('sp', lambda e: e.dma_start(out=c.x1d_s[j * 128:(j + 1) * 128, :], in_=R[:, rs, :]),
                      reads=[('R', rs)])
            P.act(lambda e: e.activation(out=R[:, rs, :], in_=R[:, rs, :], func=AF.Copy, scale=ALPHA),
                  reads=[('R', rs)], writes=[('R', rs)])
            P.pool(lambda e: e.tensor_tensor(out=R[:, rs, :], in0=R[:, rs, :], in1=bdn[:], op=ALU.add),
                   reads=[('R', rs), 'bdn'], writes=[('R', rs)])
            P.dma('pool', lambda e: e.dma_start(out=c.x1_s[j * 128:(j + 1) * 128, :], in_=R[:, rs, :]),
                  reads=[('R', rs)])

        def adv(g):
            try:
                next(g)
            except StopIteration:
                pass

        load(0)
        if NCH > 1:
            load(1)
        gens = {}
        for it in range(NCH + 2):
            cur = prev = None
            if it + 2 < NCH:
                load(it + 2)
            if it < NCH:
                gens[it] = do_chunk(it)
                cur = gens[it]
            if 0 <= it - 1 < NCH:
                prev = gens[it - 1]
            if cur is not None:
                adv(cur)
            if prev is not None:
                adv(prev)
            if cur is not None:
                adv(cur)
            if prev is not None:
                adv(prev)
            if cur is not None:
                adv(cur)
            if prev is not None:
                adv(prev)
                adv(prev)
            if 0 <= it - 2 < NCH:
                xT_part(it - 2)
            if prev is not None:
                adv(prev)
        P.run()


NCONV = 52


def emit_weight_convert(P, c, idx):
    if idx < 44:
        g, gv = idx // 2, idx % 2
        src = c.w_up.rearrange("(kt p) n -> p kt n", p=128)[:, :, gv * DFF + g * 256:gv * DFF + (g + 1) * 256]
        dst = c.wu_t[g].rearrange("p (kt gv n) -> p kt gv n", kt=16, gv=2)[:, :, gv, :]
        P.dma('pool', lambda e: e.dma_start(out=dst, in_=src), writes=[('wu_t', g, gv)])
    elif idx < 52:
        g = idx - 44
        src = c.w_down.rearrange("(b p) n -> p b n", p=128)[:, :, g * 256:(g + 1) * 256]
        dst = c.wd_t[g].rearrange("p (b n) -> p b n", b=44)
        P.dma('pool', lambda e: e.dma_start(out=dst, in_=src), writes=[('wd_t', g)])


def phase5(nc, c):
    lay = c.lay
    NT = lay.NTILE
    HT = lay.iS
    TM = lay.tiles
    P = Prog(nc, "p5", c.sems)
    xh = c.xh
    with ExitStack() as es:
        sb, ps = _alloc(es, nc)
        wu = sb("p5_wu", [128, 2, 16, 2, 256], BF16)
        wd = sb("p5_wd", [128, 2, 44, 256], BF16)
        hT = sb("p5_hT", [128, 44, 512], BF16)
        x1t = sb("p5_x1t", [128, 16, 512], BF16)
        tg = sb("p5_tg", [128, 2, 512], F32)
        tv = sb("p5_tv", [128, 2, 512], F32)
        sg = sb("p5_sg", [128, 1, 512], F32)
        R = sb("p5_R", [128, 4, 2048], F32)
        g2 = sb("p5_g2", [128, 2048], F32)
        b2 = sb("p5_b2", [128, 2048], F32)
        Ah = sb("p5_Ah", [128, 88, 2 * NT], F32)
        wc = sb("p5_wc", [128, 3, 88], F32)
        bup = sb("p5_bup", [128, 88], F32)
        bcv = sb("p5_bcv", [128, 88], F32)
        cb = sb("p5_cb", [128, 88], F32)
        w0b = sb("p5_w0b", [128, 88], F32)
        w2b = sb("p5_w2b", [128, 88], F32)
        w0bl = sb("p5_w0bl", [128, 88], F32)
        w2bl = sb("p5_w2bl", [128, 88], F32)
        w0l = sb("p5_w0l", [128, 88], F32)
        w2l = sb("p5_w2l", [128, 88], F32)
        flg = sb("p5_flg", [128, 4], F32)
        stats = sb("p5_stats", [128, 4, 6], F32)
        mv = sb("p5_mv", [128, 2], F32)
        lnv = sb("p5_lnv", [128, 1], F32)
        rstd = sb("p5_rstd", [128, 1], F32)
        nmr = sb("p5_nmr", [128, 1], F32)
        negh = sb("p5_negh", [128, 1], F32)
        psU = ps("p5_psU", [128, 4, 512])
        psH = ps("p5_psH", [128, 512])
        psD = ps("p5_psD", [128, 2, 512])

        P.dma('sp', lambda e: e.dma_start(out=wc[:].rearrange("p a b -> p (a b)"), in_=c.w_conv), writes=['wc'])
        P.dma('sp', lambda e: e.dma_start(out=bup[:], in_=c.b_up), writes=['bup'])
        P.dma('sp', lambda e: e.dma_start(out=bcv[:], in_=c.b_conv), writes=['bcv'])
        P.dma('sp', lambda e: e.dma_start(out=flg[:], in_=c.flags), writes=['flg'])
        P.dma('sp', lambda e: e.dma_start(out=g2[:], in_=c.ln2_g.to_broadcast([128, 2048])), writes=['g2'])
        P.dma('sp', lambda e: e.dma_start(out=b2[:], in_=c.ln2_b.to_broadcast([128, 2048])), writes=['b2'])
        P.dve(lambda e: e.tensor_tensor(out=cb[:], in0=wc[:, 0, :], in1=wc[:, 1, :], op=ALU.add), reads=['wc'],
              writes=['cb'])
        P.dve(lambda e: e.tensor_tensor(out=cb[:], in0=cb[:], in1=wc[:, 2, :], op=ALU.add), reads=['wc', 'cb'],
              writes=['cb'])
        P.dve(lambda e: e.tensor_tensor(out=cb[:], in0=cb[:], in1=bup[:], op=ALU.mult), reads=['bup', 'cb'],
              writes=['cb'])
        P.dve(lambda e: e.tensor_tensor(out=cb[:], in0=cb[:], in1=bcv[:], op=ALU.add), reads=['bcv', 'cb'],
              writes=['cb'])
        P.dve(lambda e: e.tensor_tensor(out=w0b[:], in0=wc[:, 0, :], in1=bup[:], op=ALU.mult), reads=['wc', 'bup'],
              writes=['w0b'])
        P.dve(lambda e: e.tensor_tensor(out=w2b[:], in0=wc[:, 2, :], in1=bup[:], op=ALU.mult), reads=['wc', 'bup'],
              writes=['w2b'])
        P.dve(lambda e: e.tensor_scalar(out=w0bl[:], in0=w0b[:], scalar1=flg[:, 1:2], scalar2=None, op0=ALU.mult),
              reads=['w0b', 'flg'], writes=['w0bl'])
        P.dve(lambda e: e.tensor_scalar(out=w2bl[:], in0=w2b[:], scalar1=flg[:, 1:2], scalar2=None, op0=ALU.mult),
              reads=['w2b', 'flg'], writes=['w2bl'])
        P.dve(lambda e: e.tensor_scalar(out=w0l[:], in0=wc[:, 0, :], scalar1=flg[:, 0:1], scalar2=None, op0=ALU.mult),
              reads=['wc', 'flg'], writes=['w0l'])
        P.dve(lambda e: e.tensor_scalar(out=w2l[:], in0=wc[:, 2, :], scalar1=flg[:, 0:1], scalar2=None, op0=ALU.mult),
              reads=['wc', 'flg'], writes=['w2l'])
        CONSTS = ['wc', 'cb', 'w0b', 'w2b', 'w0bl', 'w2bl', 'w0l', 'w2l']
        P.pool(lambda e: e.memset(negh[:], -0.5), writes=['negh'])

        wu_cnt = [0]
        wd_cnt = [0]

        def load_wu(g):
            s = wu_cnt[0] % 2
            wu_cnt[0] += 1
            P.dma('sp', lambda e: e.dma_start(out=wu[:, s, :, :, :].rearrange("p a b c -> p (a b c)"), in_=c.wu_t[g]),
                  reads=[('wu_t', g)], writes=[('wu', s)])
            return s

        def load_wd(g):
            s = wd_cnt[0] % 2
            wd_cnt[0] += 1
            P.dma('sp', lambda e: e.dma_start(out=wd[:, s, :, :].rearrange("p a b -> p (a b)"), in_=c.wd_t[g]),
                  reads=[('wd_t', g)], writes=[('wd', s)])
            return s

        def conv_path(t, blk, b, gs, i, silu):
            w0 = wc[:, 0, blk:blk + 1]
            w1 = wc[:, 1, blk:blk + 1]
            w2 = wc[:, 2, blk:blk + 1]
            key = ('t', id(t), gs)
            P.act(lambda e: e.activation(out=t[:, gs, :], in_=psU[:, b, :], func=AF.Identity, scale=w1,
                                         bias=cb[:, blk:blk + 1]),
                  banks=[('U', b)], reads=CONSTS, writes=[key])
            P.dve(lambda e: e.scalar_tensor_tensor(out=t[:, gs, 1:512], in0=psU[:, b, 0:511], scalar=w0,
                                                   in1=t[:, gs, 1:512], op0=ALU.mult, op1=ALU.add),
                  banks=[('U', b)], reads=CONSTS + [key], writes=[key])
            P.dve(lambda e: e.scalar_tensor_tensor(out=t[:, gs, 0:511], in0=psU[:, b, 1:512], scalar=w2,
                                                   in1=t[:, gs, 0:511], op0=ALU.mult, op1=ALU.add),
                  banks=[('U', b)], reads=CONSTS + [key], writes=[key])
            if i == 0:
                P.dve(lambda e: e.tensor_scalar(out=t[:, gs, 0:1], in0=t[:, gs, 0:1], scalar1=w0b[:, blk:blk + 1],
                                                scalar2=None, op0=ALU.subtract),
                      reads=CONSTS + [key], writes=[key])
            else:
                wl = w0l[:, blk:blk + 1] if i == HT else w0
                P.dve(lambda e: e.scalar_tensor_tensor(out=t[:, gs, 0:1], in0=Ah[:, blk, 2 * i:2 * i + 1], scalar=wl,
                                                       in1=t[:, gs, 0:1], op0=ALU.mult, op1=ALU.add),
                      reads=CONSTS + [key, ('Ah', blk)], writes=[key])
                if i == HT:
                    P.dve(lambda e: e.tensor_scalar(out=t[:, gs, 0:1], in0=t[:, gs, 0:1],
                                                    scalar1=w0bl[:, blk:blk + 1], scalar2=None, op0=ALU.subtract),
                          reads=CONSTS + [key], writes=[key])
            P.dve(lambda e: e.scalar_tensor_tensor(out=t[:, gs, 511:512], in0=Ah[:, blk, 2 * i + 1:2 * i + 2],
                                                   scalar=w2, in1=t[:, gs, 511:512], op0=ALU.mult, op1=ALU.add),
                  reads=CONSTS + [key, ('Ah', blk)], writes=[key])
            if silu:
                P.act(lambda e: e.activation(out=sg[:, 0, :], in_=t[:, gs, :], func=AF.Silu), reads=[key],
                      writes=[('sg', 0)])
            return key

        pcnt = [0]

        def load_x1t(i):
            for cc in range(4):
                P.dma('sp', lambda e, cc=cc: e.dma_start(
                    out=x1t[:, :, cc * 128:(cc + 1) * 128], in_=c.x1T_s[TM[i] + cc].rearrange("p (k t) -> p k t", k=16)),
                    writes=[('x1t', cc)])

        def do_tile(i):
            if i == 0:
                load_x1t(0)
                tile_state['wd_next'] = load_wd(0)
            X1T = [('x1t', cc) for cc in range(4)]
            nxt = load_wu(0) if i == 0 else tile_state['wu_next']
            for g in range(22):
                if i > 0 and g in (2, 7, 12, 17):
                    finish_pair(i - 1, 0, ms=((g - 2) // 5,))
                s = nxt
                if g + 1 < 22:
                    nxt = load_wu(g + 1)
                elif i + 1 < NT:
                    nxt = load_wu(0)
                    tile_state['wu_next'] = nxt
                for pp in range(2):
                    p = g * 2 + pp
                    gs = pcnt[0] % 2
                    pcnt[0] += 1
                    for gv in range(2):
                        b = pp * 2 + gv
                        blk = p + 44 * gv
                        for kt in range(16):
                            P.pe(lambda e, b=b, kt=kt, gv=gv, pp=pp, s=s: e.matmul(
                                psU[:, b, :], lhsT=wu[:, s, kt, gv, pp * 128:(pp + 1) * 128], rhs=x1t[:, kt, :],
                                start=(kt == 0), stop=(kt == 15)),
                                reads=[('wu', s)] + X1T, banks=[('U', b)])
                        if i == 0:
                            for kt in range(16):
                                P.pe(lambda e, kt=kt, gv=gv, pp=pp, s=s: e.matmul(
                                    psH[:, 0:2 * NT], lhsT=wu[:, s, kt, gv, pp * 128:(pp + 1) * 128], rhs=xh[:, kt, :],
                                    start=(kt == 0), stop=(kt == 15)),
                                    reads=[('wu', s), 'xh'], banks=['H'])
                            P.act(lambda e, blk=blk: e.copy(out=Ah[:, blk, :], in_=psH[:, 0:2 * NT]), banks=['H'],
                                  writes=[('Ah', blk)])
                    kg = conv_path(tg, p, pp * 2 + 0, gs, i, True)
                    if c.debug and i == 0 and p == 0:
                        c.dbg(P, 'tg', tg[:, gs, :], [128, 512], F32, [kg])
                        c.dbg(P, 'sg', sg[:, 0, :], [128, 512], F32, [('sg', 0)])
                    kv = conv_path(tv, p + 44, pp * 2 + 1, gs, i, False)
                    P.pool(lambda e, p=p, gs=gs: e.tensor_tensor(out=hT[:, p, :], in0=sg[:, 0, :], in1=tv[:, gs, :],
                                                                 op=ALU.mult),
                           reads=[('sg', 0), kv], writes=[('hT', p)])
            for m in range(4):
                j = TM[i] + m
                P.dma('sp', lambda e, j=j, m=m: e.dma_start(out=R[:, m, :], in_=c.x1_s[j * 128:(j + 1) * 128, :]),
                      writes=[('R', m)])
            if i + 1 < NT:
                load_x1t(i + 1)
            HTK = [('hT', p) for p in range(44)]
            if i == 0:
                c.dbg(P, 'hT', hT[:].rearrange("p a b -> p (a b)"), [128, 44 * 512], BF16, HTK)
            for dg in range(8):
                sd = tile_state['wd_next']
                if dg + 1 < 8:
                    tile_state['wd_next'] = load_wd(dg + 1)
                elif i + 1 < NT:
                    tile_state['wd_next'] = load_wd(0)
                for m in range(4):
                    rs = m
                    b = m % 2
                    for blk in range(44):
                        P.pe(lambda e, b=b, blk=blk, m=m, sd=sd: e.matmul(
                            psD[:, b, 0:256], lhsT=hT[:, blk, m * 128:(m + 1) * 128], rhs=wd[:, sd, blk, :],
                            start=(blk == 0), stop=(blk == 43)),
                            reads=[('hT', blk), ('wd', sd)], banks=[('D', b)])
                    P.dve(lambda e, b=b, rs=rs, dg=dg: e.tensor_tensor(
                        out=R[:, rs, dg * 256:(dg + 1) * 256], in0=psD[:, b, 0:256],
                        in1=R[:, rs, dg * 256:(dg + 1) * 256], op=ALU.add),
                        banks=[('D', b)], reads=[('R', rs)], writes=[('R', rs)])
            if i == NT - 1:
                finish_pair(i, 0)
                finish_pair(i, 1)

        def finish_pair(i, mh, ms=None):
            for m in (ms if ms is not None else (2 * mh, 2 * mh + 1)):
                j = 4 * i + m
                rs = m
                for q in range(4):
                    P.dve(lambda e, q=q, rs=rs: e.bn_stats(out=stats[:, q, :], in_=R[:, rs, q * 512:(q + 1) * 512]),
                          reads=[('R', rs)], writes=['stats'])
                P.dve(lambda e: e.bn_aggr(out=mv[:], in_=stats[:].rearrange("p a b -> p (a b)")),
                      reads=['stats'], writes=['mv'])
                P.pool(lambda e: e.tensor_scalar(out=lnv[:], in0=mv[:, 1:2], scalar1=LN_EPS, scalar2=1.0,
                                                 op0=ALU.add, op1=ALU.mult), reads=['mv'], writes=['lnv'])
                P.pool(lambda e: e.tensor_tensor(out=rstd[:], in0=lnv[:], in1=negh[:], op=ALU.pow),
                       reads=['lnv', 'negh'], writes=['rstd'])
                P.dve(lambda e: e.scalar_tensor_tensor(out=nmr[:], in0=mv[:, 0:1], scalar=-1.0, in1=rstd[:],
                                                       op0=ALU.mult, op1=ALU.mult),
                      reads=['mv', 'rstd'], writes=['nmr'])
                P.dve(lambda e, rs=rs: e.tensor_scalar(out=R[:, rs, :], in0=R[:, rs, :], scalar1=rstd[:, 0:1],
                                                       scalar2=nmr[:, 0:1], op0=ALU.mult, op1=ALU.add),
                      reads=[('R', rs), 'rstd', 'nmr'], writes=[('R', rs)])
                P.dve(lambda e, rs=rs: e.tensor_tensor(out=R[:, rs, :], in0=R[:, rs, :], in1=g2[:], op=ALU.mult),
                      reads=[('R', rs), 'g2'], writes=[('R', rs)])
                P.pool(lambda e, rs=rs: e.tensor_tensor(out=R[:, rs, :], in0=R[:, rs, :], in1=b2[:], op=ALU.add),
                       reads=[('R', rs), 'b2'], writes=[('R', rs)])
                P.dma('pool', lambda e, j=j, rs=rs: e.dma_start(out=c.y[j * 128:(j + 1) * 128, :], in_=R[:, rs, :]),
                      reads=[('R', rs)])

        tile_state = {}
        for i in range(NT):
            do_tile(i)
        P.run()


def build(lay_args=LAY_FULL, debug=False):
    nc = bass.Bass("TRN2", target_bir_lowering=False)
    lay = Lay(*lay_args)
    with ExitStack() as es:
        c = declare(nc, lay, debug, es)
        phase1(nc, c)
        phase2(nc, c)
        with ExitStack() as es34:
            c.wout_pre = es34.enter_context(nc.sbuf_tensor("g_wout", [128, 16, 2048], BF16))
            phase3(nc, c)
            phase4(nc, c)
        phase5(nc, c)
    return nc


GPERM = [0, 1, 2, 3, 8, 9, 10, 11, 4, 5, 6, 7, 12, 13, 14, 15]
GPERM_REV = [8, 9, 10, 11, 0, 1, 2, 3, 12, 13, 14, 15, 4, 5, 6, 7]


def shared_maps(inp):
    w_in = inp["w_in"][0]
    b_in = inp["b_in"][0]
    m = {}
    m["w_kv"] = np.ascontiguousarray(w_in[:, 1024:3072])
    m["w_qou"] = np.ascontiguousarray(np.concatenate([w_in[:, 0:1024], w_in[:, 3072:4096], w_in[:, 4112:5136]], axis=1))
    m["b_q"] = np.ascontiguousarray(b_in[0:1024].reshape(8, 128).T)
    m["b_ou"] = np.ascontiguousarray(np.concatenate([b_in[3072:4096], b_in[4112:5136]])[None, :])
    m["gain"] = np.ascontiguousarray(inp["mh_norm_g"][0][None, :])
    m["w_pool"] = np.ascontiguousarray(inp["w_pool"][0])
    m["pool_scale"] = np.ascontiguousarray(inp["pool_scale"][0][None, :])
    m["w_out"] = np.ascontiguousarray(inp["w_out"][0])
    m["b_out"] = np.ascontiguousarray(inp["b_out"][0][None, :])
    m["ln1_g"] = np.ascontiguousarray(inp["ln1_g"][0][None, :])
    m["ln1_b"] = np.ascontiguousarray(inp["ln1_b"][0][None, :])
    m["w_up"] = np.ascontiguousarray(inp["w_up"][0])
    m["b_up"] = np.ascontiguousarray(inp["b_up"][0].reshape(88, 128).T)
    m["b_conv"] = np.ascontiguousarray(inp["b_conv"][0].reshape(88, 128).T)
    m["w_down"] = np.ascontiguousarray(inp["w_down"][0])
    m["b_down"] = np.ascontiguousarray(inp["b_down"][0][None, :])
    m["ln2_g"] = np.ascontiguousarray(inp["ln2_g"][0][None, :])
    m["ln2_b"] = np.ascontiguousarray(inp["ln2_b"][0][None, :])
    var = {}
    for rev in (False, True):
        v = {}
        gp = GPERM_REV if rev else GPERM
        v["w_g"] = np.ascontiguousarray(w_in[:, 4096:4112][:, gp])
        v["b_kvg"] = np.ascontiguousarray(np.concatenate([b_in[1024:3072], b_in[4096:4112][gp]])[None, :])
        wc = inp["w_conv"][0]
        if rev:
            wc = wc[::-1]
        v["w_conv"] = np.ascontiguousarray(wc.reshape(3, 88, 128).transpose(2, 0, 1).reshape(128, 264))
        cst, pm = _consts(rev)
        v["consts"] = cst
        v["pmats"] = pm
        var[rev] = v
    return m, var


def _xT(chunks):
    n = chunks.shape[0]
    return np.ascontiguousarray(chunks.reshape(n, 128, 16, 128).transpose(0, 3, 2, 1).reshape(n, 128, 2048))


def core_map(shared, var, lay, Xp, Xs, stype, rev):
    m = dict(shared)
    m.update(var[bool(rev)])
    if rev:
        Xp = Xp[::-1]
        Xs = Xs[::-1]
    cp = Xp.reshape(-1, 128, D)
    cs = Xs.reshape(-1, 128, D)
    z = np.zeros((128, D), np.float32)
    slots = [cp[i] for i in range(lay.NPS)]
    nb = lay.S_bef + 1
    if stype == 0:
        seq = [None] * nb + list(range(0, lay.S_own + 1 + lay.S_aft))
    else:
        nreal = lay.S_aft - lay.S_pad
        seq = list(range(0, nb + lay.S_own + 1 + nreal)) + [None] * lay.S_pad
    assert len(seq) == lay.NSS
    slots += [z if i is None else cs[i] for i in seq]
    allc = np.stack(slots, axis=0)
    m["xT"] = _xT(allc)
    m["x"] = np.ascontiguousarray(allc[lay.M].reshape(lay.NM * 128, D))
    fl = np.zeros((128, 4), np.float32)
    fl[:, 0] = 1.0 if stype == 1 else 0.0
    fl[:, 1] = 1.0 - fl[:, 0]
    fl[:, 2] = 0.0 if (stype == 1 and lay.S_pad) else 1.0
    m["flags"] = fl
    return m


def place_outputs(lay, y, stype, rev, yp_out, ys_out):
    npo = lay.P_own * 128
    yp = y[:npo]
    ys = y[npo:]
    Sp = yp_out.shape[0]
    Ss = ys_out.shape[0]
    own0 = 0 if stype == 0 else (lay.S_bef + 1) * 128
    nso = lay.S_own * 128
    if not rev:
        yp_out[0:npo] = yp
        ys_out[own0:own0 + nso] = ys
    else:
        yp_out[Sp - npo:Sp] = yp[::-1]
        ys_out[Ss - own0 - nso:Ss - own0] = ys[::-1]


CORE_ASSIGN = [(0, 0, 0, 0), (0, 0, 0, 1), (1, 0, 1, 0), (1, 0, 1, 1),
               (2, 1, 0, 0), (2, 1, 0, 1), (3, 1, 1, 0), (3, 1, 1, 1)]

_NC_CACHE = {}


def kernel(**inputs):
    inp = {k: np.asarray(v) for k, v in inputs.items()}
    lay = Lay(*LAY_FULL)
    shared, var = shared_maps(inp)
    xp = inp["x_prompt"].astype(np.float32, copy=False)
    xs = inp["x_sample"].astype(np.float32, copy=False)
    maps = [core_map(shared, var, lay, xp[p], xs[s], st, rv) for (p, s, st, rv) in CORE_ASSIGN]
    if "full" not in _NC_CACHE:
        _NC_CACHE["full"] = build(LAY_FULL)
    nc = _NC_CACHE["full"]
    res = run_bass_kernel_spmd(nc, maps, core_ids=list(range(NCORES)))
    y_prompt = np.zeros(xp.shape, np.float32)
    y_sample = np.zeros(xs.shape, np.float32)
    for core, (p, s, st, rv) in enumerate(CORE_ASSIGN):
        place_outputs(lay, np.asarray(res.results[core]["y"], dtype=np.float32), st, rv, y_prompt[p], y_sample[s])
    return (y_prompt, y_sample)
```
